# Optimizing a Trainium2 kernel written in Bass

```python
import jax, jax.numpy as jnp
from jax import lax
import numpy as np

D_MODEL = 1024
BATCH = 8
SEQ = 4096
DEPTH = 1

CTX_LEN = 256
GRID_W = 64
D_HGRN = 512
HGRN_HEADS = 4
HGRN_HEAD_DIM = D_HGRN // HGRN_HEADS
HGRN_CHUNK = 64
D_CONV = 512
CONV_WIDTH = 31
D_MIX = D_HGRN + D_CONV
SPLITS = [D_HGRN, 2 * D_HGRN, 3 * D_HGRN, 4 * D_HGRN, 5 * D_HGRN,
          5 * D_HGRN + D_CONV, 5 * D_HGRN + 2 * D_CONV]
D_IN = 5 * D_HGRN + 3 * D_CONV
EPS = 1e-6

kernel_name = "hymba_hgrn2_conformer_dit_block"


def rmsnorm(x, g):
    xf = x.astype(jnp.float32)
    y = xf * lax.rsqrt(jnp.mean(xf * xf, axis=-1, keepdims=True) + EPS)
    return (y * g.astype(jnp.float32)).astype(x.dtype)


def layernorm(x, g, b):
    xf = x.astype(jnp.float32)
    mu = jnp.mean(xf, axis=-1, keepdims=True)
    var = jnp.mean(jnp.square(xf - mu), axis=-1, keepdims=True)
    return ((xf - mu) * lax.rsqrt(var + EPS) * g.astype(jnp.float32) + b.astype(jnp.float32)).astype(x.dtype)


def split_heads(a):
    bsz, t, _ = a.shape
    return a.reshape(bsz, t, HGRN_HEADS, HGRN_HEAD_DIM).transpose(0, 2, 1, 3)


def merge_heads(a):
    bsz, _, t, _ = a.shape
    return a.transpose(0, 2, 1, 3).reshape(bsz, t, D_HGRN)


def forget_gate(z, lb):
    f = lb + (1.0 - lb) * jax.nn.sigmoid(z.astype(jnp.float32))
    return jnp.log(f), 1.0 - f


def hgrn2_scan(q, k, v, log_f, s0):
    bsz, h, t, dk = q.shape
    dv = v.shape[-1]
    n = t // HGRN_CHUNK

    def to_chunks(a):
        return jnp.moveaxis(a.astype(jnp.float32).reshape(bsz, h, n, HGRN_CHUNK, a.shape[-1]), 2, 0)

    pos = jnp.arange(HGRN_CHUNK)
    lower_tri = pos[:, None] >= pos[None, :]

    def step(s, inp):
        qc, kc, vc, gc = inp
        b = jnp.cumsum(gc, axis=-2)
        b_last = b[..., -1:, :]
        inter = jnp.einsum('bhtd,bhde->bhte', qc * jnp.exp(b), s)
        diff = b[..., :, None, :] - b[..., None, :, :]
        decay = jnp.where(lower_tri[:, :, None], jnp.exp(jnp.minimum(diff, 0.0)), 0.0)
        scores = jnp.einsum('bhtd,bhsd,bhtsd->bhts', qc, kc, decay)
        intra = jnp.einsum('bhts,bhse->bhte', scores, vc)
        s_new = jnp.exp(b_last[..., 0, :])[..., None] * s + jnp.einsum(
            'bhsd,bhse->bhde', kc * jnp.exp(b_last - b), vc)
        return s_new, intra + inter

    s_fin, o = lax.scan(step, s0, (to_chunks(q), to_chunks(k), to_chunks(v), to_chunks(log_f)))
    o = jnp.moveaxis(o, 0, 2).reshape(bsz, h, t, dv)
    return o, s_fin


def hgrn2_bidir(q, v, z_fwd, z_bwd, lb_fwd, lb_bwd, s0_fwd, s0_bwd):
    g_f, k_f = forget_gate(z_fwd, lb_fwd)
    g_b, k_b = forget_gate(z_bwd, lb_bwd)
    qh, vh = split_heads(q), split_heads(v)
    o_f, s_f = hgrn2_scan(qh, split_heads(k_f), vh, split_heads(g_f), s0_fwd)
    flip = lambda a: jnp.flip(a, axis=2)
    o_b, s_b = hgrn2_scan(flip(qh), flip(split_heads(k_b)), flip(vh), flip(split_heads(g_b)), s0_bwd)
    o = merge_heads(o_f + flip(o_b)).astype(q.dtype)
    return o, s_f, s_b


def head_rmsnorm(o, g):
    bsz, t, _ = o.shape
    oh = o.reshape(bsz, t, HGRN_HEADS, HGRN_HEAD_DIM)
    of = oh.astype(jnp.float32)
    of = of * lax.rsqrt(jnp.mean(of * of, axis=-1, keepdims=True) + EPS)
    return (of.reshape(bsz, t, D_HGRN) * g.astype(jnp.float32)).astype(o.dtype)


def depthwise_conv2d(x, k):
    return lax.conv_general_dilated(x, k.astype(x.dtype), window_strides=(1, 1), padding='SAME',
                                    dimension_numbers=('NHWC', 'HWIO', 'NHWC'),
                                    feature_group_count=x.shape[-1])


def conv_tail(y, conv_b_l, ln_g_l, ln_b_l):
    return jax.nn.silu(layernorm(y + conv_b_l, ln_g_l, ln_b_l))


def conformer_conv_latent(u, rows, conv_w_l, conv_b_l, ln_g_l, ln_b_l):
    bsz, t, ch = u.shape
    half = ch // 2
    grid = u.reshape(bsz, rows, GRID_W, ch)
    k_row = conv_w_l[:, :half].reshape(1, CONV_WIDTH, 1, half)
    k_col = conv_w_l[:, half:].reshape(CONV_WIDTH, 1, 1, half)
    y = jnp.concatenate([depthwise_conv2d(grid[..., :half], k_row),
                         depthwise_conv2d(grid[..., half:], k_col)], axis=-1).reshape(bsz, t, ch)
    return conv_tail(y, conv_b_l, ln_g_l, ln_b_l)


def conformer_conv_context(u, conv_w_l, conv_b_l, ln_g_l, ln_b_l):
    bsz, t, ch = u.shape
    y = depthwise_conv2d(u[:, None], conv_w_l.reshape(1, CONV_WIDTH, 1, ch)).reshape(bsz, t, ch)
    return conv_tail(y, conv_b_l, ln_g_l, ln_b_l)


def merge_branches(o_hgrn, g_a, conv_y, g_b, hgrn_norm_g_l, w_out_l):
    branch_a = head_rmsnorm(o_hgrn, hgrn_norm_g_l) * jax.nn.silu(g_a)
    branch_b = conv_y * jax.nn.silu(g_b)
    return jnp.concatenate([branch_a, branch_b], axis=-1) @ w_out_l


def setup_inputs(seed: int = 0) -> dict:
    key = jax.random.key(seed)
    ks = jax.random.split(key, 20)
    nrm = jax.random.normal
    f32 = jnp.float32
    return {
        "x": nrm(ks[0], (BATCH, SEQ, D_MODEL), f32),
        "c": nrm(ks[1], (BATCH, D_MODEL), f32),
        "ctx": nrm(ks[2], (BATCH, CTX_LEN, D_MODEL), f32),
        "c_ctx": nrm(ks[3], (D_MODEL,), f32),
        "norm_g": 1.0 + 0.02 * nrm(ks[4], (DEPTH, D_MODEL), f32),
        "w_mod": 0.5 * D_MODEL ** -0.5 * nrm(ks[5], (DEPTH, D_MODEL, 3 * D_MODEL), f32),
        "b_mod": 0.02 * nrm(ks[6], (DEPTH, 3 * D_MODEL), f32),
        "w_in": D_MODEL ** -0.5 * nrm(ks[7], (DEPTH, D_MODEL, D_IN), f32),
        "lb_logits": 0.5 * nrm(ks[8], (DEPTH + 1, 2, D_HGRN), f32),
        "hgrn_norm_g": 1.0 + 0.02 * nrm(ks[9], (DEPTH, D_HGRN), f32),
        "conv_w": CONV_WIDTH ** -0.5 * nrm(ks[10], (DEPTH, CONV_WIDTH, D_CONV), f32),
        "conv_b": 0.02 * nrm(ks[11], (DEPTH, D_CONV), f32),
        "conv_ln_g": 1.0 + 0.02 * nrm(ks[12], (DEPTH, D_CONV), f32),
        "conv_ln_b": 0.02 * nrm(ks[13], (DEPTH, D_CONV), f32),
        "w_out": D_MIX ** -0.5 * nrm(ks[14], (DEPTH, D_MIX, D_MODEL), f32),
        "final_norm_g": 1.0 + 0.02 * nrm(ks[15], (D_MODEL,), f32),
    }


def reference(x, c, ctx, c_ctx, norm_g, w_mod, b_mod, w_in, lb_logits, hgrn_norm_g,
              conv_w, conv_b, conv_ln_g, conv_ln_b, w_out, final_norm_g):
    bsz, seq_len, _ = x.shape
    rows = seq_len // GRID_W
    lower_bounds = jnp.cumsum(jax.nn.softmax(lb_logits.astype(jnp.float32), axis=0), axis=0)
    zero_state = jnp.zeros((bsz, HGRN_HEADS, HGRN_HEAD_DIM, HGRN_HEAD_DIM), jnp.float32)
    h_lat, h_ctx = x, ctx
    for l in range(DEPTH):
        mod_lat = jax.nn.silu(c) @ w_mod[l] + b_mod[l]
        mod_ctx = jax.nn.silu(c_ctx) @ w_mod[l] + b_mod[l]
        sh_l, sc_l, gt_l = jnp.split(mod_lat[:, None, :], 3, axis=-1)
        sh_c, sc_c, gt_c = jnp.split(mod_ctx, 3, axis=-1)
        a_lat = rmsnorm(h_lat, norm_g[l]) * (1.0 + sc_l) + sh_l
        a_ctx = rmsnorm(h_ctx, norm_g[l]) * (1.0 + sc_c) + sh_c
        q_l, zf_l, zb_l, v_l, ga_l, u_l, ug_l, gb_l = jnp.split(a_lat @ w_in[l], SPLITS, axis=-1)
        q_c, zf_c, zb_c, v_c, ga_c, u_c, ug_c, gb_c = jnp.split(a_ctx @ w_in[l], SPLITS, axis=-1)
        lb_f, lb_b = lower_bounds[l, 0], lower_bounds[l, 1]
        o_c, s_f, s_b = hgrn2_bidir(q_c, v_c, zf_c, zb_c, lb_f, lb_b, zero_state, zero_state)
        o_l, _, _ = hgrn2_bidir(q_l, v_l, zf_l, zb_l, lb_f, lb_b, s_f, s_b)
        y_l = conformer_conv_latent(u_l * jax.nn.sigmoid(ug_l), rows, conv_w[l], conv_b[l],
                                    conv_ln_g[l], conv_ln_b[l])
        out_lat = merge_branches(o_l, ga_l, y_l, gb_l, hgrn_norm_g[l], w_out[l])
        if l < DEPTH - 1:
            y_c = conformer_conv_context(u_c * jax.nn.sigmoid(ug_c), conv_w[l], conv_b[l],
                                         conv_ln_g[l], conv_ln_b[l])
            h_ctx = h_ctx + gt_c * merge_branches(o_c, ga_c, y_c, gb_c, hgrn_norm_g[l], w_out[l])
        h_lat = h_lat + gt_l * out_lat
    return rmsnorm(h_lat, final_norm_g)
```

```python
import numpy as np
from contextlib import ExitStack

import concourse.bass as bass
import concourse.mybir as mybir
from concourse.bass_utils import run_bass_kernel_spmd

F32 = mybir.dt.float32
BF16 = mybir.dt.bfloat16
U32 = mybir.dt.uint32
AF = mybir.ActivationFunctionType
ALU = mybir.AluOpType
AX = mybir.AxisListType

N_CORES = 8
SEQ = 4096
DM = 1024
CTX = 256
D_IN = 4096
EPS = 1e-6
NTAP = 31
ARENA_BYTES = 207 * 1024

DEBUG = False
MAXPHASE = 99
SUB = 99
VAR = 1
LV = 99
LQ = 31
P5 = 99
S3 = 99
REORDER = True
REORDER_PE = True
SLACK = 0.25


def _isz(dt):
    return mybir.dt.size(dt)


class _Reg:
    __slots__ = ("name", "p0", "p1", "ivs", "lo", "hi")

    def __init__(self, name, p0, p1, ivs):
        self.name, self.p0, self.p1, self.ivs = name, p0, p1, ivs
        self.lo = ivs[0][0]
        self.hi = ivs[-1][1]


def _region(ap):
    t = ap.tensor
    if type(t).__name__.startswith("DRam"):
        return None
    isz = _isz(ap.dtype)
    dims = [tuple(d) for d in ap.ap]
    row = int(t.shape[-1]) * _isz(t.dtype) if len(t.shape) == 2 else None
    if row is None:
        n = 1
        for s in t.shape[1:]:
            n *= int(s)
        row = n * _isz(t.dtype)
    off = int(ap.offset) * isz
    p0 = off // row
    fo = off % row
    pn = dims[0][1]
    free = [(s * isz, c) for (s, c) in dims[1:] if c > 1]
    span = isz
    for s, c in free:
        span += abs(s) * (c - 1)
    ivs = [(fo, fo + span)]
    if len(free) == 2:
        (s0, c0), (s1, c1) = free
        if s1 == isz and c0 <= 64 and c1 * isz < s0:
            ivs = [(fo + i * s0, fo + i * s0 + c1 * isz) for i in range(c0)]
    elif len(free) == 3:
        (s0, c0), (s1, c1), (s2, c2) = free
        if s2 == isz and c0 * c1 <= 64 and c2 * isz < s1 and s1 * c1 <= s0:
            ivs = [(fo + i * s0 + j * s1, fo + i * s0 + j * s1 + c2 * isz)
                   for i in range(c0) for j in range(c1)]
    return _Reg(t.name, p0, p0 + pn, ivs)


def _overlap(a, b):
    if a.p1 <= b.p0 or b.p1 <= a.p0 or a.hi <= b.lo or b.hi <= a.lo:
        return False
    if len(a.ivs) == 1 and len(b.ivs) == 1:
        return True
    for (l0, h0) in a.ivs:
        for (l1, h1) in b.ivs:
            if l0 < h1 and l1 < h0:
                return True
    return False


def _covers(w, r):
    if w.p0 > r.p0 or w.p1 < r.p1:
        return False
    for (l, h) in r.ivs:
        ok = False
        for (wl, wh) in w.ivs:
            if wl <= l and h <= wh:
                ok = True
                break
        if not ok:
            return False
    return True


class _Op:
    __slots__ = ("eng", "fn", "deps", "raw", "signal", "count", "is_dma", "semi", "target", "idx", "dur", "sdeps")


class Sched:
    NDMA = 8

    def __init__(self, nc, es):
        self.nc = nc
        self.h = {"pe": nc.tensor, "act": nc.scalar, "dve": nc.vector, "pool": nc.gpsimd, "sp": nc.sync}
        self.sem = {e: es.enter_context(nc.semaphore("s_" + e)) for e in ("pe", "act", "dve", "pool")}
        self.dsem = [es.enter_context(nc.semaphore("s_dma%d" % i)) for i in range(self.NDMA)]
        self.ops = []
        self.rec = {}
        self.ndma = 0
        self.last_dma = None
        self.last_on = {}

    def op(self, eng, fn, reads=(), writes=(), dma=False, dur=0.3):
        o = _Op()
        o.dur = dur
        o.eng, o.fn, o.is_dma = eng, fn, dma
        o.signal, o.count, o.idx = False, 0, len(self.ops)
        deps = {}

        def add(d, raw):
            k = d.idx
            if k in deps:
                deps[k] = (d, deps[k][1] or raw)
            else:
                deps[k] = (d, raw)

        rregs = [r for r in (_region(a) for a in reads if a is not None) if r is not None]
        wregs = [r for r in (_region(a) for a in writes if a is not None) if r is not None]
        def _bankify(w):
            if w.name != "ps":
                return w
            return _Reg(w.name, 0, 128, [((w.lo >> 11) << 11, (((w.hi - 1) >> 11) << 11) + 2048)])
        wregs = [_bankify(w) for w in wregs]
        rregs = [_bankify(r) for r in rregs]
        def keys(r):
            return [(r.name, b) for b in range(r.lo >> 13, ((r.hi - 1) >> 13) + 1)]

        for r in rregs:
            for key in keys(r):
                for (reg, d, isw) in self.rec.get(key, ()):
                    if (isw or (r.name == "ps" and d.eng != eng)) and _overlap(reg, r):
                        add(d, True)
        for w in wregs:
            for key in keys(w):
                for (reg, d, isw) in self.rec.get(key, ()):
                    if _overlap(reg, w):
                        add(d, False)
        o.deps = []
        o.sdeps = [d for (d, raw) in deps.values()]
        if eng in (("sp",) if REORDER_PE else ("pe", "sp")) and self.last_on.get(eng) is not None:
            o.sdeps.append(self.last_on[eng])
        self.last_on[eng] = o
        for (d, raw) in deps.values():
            if d.is_dma:
                o.deps.append(d)
                continue
            if d.eng == eng and eng == "pe":
                continue
            o.deps.append(d)
            d.signal = True
        for w in wregs:
            for key in keys(w):
                lst = self.rec.setdefault(key, [])
                lst[:] = [x for x in lst if not _covers(w, x[0])]
                lst.append((w, o, True))
        for r in rregs:
            for key in keys(r):
                lst = self.rec.setdefault(key, [])
                if eng == "pe" and not REORDER_PE:
                    lst[:] = [x for x in lst if not ((not x[2]) and x[1].eng == eng and _covers(r, x[0]))]
                lst.append((r, o, False))
        if dma:
            o.semi = self.ndma % self.NDMA
            o.target = 16 * (self.ndma // self.NDMA + 1)
            self.ndma += 1
            self.last_dma = o
        self.ops.append(o)
        return o

    def reorder(self, window=600):
        ops = self.ops
        n = len(ops)
        succ = [[] for _ in range(n)]
        indeg = [0] * n
        for o in ops:
            seen = set()
            for d in o.sdeps:
                if d.idx in seen:
                    continue
                seen.add(d.idx)
                succ[d.idx].append(o.idx)
                indeg[o.idx] += 1
        bl = [0.0] * n
        for i in range(n - 1, -1, -1):
            m = 0.0
            for j in succ[i]:
                if bl[j] > m:
                    m = bl[j]
            bl[i] = ops[i].dur + m
        fin = [0.0] * n
        efree = {e: 0.0 for e in self.h}
        ready = [i for i in range(n) if indeg[i] == 0]
        rstart = {i: 0.0 for i in ready}
        done = [False] * n
        lo = 0
        order = []
        while ready:
            while lo < n and done[lo]:
                lo += 1
            best, bkey = None, None
            cands = []
            for i in ready:
                if i > lo + window:
                    continue
                st = max(efree[ops[i].eng], rstart[i])
                cands.append((st, i))
                key = (st, i)
                if bkey is None or key < bkey:
                    best, bkey = i, key
            if best is not None and SLACK > 0:
                lim = bkey[0] + SLACK
                bb = None
                for (st, i) in cands:
                    if st <= lim:
                        k2 = (-bl[i], st, i)
                        if bb is None or k2 < bb:
                            bb = k2
                best = bb[2]
                bkey = (bb[1], best)
            if best is None:
                best = min(ready)
                bkey = (max(efree[ops[best].eng], rstart[best]), best)
            o = ops[best]
            ready.remove(best)
            done[best] = True
            order.append(o)
            f = bkey[0] + o.dur
            if o.is_dma:
                efree[o.eng] = bkey[0] + 0.1
            else:
                efree[o.eng] = f
            fin[best] = f
            for j in succ[best]:
                indeg[j] -= 1
                lat = 0.0 if ops[j].eng == o.eng else 0.2
                rstart[j] = max(rstart.get(j, 0.0), f + lat)
                if indeg[j] == 0:
                    ready.append(j)
        assert len(order) == n, (len(order), n)
        self.ops = order
        self.est_total = max(fin) if fin else 0.0

    def emit(self):
        cnt = {e: 0 for e in self.sem}
        for o in self.ops:
            if (not o.is_dma) and o.signal:
                cnt[o.eng] += 1
                o.count = cnt[o.eng]
        waited = {}
        last_dma_on = {}
        for o in self.ops:
            eng = self.h[o.eng]
            waits = {}
            for d in o.deps:
                if d.is_dma:
                    key = ("d", d.semi)
                    waits[key] = max(waits.get(key, 0), d.target)
                else:
                    key = ("e", d.eng)
                    waits[key] = max(waits.get(key, 0), d.count)
            if o.is_dma and o.target > 16:
                key = ("d", o.semi)
                waits[key] = max(waits.get(key, 0), o.target - 16)
            for key, val in waits.items():
                wk = (o.eng, key)
                if waited.get(wk, 0) >= val:
                    continue
                waited[wk] = val
                s = self.dsem[key[1]] if key[0] == "d" else self.sem[key[1]]
                eng.wait_ge(s, val)
            ins = o.fn(eng)
            if o.is_dma:
                ins.then_inc(self.dsem[o.semi], 16)
                last_dma_on[o.semi] = o.target
            elif o.signal:
                ins.then_inc(self.sem[o.eng], 1)
        for semi, tgt in last_dma_on.items():
            self.h["sp"].wait_ge(self.dsem[semi], tgt)
        for e, c in cnt.items():
            if c > 0:
                self.h["sp"].wait_ge(self.sem[e], c)


def build_program():
    nc = bass.Bass("TRN2", target_bir_lowering=False)
    dr = {}

    def din(name, shape, dt=F32):
        dr[name] = nc.dram_tensor(name, list(shape), dt, kind="ExternalInput").ap()
        return dr[name]

    x_d = din("x", [SEQ, DM])
    ctx_d = din("ctx", [CTX, DM])
    wmod_d = din("w_mod", [DM, 3 * DM])
    bmod_d = din("b_mod2", [2, 3 * DM])
    win_d = din("w_in", [DM, D_IN])
    wout_d = din("w_out", [DM, DM])
    vecs_d = din("vecs", [128, 192])
    ident_d = din("ident", [128, 128])
    masks_d = din("masks", [128, 256], F32)
    sel_d = din("sel", [2, 256])
    fg_d = din("fg_rep", [128, DM])
    rmask_d = din("rmask", [128, 512])
    out_d = nc.dram_tensor("out", [SEQ, DM], F32, kind="ExternalOutput").ap()
    if DEBUG:
        dbg_d = nc.dram_tensor("dbg", [128, 16 * 4096], BF16, kind="ExternalOutput").ap()

    es = ExitStack()
    arena = es.enter_context(nc.sbuf_tensor("arena", [128, ARENA_BYTES // 4], F32))
    psum = es.enter_context(nc.psum_tensor("ps", [128, 4096], F32))
    S = Sched(nc, es)

    def A(off, nbytes, dt, parts=(0, 128)):
        assert off % 4 == 0 and nbytes % 4 == 0 and off + nbytes <= ARENA_BYTES, (off, nbytes)
        v = arena[parts[0]:parts[1], off // 4:(off + nbytes) // 4]
        if dt != F32:
            v = v.bitcast(dt)
        return v

    def bank(i):
        return psum[:, i * 512:(i + 1) * 512]

    def _n(ap):
        n = 1
        for d in list(ap.shape)[1:]:
            n *= int(d)
        return n

    def dma(out, in_):
        nb = _n(out) * _isz(out.dtype) * int(out.shape[0])
        return S.op("sp", lambda e: e.dma_start(out=out, in_=in_), reads=[in_], writes=[out], dma=True,
                    dur=2.0 + nb / 150e3)

    def mm(out, lhsT, rhs, start=True, stop=True):
        return S.op("pe", lambda e: e.matmul(out, lhsT, rhs, start=start, stop=stop),
                    reads=[lhsT, rhs], writes=[out],
                    dur=(0.06 + max(_n(rhs), 64) * 0.0005) * (4 if rhs.dtype == F32 else 1))

    def tr(out, in_, ident):
        return S.op("pe", lambda e: e.transpose(out, in_, ident), reads=[in_, ident], writes=[out], dur=0.1)

    def act(out, in_, func, bias=None, scale=None, accum_out=None):
        kw = {}
        if (bias is not None and not isinstance(bias, (int, float))
                and (scale is None or (isinstance(scale, (int, float)) and float(scale) == 1.0))
                and func != AF.Ln):
            scale = onecol
        if bias is not None:
            kw["bias"] = bias
        if scale is not None:
            kw["scale"] = scale
        if accum_out is not None:
            kw["accum_out"] = accum_out
        rd = [in_] + [a for a in (bias, scale) if a is not None and not isinstance(a, (int, float))]
        return S.op("act", lambda e: e.activation(out, in_, func, **kw), reads=rd,
                    writes=[out] + ([accum_out] if accum_out is not None else []), dur=0.22 + _n(in_) / 1400.0)

    def ts(eng, out, in0, s1, s2, op0, op1=None):
        rd = [in0] + [a for a in (s1, s2) if a is not None and not isinstance(a, (int, float))]
        du = (0.07 + _n(in0) / 960.0) if eng == "dve" else (0.3 + _n(in0) / 450.0)
        if op1 is None:
            return S.op(eng, lambda e: e.tensor_scalar(out, in0, s1, None, op0), reads=rd, writes=[out], dur=du)
        return S.op(eng, lambda e: e.tensor_scalar(out, in0, s1, s2, op0, op1), reads=rd, writes=[out], dur=du)

    def tt(eng, out, in0, in1, op):
        du = (0.07 + _n(in0) / 960.0) if eng == "dve" else (0.3 + _n(in0) / 450.0)
        return S.op(eng, lambda e: e.tensor_tensor(out, in0, in1, op), reads=[in0, in1], writes=[out], dur=du)

    def stt(out, in0, scalar, in1, op0, op1):
        rd = [in0, in1] + ([scalar] if not isinstance(scalar, (int, float)) else [])
        return S.op("dve", lambda e: e.scalar_tensor_tensor(out, in0, scalar, in1, op0, op1),
                    reads=rd, writes=[out], dur=0.07 + _n(in0) / 960.0)

    def cp(eng, out, in_):
        du = (0.07 + _n(in_) / 960.0) if eng == "dve" else (0.3 + _n(in_) / 450.0)
        return S.op(eng, lambda e: e.tensor_copy(out, in_), reads=[in_], writes=[out], dur=du)

    def memset(eng, ap, val):
        du = (0.07 + _n(ap) / 960.0) if eng == "dve" else (0.3 + _n(ap) / 450.0)
        return S.op(eng, lambda e: e.memset(ap, val), reads=[], writes=[ap], dur=du)

    AT0 = 0
    BR0 = 65536
    P0 = 131072
    aT = A(AT0, 65536, BF16).rearrange("p (k t) -> p k t", k=8)
    br = A(BR0, 65536, BF16).rearrange("p (k t) -> p k t", k=8)
    o = P0
    ident = A(o, 512, F32); o += 512
    ident_bf = A(o, 256, BF16); o += 256
    ones_bf = A(o, 256, BF16); o += 256
    masks = A(o, 1024, F32); o += 1024
    maskF, maskB = masks[:, 0:128], masks[:, 128:256]
    vecs = A(o, 768, F32); o += 768
    dv = A(o, 1024, F32); o += 1024
    gt_rep = A(o, 4096, F32); o += 4096
    aTc = A(o, 4096, BF16).rearrange("p (k t) -> p k t", k=8); o += 4096
    rmask = A(o, 2048, F32); o += 2048
    small = A(o, 1024, F32); o += 1024
    onecol = small[:, 255:256]
    PH0 = o
    PH_BYTES = ARENA_BYTES - PH0

    cv = vecs[:, 0:16].rearrange("p (k c) -> p k c", c=2)
    ng = vecs[:, 16:24]
    lbl = vecs[:, 24:40]
    cw = vecs[:, 44:168].rearrange("p (s j) -> p s j", j=NTAP)
    convb = vecs[:, 168:172]
    lng = vecs[:, 172:176]
    lnb = vecs[:, 176:180]
    gs_l, sh_l, gs_c, sh_c = dv[:, 0:8], dv[:, 8:16], dv[:, 16:24], dv[:, 24:32]
    lbv, l1mlb = dv[:, 32:40], dv[:, 40:48]
    cwh = dv[:, 48:172].rearrange("p (s j) -> p s j", j=NTAP)
    tmp8 = dv[:, 172:236]

    dma(vecs, vecs_d)
    dma(ident, ident_d)
    dma(masks, masks_d)
    dma(rmask, rmask_d)
    cp("dve", ident_bf, ident)
    memset("pool", ones_bf, 1.0)
    memset("pool", onecol, 1.0)

    o = PH0
    mod_sb = A(o, 12288, F32, parts=(0, 2)); o += 12288
    bmod = A(o, 12288, F32, parts=(0, 2)); o += 12288
    wm = []
    for i in range(2):
        wm.append(A(o, 8192, F32).rearrange("p (k n) -> p k n", k=8)); o += 8192
    xs = []
    for i in range(3):
        xs.append(A(o, 4096, F32)); o += 4096
    junk = A(o, 2048, BF16); o += 2048
    sel = A(o, 1024, F32, parts=(0, 2)); o += 1024
    assert o <= ARENA_BYTES, o
    dma(sel, sel_d)

    dma(bmod, bmod_d)
    act(vecs[:, 0:16], vecs[:, 0:16], AF.Silu)
    wmod_v = wmod_d.rearrange("(k p) n -> p k n", p=128)
    for blk in range(12):
        w = wm[blk % 2]
        dma(w, wmod_v[:, :, blk * 256:(blk + 1) * 256])
        pb = bank(blk % 2)[0:2, 0:256]
        for k in range(8):
            mm(pb, cv[:, k, :], w[:, k, :], start=(k == 0), stop=(k == 7))
        tt("dve", mod_sb[:, blk * 256:(blk + 1) * 256], pb, bmod[:, blk * 256:(blk + 1) * 256], ALU.add)

    def extract(dst8, cb0, selv):
        for half in range(2):
            pb = bank(2 + half)
            mm(pb, selv, mod_sb[:, (cb0 + half) * 512:(cb0 + half + 1) * 512])
            for kk in range(4):
                k = half * 4 + kk
                scr = junk.bitcast(F32)[:, 0:128]
                tt("dve", scr, pb[:, kk * 128:(kk + 1) * 128], ident, ALU.mult)
                S.op("dve", lambda e, scr=scr, d=dst8[:, k:k + 1]: e.tensor_reduce(d, scr, AX.X, ALU.add),
                     reads=[scr], writes=[dst8[:, k:k + 1]])

    sel_l, sel_c = sel[:, 0:128], sel[:, 128:256]
    extract(sh_l, 0, sel_l)
    extract(gs_l, 2, sel_l)
    extract(sh_c, 0, sel_c)
    extract(gs_c, 2, sel_c)
    for half in range(2):
        pb = bank(2 + half)
        mm(pb, sel_l, mod_sb[:, (4 + half) * 512:(5 + half) * 512])
        act(gt_rep[:, half * 512:(half + 1) * 512], pb, AF.Copy)
    for g in (gs_l, gs_c):
        stt(g, g, 1.0, ng, ALU.add, ALU.mult)
    dlt = small[:, 0:8]
    tt("dve", dlt, lbl[:, 0:8], lbl[:, 8:16], ALU.subtract)
    act(dlt, dlt, AF.Exp, scale=-1.0)
    ts("dve", dlt, dlt, 1.0, None, ALU.add)
    S.op("dve", lambda e: e.reciprocal(lbv, dlt), reads=[dlt], writes=[lbv])
    act(l1mlb, lbv, AF.Ln, scale=-1.0, bias=1.0)
    ts("dve", dv[:, 48:172], vecs[:, 44:168], 0.5, None, ALU.mult)

    if MAXPHASE >= 1:
        ssv = small[:, 16:80]
        tile_ctr = [0]

        def norm_tile(src_rows, dst_fn, gs, sh):
            i = tile_ctr[0]
            tile_ctr[0] += 1
            xt = xs[i % 3]
            ss = ssv[:, (i % 16) * 2:(i % 16) * 2 + 1]
            rs = ssv[:, (i % 16) * 2 + 1:(i % 16) * 2 + 2]
            dma(xt, src_rows)
            act(junk, xt, AF.Square, accum_out=ss)
            act(rs, ss, AF.Ln, scale=1.0 / DM, bias=EPS)
            act(rs, rs, AF.Exp, scale=-0.5)
            act(xt, xt, AF.Copy, scale=rs)
            for k in range(8):
                pb = bank(4 + (i % 2) * 2 + k // 4)
                tr(pb[:, (k % 4) * 128:(k % 4 + 1) * 128], xt[:, k * 128:(k + 1) * 128], ident)
            for k in range(8):
                pb = bank(4 + (i % 2) * 2 + k // 4)
                ts("dve", dst_fn(k), pb[:, (k % 4) * 128:(k % 4 + 1) * 128], gs[:, k:k + 1], sh[:, k:k + 1],
                   ALU.mult, ALU.add)

        for t in range(2):
            norm_tile(ctx_d[t * 128:(t + 1) * 128, :], lambda k, t=t: aTc[:, k, t * 128:(t + 1) * 128], gs_c, sh_c)
        for t in range(32):
            norm_tile(x_d[t * 128:(t + 1) * 128, :], lambda k, t=t: aT[:, k, t * 128:(t + 1) * 128], gs_l, sh_l)

    win_v = win_d.rearrange("(k p) n -> p k n", p=128)

    def load_w(dst_bf, stage, col0, ncols=128):
        dma(stage, win_v[:, :, col0:col0 + ncols])
        act(dst_bf, stage, AF.Copy)

    if MAXPHASE >= 2:
        o = PH0
        wst = []
        for i in range(2):
            wst.append(A(o, 4096, F32).rearrange("p (k n) -> p k n", k=8)); o += 4096
        wgb = []
        for i in range(4):
            wgb.append(A(o, 2048, BF16).rearrange("p (k n) -> p k n", k=8)); o += 2048
        cB0 = o
        wu = []
        for i in range(4):
            wu.append(A(o, 2048, BF16).rearrange("p (k n) -> p k n", k=8)); o += 2048
        gin = A(o, 12288, BF16); o += 12288
        diag = []
        for i in range(2):
            diag.append(A(o, NTAP * 256, BF16).rearrange("p (j n) -> p j n", j=NTAP)); o += NTAP * 256
        th = []
        for i in range(2):
            th.append(A(o, 2048, F32)); o += 2048
        assert o <= ARENA_BYTES, o

        for s in (range(4) if SUB > 10 else ([2] if SUB == 6 else [0])):
            rowmode = s < 2
            wU, wG = wu[(s % 2) * 2], wu[(s % 2) * 2 + 1]
            load_w(wU, wst[0], 2560 + s * 128)
            load_w(wG, wst[1], 3072 + s * 128)
            dg = diag[s % 2]
            for j in range(NTAP if SUB >= 2 else 0):
                ts("dve", dg[:, j, :], ident_bf, cwh[:, s, j:j + 1], None, ALU.mult)
            if (s == 0 or s == 2) and SUB >= 3:
                memset("pool", gin, 0.0)
            for blk in range(8 if SUB >= 4 else 0):
                pu, pg = bank(0 + (blk % 2) * 2), bank(1 + (blk % 2) * 2)
                for k in range(8):
                    mm(pu, wU[:, k, :], aT[:, k, blk * 512:(blk + 1) * 512], start=(k == 0), stop=(k == 7))
                for k in range(8):
                    mm(pg, wG[:, k, :], aT[:, k, blk * 512:(blk + 1) * 512], start=(k == 0), stop=(k == 7))
                tb = th[blk % 2]
                act(tb, pg, AF.Tanh, scale=0.5)
                if rowmode:
                    r0 = blk * 8
                    dst = gin[:, r0 * 80 + 15:r0 * 80 + 15 + 640].rearrange("p (r c) -> p r c", c=80)[:, :, 0:64]
                else:
                    dst = gin[:, (blk * 8 + 15) * 64:(blk * 8 + 15) * 64 + 512].rearrange("p (r c) -> p r c", c=64)
                stt(dst, tb.rearrange("p (r c) -> p r c", c=64), 1.0, pu.rearrange("p (r c) -> p r c", c=64),
                    ALU.add, ALU.mult)
            for blk in range(8 if SUB >= 5 else 0):
                py = bank(4 + blk % 2)
                for j in range(NTAP):
                    if rowmode:
                        r0 = blk * 8
                        src = gin[:, r0 * 80 + j:r0 * 80 + j + 640].rearrange("p (r c) -> p r c", c=80)[:, :, 0:64]
                        dstp = py.rearrange("p (r c) -> p r c", c=64)
                    else:
                        src = gin[:, (blk * 8 + j) * 64:(blk * 8 + j) * 64 + 512]
                        dstp = py
                    mm(dstp, dg[:, j, :], src, start=(j == 0), stop=(j == NTAP - 1))
                if VAR == 0:
                    act(br[:, 4 + s, blk * 512:(blk + 1) * 512], py, AF.Identity, bias=convb[:, s:s + 1])
                elif VAR == 1:
                    ts("dve", br[:, 4 + s, blk * 512:(blk + 1) * 512], py, convb[:, s:s + 1], None, ALU.add)
                elif VAR == 2:
                    act(br[:, 4 + s, blk * 512:(blk + 1) * 512], py, AF.Copy)

    if MAXPHASE >= 3:
        o = cB0
        sqb = []
        for i in range(4):
            sqb.append(A(o, 1024, BF16)); o += 1024
        mean = A(o, 2048, F32); o += 2048
        msq = A(o, 2048, F32); o += 2048
        rstdc = A(o, 2048, F32); o += 2048
        mhalf = A(o, 2048, F32); o += 2048
        tbuf = []
        for i in range(2):
            tbuf.append(A(o, 2048, F32)); o += 2048
        sgb = []
        for i in range(2):
            sgb.append(A(o, 2048, F32)); o += 2048
        assert o <= ARENA_BYTES, o
        memset("pool", mhalf, -0.5)
        for s in range(4):
            load_w(wgb[s], wst[s % 2], 3584 + s * 128)
        for blk in range(8):
            sl = slice(blk * 512, (blk + 1) * 512)
            p1, p2 = bank(0 + (blk % 2) * 4), bank(1 + (blk % 2) * 4)
            for s in range(4):
                act(sqb[s], br[:, 4 + s, sl], AF.Square)
            for s in range(4):
                mm(p1, ones_bf, br[:, 4 + s, sl], start=(s == 0), stop=(s == 3))
            for s in range(4):
                mm(p2, ones_bf, sqb[s], start=(s == 0), stop=(s == 3))
            act(mean, p1, AF.Copy, scale=1.0 / 512)
            tt("dve", msq, mean, mean, ALU.mult)
            ts("dve", rstdc, p2, 1.0 / 512, EPS, ALU.mult, ALU.add)
            tt("dve", rstdc, rstdc, msq, ALU.subtract)
            act(rstdc, rstdc, AF.Ln)
            act(rstdc, rstdc, AF.Exp, scale=-0.5)
            for s in range(4):
                tb = tbuf[s % 2]
                tt("dve", tb, br[:, 4 + s, sl], mean, ALU.subtract)
                tt("dve", tb, tb, rstdc, ALU.mult)
                act(tb, tb, AF.Silu, scale=lng[:, s:s + 1], bias=lnb[:, s:s + 1])
                pgb = bank(2 + s % 2)
                for k in range(8):
                    mm(pgb, wgb[s][:, k, :], aT[:, k, sl], start=(k == 0), stop=(k == 7))
                sg = sgb[s % 2]
                act(sg, pgb, AF.Silu)
                tt("dve", br[:, 4 + s, sl], tb, sg, ALU.mult)

    if MAXPHASE >= 4:
        o = PH0
        wst3 = A(o, 4096, F32).rearrange("p (k n) -> p k n", k=8); o += 4096
        wAs, wBs = [], []
        for i in range(2):
            wAs.append(A(o, 2048, BF16).rearrange("p (k n) -> p k n", k=8)); o += 2048
            wBs.append(A(o, 2048, BF16).rearrange("p (k n) -> p k n", k=8)); o += 2048
        wC = A(o, 2048, BF16).rearrange("p (k n) -> p k n", k=8); o += 2048
        vh = A(o, 34 * 256, BF16).rearrange("p (c e) -> p c e", e=128); o += 34 * 256
        qbf = A(o, 8192, BF16); o += 8192
        e2b, l1b, l2b, bb, bxb = [], [], [], [], []
        for lst in (e2b, l1b, l2b, bb):
            for i in range(2):
                lst.append(A(o, 2048, F32)); o += 2048
        bxb.append(A(o, 2048, F32)); o += 2048
        bxb.append(bxb[0])
        qtb, ktb = [], []
        for lst in (qtb, ktb):
            for i in range(2):
                lst.append(A(o, 1024, BF16)); o += 1024
        kTall = [A(o, 1024, BF16)]; o += 1024
        kTall.append(kTall[0])
        osb1 = A(o, 2048, F32); o += 2048
        etmp = A(o, 2048, F32); o += 2048
        PFall = A(o, 1024, BF16); o += 1024
        PBall = A(o, 1024, BF16); o += 1024
        Zb = []
        for i in range(2):
            Zb.append(A(o, 256, BF16)); o += 256
        Yst = A(o, 512, F32); o += 512
        sc3 = A(o, 512, F32); o += 512
        sq3 = A(o, 1024, BF16); o += 1024
        Uall = A(o, 2048, F32); o += 2048
        assert o <= ARENA_BYTES, o

        memset("pool", PFall, 0.0)
        memset("pool", PBall, 0.0)

        blkctr = [0]
        chctr = [0]

        items = []
        sweep_no = [0]

        def make_vproj(h):
            def vproj():
                for c in range(34):
                    src = aTc[:, :, c * 128:(c + 1) * 128] if c < 2 else aT[:, :, (c - 2) * 128:(c - 1) * 128]
                    pv = bank(4 + c % 2)[:, 0:128]
                    for k in range(8):
                        mm(pv, src[:, k, :], wC[:, k, :], start=(k == 0), stop=(k == 7))
                    if c % 2 == 0:
                        act(vh[:, c, :], pv, AF.Copy)
                    else:
                        cp("dve", vh[:, c, :], pv)
                if h + 1 < NH:
                    load_w(wC, wst3, 1536 + (h + 1) * 128)
            return vproj

        NH = 4 if S3 > 10 else 1
        load_w(wC, wst3, 1536)
        for h in range(NH):
            head_init = make_vproj(h)

            for dirn in range(2):
                fwd = dirn == 0
                lb_ap = lbv[:, dirn * 4 + h:dirn * 4 + h + 1]
                l1m_ap = l1mlb[:, dirn * 4 + h:dirn * 4 + h + 1]
                sw = sweep_no[0]
                sweep_no[0] += 1
                wA, wB = wAs[sw % 2], wBs[sw % 2]

                def wload(wA=wA, wB=wB, fwd=fwd, h=h):
                    load_w(wA, wst3, (512 if fwd else 1024) + h * 128)
                    load_w(wB, wst3, (0 if fwd else 2048) + h * 128)

                def sweep_init():
                    memset("pool", Yst, 0.0)
                blocks = [("ctx", 0)] + [("lat", j) for j in (range(8) if fwd else range(7, -1, -1))]
                msk = maskF if fwd else maskB

                def stage1(kind, j, fwd=fwd, lb_ap=lb_ap, l1m_ap=l1m_ap, wA=wA, wB=wB):
                    bi = blkctr[0]
                    blkctr[0] += 1
                    par = bi % 2
                    lat = kind == "lat"
                    nt = 512 if lat else 256
                    nch = nt // 128
                    src = aT[:, :, j * 512:(j + 1) * 512] if lat else aTc
                    tsl = slice(j * 512, (j + 1) * 512)
                    pz = bank(0 + par * 2)[:, 0:nt]
                    pq = bank(1 + par * 2)[:, 0:nt]
                    for k in range(8):
                        mm(pz, wA[:, k, :], src[:, k, :], start=(k == 0), stop=(k == 7))
                    if lat and fwd:
                        for k in range(8):
                            mm(pq, wB[:, k, :], src[:, k, :], start=(k == 0), stop=(k == 7))
                    e2, l1, l2, bq, bx = (e2b[par][:, 0:nt], l1b[par][:, 0:nt], l2b[par][:, 0:nt],
                                          bb[par][:, 0:nt], bxb[par][:, 0:nt])
                    qt, kt = qtb[par][:, 0:nt], ktb[par][:, 0:nt]
                    act(e2, pz, AF.Exp)
                    act(l1, e2, AF.Ln, bias=lb_ap)
                    act(l2, e2, AF.Ln, bias=1.0)
                    tt("pool", l1, l1, l2, ALU.subtract)
                    S.op("dve", lambda e, bq=bq, l1=l1, nt=nt: e.tensor_tensor_scan(
                        bq, rmask[:, 0:nt], l1, 0.0, ALU.mult, ALU.add), reads=[rmask[:, 0:nt], l1], writes=[bq],
                        dur=0.1 + nt / 480.0)
                    bq3 = bq.rearrange("p (c t) -> p c t", t=128)
                    m4b = bq3[:, :, 63:64].to_broadcast([128, nch, 128])
                    bx3 = bx.rearrange("p (c t) -> p c t", t=128)
                    if fwd:
                        tt("dve", bx3, bq3, m4b, ALU.subtract)
                        tt("dve", l2, l2, bx, ALU.add)
                    else:
                        tt("dve", bx, bq, l1, ALU.subtract)
                        tt("dve", bx3, bx3, m4b, ALU.subtract)
                        tt("dve", l2, bx, l2, ALU.subtract)
                    base = 8 + par * 40
                    m4 = bq3[:, :, 63]
                    tot4 = bq3[:, :, 127]
                    d4 = sc3[:, base:base + nch]
                    cA4 = sc3[:, base + 12:base + 12 + nch]
                    cB4 = sc3[:, base + 16:base + 16 + nch]
                    tt("dve", d4, tot4, m4, ALU.subtract)
                    cAB4 = sc3[:, base + 20:base + 20 + nch]
                    if fwd:
                        act(cA4, m4, AF.Exp)
                        act(cB4, d4, AF.Exp)
                    else:
                        act(cA4, d4, AF.Exp)
                        act(cB4, m4, AF.Exp)
                    act(cAB4, tot4, AF.Exp)
                    if lat:
                        act(e2, bx, AF.Exp, scale=(1.0 if fwd else -1.0))
                    act(kt, l2, AF.Exp, scale=(-1.0 if fwd else 1.0), bias=l1m_ap)
                    if lat:
                        if fwd:
                            tt("dve", qt, pq, e2, ALU.mult)
                            act(qbf[:, tsl], pq, AF.Copy)
                        else:
                            tt("dve", qt, qbf[:, tsl], e2, ALU.mult)
                    return dict(par=par, lat=lat, nt=nt, nch=nch, src=src, tsl=tsl, pz=pz, pq=pq, qt=qt, kt=kt,
                                cA4=cA4, cB4=cB4, cAB4=cAB4, j=j)

                def stage1b(st):
                    par, nch, kt = st["par"], st["nch"], st["kt"]
                    pzb = bank(0 + par * 2).bitcast(BF16)
                    for c in range(nch):
                        tr(pzb[:, c * 128:(c + 1) * 128], kt[:, c * 128:(c + 1) * 128], ident_bf)
                    act(kTall[par][:, 0:nch * 128], pzb[:, 0:nch * 128], AF.Copy)

                def stage2(st, fwd=fwd, msk=msk, h=h, wB=wB):
                    par, lat, nch, tsl, qt, kt = st["par"], st["lat"], st["nch"], st["tsl"], st["qt"], st["kt"]
                    cA4, cB4, j, pq, src = st["cA4"], st["cB4"], st["j"], st["pq"], st["src"]
                    cAB4 = st["cAB4"]
                    po = bank(4 + par)
                    bD, bS = bank(6), bank(7)
                    Pall = PFall if fwd else PBall
                    kTa = kTall[par]
                    corder = list(range(nch)) if fwd else list(range(nch - 1, -1, -1))
                    for c in corder:
                        gch = (j * 4 + c + 2) if lat else c
                        cs = slice(c * 128, (c + 1) * 128)
                        mm(bD[:, cs], kTa[:, cs], vh[:, gch, :])
                    for c in corder:
                        cs = slice(c * 128, (c + 1) * 128)
                        ts("dve", Uall[:, cs], bD[:, cs], cB4[:, c:c + 1], None, ALU.mult)
                    if lat:
                        if fwd:
                            pieces = [(slice(0, 128), slice(64, 128)), (slice(0, 64), slice(0, 64))]
                        else:
                            pieces = [(slice(0, 128), slice(0, 64)), (slice(64, 128), slice(64, 128))]
                        for c in corder:
                            for (sp, tp) in pieces:
                                tq = slice(c * 128 + tp.start, c * 128 + tp.stop)
                                mm(bS[sp, tq], kt[:, c * 128 + sp.start:c * 128 + sp.stop], qt[:, tq])
                        for (sp, tp) in pieces:
                            w = tp.stop - tp.start
                            np_ = sp.stop - sp.start
                            P3 = Pall[sp, :].rearrange("p (c t) -> p c t", t=128)[:, :, tp]
                            S3v = bS[sp, :].rearrange("p (c t) -> p c t", t=128)[:, :, tp]
                            M3 = msk[sp, tp].unsqueeze(1).to_broadcast([np_, nch, w])
                            tt("dve", P3, S3v, M3, ALU.mult)
                    for c in corder:
                        ci = chctr[0]
                        chctr[0] += 1
                        cpar = ci % 2
                        cs = slice(c * 128, (c + 1) * 128)
                        gch = (j * 4 + c + 2) if lat else c
                        if lat:
                            ts("pool", Zb[cpar], Yst, cA4[:, c:c + 1], 1.0, ALU.mult, ALU.mult)
                            mm(po[:, cs], vh[:, gch, :], Pall[:, cs], start=True, stop=False)
                            mm(po[:, cs], Zb[cpar], qt[:, cs], start=False, stop=True)
                        stt(Yst, Yst, cAB4[:, c:c + 1], Uall[:, cs], ALU.mult, ALU.add)
                    if lat and fwd:
                        act(br[:, h, tsl], po, AF.Copy)
                    if lat and not fwd and S3 == 7:
                        act(br[:, h, tsl], po, AF.Copy)
                    elif lat and not fwd:
                        osb = osb1
                        tt("dve", osb, po, br[:, h, tsl], ALU.add)
                        act(sq3, osb, AF.Square)
                        pn = bank(0 + par * 2)
                        mm(pn, ones_bf, sq3)
                        for k in range(8):
                            mm(pq, wB[:, k, :], src[:, k, :], start=(k == 0), stop=(k == 7))
                        act(etmp, pn, AF.Ln, scale=1.0 / 128, bias=EPS)
                        act(etmp, etmp, AF.Exp, scale=-0.5)
                        tt("pool", osb, osb, etmp, ALU.mult)
                        act(etmp, pq, AF.Exp, scale=-1.0)
                        act(etmp, etmp, AF.Ln, bias=1.0)
                        act(etmp, etmp, AF.Exp, scale=-1.0)
                        tt("dve", etmp, pq, etmp, ALU.mult)
                        sg3 = etmp
                        tt("pool", br[:, h, tsl], osb, sg3, ALU.mult)

                for bi_, blk in enumerate(blocks):
                    items.append(dict(s1=stage1, s1b=stage1b, s2=stage2, blk=blk, first_sweep=(bi_ == 0),
                                      first_head=(bi_ == 0 and dirn == 0), wload=wload, sweep_init=sweep_init,
                                      head_init=head_init))

        sweeps_first = [it for it in items if it["first_sweep"]]
        sweeps_first[0]["wload"]()
        nxt_sweep = 1
        prev, prev_st = None, None
        for it in items:
            cur_st = it["s1"](*it["blk"])
            if prev is not None:
                if prev["first_head"]:
                    prev["head_init"]()
                if prev["first_sweep"]:
                    prev["sweep_init"]()
                prev["s2"](prev_st)
            if it["first_sweep"] and nxt_sweep < len(sweeps_first):
                sweeps_first[nxt_sweep]["wload"]()
                nxt_sweep += 1
            it["s1b"](cur_st)
            prev, prev_st = it, cur_st
        if prev["first_head"]:
            prev["head_init"]()
        if prev["first_sweep"]:
            prev["sweep_init"]()
        prev["s2"](prev_st)

    if MAXPHASE >= 5:
        o = PH0
        wo = A(o, 16384, BF16).rearrange("p (k n) -> p k n", k=8); o += 16384
        wos = A(o, 4096, F32); o += 4096
        xr = []
        for i in range(3):
            xr.append(A(o, 4096, F32)); o += 4096
        ht = []
        for i in range(2):
            ht.append(A(o, 4096, F32)); o += 4096
        junk4 = A(o, 2048, BF16); o += 2048
        fs = A(o, 512, F32); o += 512
        fg_rep = A(o, 4096, F32); o += 4096
        dma(fg_rep, fg_d)
        assert o <= ARENA_BYTES, o
        hgn = vecs[:, 40:44]
        wout_v = wout_d.rearrange("(k p) n -> p k n", p=128)
        for k in range(8):
            dma(wos, wout_v[:, k, :])
            if k < 4:
                stt(wo[:, k, :], wos, hgn[:, k:k + 1], gt_rep, ALU.mult, ALU.mult)
            else:
                tt("dve", wo[:, k, :], wos, gt_rep, ALU.mult)
        for t in range(2 if P5 >= 2 else 0):
            dma(xr[t % 3], x_d[t * 128:(t + 1) * 128, :])
        for t in range(32 if P5 >= 2 else 0):
            rows = slice(t * 128, (t + 1) * 128)
            xt = xr[t % 3]
            if t + 2 < 32:
                dma(xr[(t + 2) % 3], x_d[(t + 2) * 128:(t + 3) * 128, :])
            hb = ht[t % 2]
            for half in range(2):
                pb = bank((t % 2) * 2 + half)
                for g in range(8):
                    mm(pb, br[:, g, rows], wo[:, g, half * 512:(half + 1) * 512], start=(g == 0), stop=(g == 7))
                tt("dve", hb[:, half * 512:(half + 1) * 512], pb, xt[:, half * 512:(half + 1) * 512], ALU.add)
            ss = fs[:, (t % 16) * 2:(t % 16) * 2 + 1]
            rs = fs[:, (t % 16) * 2 + 1:(t % 16) * 2 + 2]
            if P5 >= 3:
                act(junk4, hb, AF.Square, accum_out=ss)
                act(rs, ss, AF.Ln, scale=1.0 / DM, bias=EPS)
                act(rs, rs, AF.Exp, scale=-0.5)
                stt(hb, hb, rs, fg_rep, ALU.mult, ALU.mult)
            if P5 >= 4:
                dma(out_d[rows, :], hb)

    if DEBUG:
        if MAXPHASE >= 1:
            dma(dbg_d[:, 0:8 * 4096], A(AT0, 65536, BF16))
        if MAXPHASE >= 2 and SUB > 10:
            dma(dbg_d[:, 12 * 4096:16 * 4096], A(BR0 + 32768, 32768, BF16))
        if MAXPHASE >= 4 and S3 > 10:
            dma(dbg_d[:, 8 * 4096:12 * 4096], A(BR0, 32768, BF16))
        if MAXPHASE >= 4 and S3 in (5, 7):
            dma(dbg_d[:, 8 * 4096:9 * 4096], A(BR0, 8192, BF16))

    if REORDER:
        S.reorder()
    S.emit()
    es.close()
    return nc


def _fm(v, k):
    return np.ascontiguousarray(np.asarray(v, np.float32).reshape(k, 128).T)


_NC_CACHE = {}


def kernel(x, c, ctx, c_ctx, norm_g, w_mod, b_mod, w_in, lb_logits, hgrn_norm_g,
           conv_w, conv_b, conv_ln_g, conv_ln_b, w_out, final_norm_g):
    f32 = np.float32
    x = np.asarray(x, f32)
    ctx = np.asarray(ctx, f32)
    c = np.asarray(c, f32)
    c_ctx = np.asarray(c_ctx, f32)
    w_mod0 = np.ascontiguousarray(np.asarray(w_mod, f32)[0])
    w_in0 = np.ascontiguousarray(np.asarray(w_in, f32)[0])
    w_out0 = np.ascontiguousarray(np.asarray(w_out, f32)[0])
    b_mod2 = np.ascontiguousarray(np.tile(np.asarray(b_mod, f32)[0][None, :], (2, 1)))
    fg_rep = np.ascontiguousarray(np.tile(np.asarray(final_norm_g, f32)[None, :], (128, 1)))
    ident = np.eye(128, dtype=f32)
    s_idx = np.arange(128)[:, None]
    t_idx = np.arange(128)[None, :]
    masks = np.concatenate([(s_idx <= t_idx), (s_idx >= t_idx)], axis=1).astype(np.float32)
    sel = np.zeros((2, 256), f32)
    sel[0, 0:128] = 1.0
    sel[1, 128:256] = 1.0
    rmask = np.ones((128, 512), f32)
    rmask[:, 0::128] = 0.0

    lbl = np.asarray(lb_logits, f32)
    lbl_fm = lbl.reshape(2, 2, 4, 128).transpose(3, 0, 1, 2).reshape(128, 16)
    cwv = np.asarray(conv_w, f32)[0]
    cw_fm = cwv.reshape(NTAP, 4, 128).transpose(2, 1, 0).reshape(128, 4 * NTAP)

    in_maps = []
    for b in range(N_CORES):
        vecs = np.zeros((128, 192), f32)
        cvv = np.stack([_fm(c[b], 8), _fm(c_ctx, 8)], axis=-1)
        vecs[:, 0:16] = cvv.reshape(128, 16)
        vecs[:, 16:24] = _fm(np.asarray(norm_g, f32)[0], 8)
        vecs[:, 24:40] = lbl_fm
        vecs[:, 40:44] = _fm(np.asarray(hgrn_norm_g, f32)[0], 4)
        vecs[:, 44:168] = cw_fm
        vecs[:, 168:172] = _fm(np.asarray(conv_b, f32)[0], 4)
        vecs[:, 172:176] = _fm(np.asarray(conv_ln_g, f32)[0], 4)
        vecs[:, 176:180] = _fm(np.asarray(conv_ln_b, f32)[0], 4)
        in_maps.append({
            "x": np.ascontiguousarray(x[b]),
            "ctx": np.ascontiguousarray(ctx[b]),
            "w_mod": w_mod0, "b_mod2": b_mod2, "w_in": w_in0, "w_out": w_out0,
            "vecs": vecs, "ident": ident, "masks": masks, "sel": sel,
            "fg_rep": fg_rep, "rmask": rmask,
        })
    if "nc" not in _NC_CACHE:
        _NC_CACHE["nc"] = build_program()
    nc = _NC_CACHE["nc"]
    res = run_bass_kernel_spmd(nc, in_maps, core_ids=list(range(N_CORES)))
    out = np.stack([np.asarray(r["out"], f32) for r in res.results], axis=0)
    if DEBUG:
        kernel.dbg = [np.asarray(r["dbg"]) for r in res.results]
    return out
```

```python
import numpy as np
from contextlib import ExitStack

import concourse.bass as bass
import concourse.mybir as mybir
from concourse.bass_utils import run_bass_kernel_spmd

F32 = mybir.dt.float32
BF16 = mybir.dt.bfloat16
U32 = mybir.dt.uint32
AF = mybir.ActivationFunctionType
ALU = mybir.AluOpType
AX = mybir.AxisListType

N_CORES = 8
SEQ = 4096
DM = 1024
CTX = 256
D_IN = 4096
EPS = 1e-6
NTAP = 31
ARENA_BYTES = 207 * 1024

DEBUG = False
MAXPHASE = 99
SUB = 99
VAR = 1
LV = 99
LQ = 31
P5 = 99
S3 = 99
REORDER = True
REORDER_PE = True
SLACK = 0.25


def _isz(dt):
    return mybir.dt.size(dt)


class _Reg:
    __slots__ = ("name", "p0", "p1", "ivs", "lo", "hi")

    def __init__(self, name, p0, p1, ivs):
        self.name, self.p0, self.p1, self.ivs = name, p0, p1, ivs
        self.lo = ivs[0][0]
        self.hi = ivs[-1][1]


def _region(ap):
    t = ap.tensor
    if type(t).__name__.startswith("DRam"):
        return None
    isz = _isz(ap.dtype)
    dims = [tuple(d) for d in ap.ap]
    row = int(t.shape[-1]) * _isz(t.dtype) if len(t.shape) == 2 else None
    if row is None:
        n = 1
        for s in t.shape[1:]:
            n *= int(s)
        row = n * _isz(t.dtype)
    off = int(ap.offset) * isz
    p0 = off // row
    fo = off % row
    pn = dims[0][1]
    free = [(s * isz, c) for (s, c) in dims[1:] if c > 1]
    span = isz
    for s, c in free:
        span += abs(s) * (c - 1)
    ivs = [(fo, fo + span)]
    if len(free) == 2:
        (s0, c0), (s1, c1) = free
        if s1 == isz and c0 <= 64 and c1 * isz < s0:
            ivs = [(fo + i * s0, fo + i * s0 + c1 * isz) for i in range(c0)]
    elif len(free) == 3:
        (s0, c0), (s1, c1), (s2, c2) = free
        if s2 == isz and c0 * c1 <= 64 and c2 * isz < s1 and s1 * c1 <= s0:
            ivs = [(fo + i * s0 + j * s1, fo + i * s0 + j * s1 + c2 * isz)
                   for i in range(c0) for j in range(c1)]
    return _Reg(t.name, p0, p0 + pn, ivs)


def _overlap(a, b):
    if a.p1 <= b.p0 or b.p1 <= a.p0 or a.hi <= b.lo or b.hi <= a.lo:
        return False
    if len(a.ivs) == 1 and len(b.ivs) == 1:
        return True
    for (l0, h0) in a.ivs:
        for (l1, h1) in b.ivs:
            if l0 < h1 and l1 < h0:
                return True
    return False


def _covers(w, r):
    if w.p0 > r.p0 or w.p1 < r.p1:
        return False
    for (l, h) in r.ivs:
        ok = False
        for (wl, wh) in w.ivs:
            if wl <= l and h <= wh:
                ok = True
                break
        if not ok:
            return False
    return True


class _Op:
    __slots__ = ("eng", "fn", "deps", "raw", "signal", "count", "is_dma", "semi", "target", "idx", "dur", "sdeps")


class Sched:
    NDMA = 8

    def __init__(self, nc, es):
        self.nc = nc
        self.h = {"pe": nc.tensor, "act": nc.scalar, "dve": nc.vector, "pool": nc.gpsimd, "sp": nc.sync}
        self.sem = {e: es.enter_context(nc.semaphore("s_" + e)) for e in ("pe", "act", "dve", "pool")}
        self.dsem = [es.enter_context(nc.semaphore("s_dma%d" % i)) for i in range(self.NDMA)]
        self.ops = []
        self.rec = {}
        self.ndma = 0
        self.last_dma = None
        self.last_on = {}

    def op(self, eng, fn, reads=(), writes=(), dma=False, dur=0.3):
        o = _Op()
        o.dur = dur
        o.eng, o.fn, o.is_dma = eng, fn, dma
        o.signal, o.count, o.idx = False, 0, len(self.ops)
        deps = {}

        def add(d, raw):
            k = d.idx
            if k in deps:
                deps[k] = (d, deps[k][1] or raw)
            else:
                deps[k] = (d, raw)

        rregs = [r for r in (_region(a) for a in reads if a is not None) if r is not None]
        wregs = [r for r in (_region(a) for a in writes if a is not None) if r is not None]
        def _bankify(w):
            if w.name != "ps":
                return w
            return _Reg(w.name, 0, 128, [((w.lo >> 11) << 11, (((w.hi - 1) >> 11) << 11) + 2048)])
        wregs = [_bankify(w) for w in wregs]
        rregs = [_bankify(r) for r in rregs]
        def keys(r):
            return [(r.name, b) for b in range(r.lo >> 13, ((r.hi - 1) >> 13) + 1)]

        for r in rregs:
            for key in keys(r):
                for (reg, d, isw) in self.rec.get(key, ()):
                    if (isw or (r.name == "ps" and d.eng != eng)) and _overlap(reg, r):
                        add(d, True)
        for w in wregs:
            for key in keys(w):
                for (reg, d, isw) in self.rec.get(key, ()):
                    if _overlap(reg, w):
                        add(d, False)
        o.deps = []
        o.sdeps = [d for (d, raw) in deps.values()]
        if eng in (("sp",) if REORDER_PE else ("pe", "sp")) and self.last_on.get(eng) is not None:
            o.sdeps.append(self.last_on[eng])
        self.last_on[eng] = o
        for (d, raw) in deps.values():
            if d.is_dma:
                o.deps.append(d)
                continue
            if d.eng == eng and eng == "pe":
                continue
            o.deps.append(d)
            d.signal = True
        for w in wregs:
            for key in keys(w):
                lst = self.rec.setdefault(key, [])
                lst[:] = [x for x in lst if not _covers(w, x[0])]
                lst.append((w, o, True))
        for r in rregs:
            for key in keys(r):
                lst = self.rec.setdefault(key, [])
                if eng == "pe" and not REORDER_PE:
                    lst[:] = [x for x in lst if not ((not x[2]) and x[1].eng == eng and _covers(r, x[0]))]
                lst.append((r, o, False))
        if dma:
            o.semi = self.ndma % self.NDMA
            o.target = 16 * (self.ndma // self.NDMA + 1)
            self.ndma += 1
            self.last_dma = o
        self.ops.append(o)
        return o

    def reorder(self, window=600):
        ops = self.ops
        n = len(ops)
        succ = [[] for _ in range(n)]
        indeg = [0] * n
        for o in ops:
            seen = set()
            for d in o.sdeps:
                if d.idx in seen:
                    continue
                seen.add(d.idx)
                succ[d.idx].append(o.idx)
                indeg[o.idx] += 1
        bl = [0.0] * n
        for i in range(n - 1, -1, -1):
            m = 0.0
            for j in succ[i]:
                if bl[j] > m:
                    m = bl[j]
            bl[i] = ops[i].dur + m
        fin = [0.0] * n
        efree = {e: 0.0 for e in self.h}
        ready = [i for i in range(n) if indeg[i] == 0]
        rstart = {i: 0.0 for i in ready}
        done = [False] * n
        lo = 0
        order = []
        while ready:
            while lo < n and done[lo]:
                lo += 1
            best, bkey = None, None
            cands = []
            for i in ready:
                if i > lo + window:
                    continue
                st = max(efree[ops[i].eng], rstart[i])
                cands.append((st, i))
                key = (st, i)
                if bkey is None or key < bkey:
                    best, bkey = i, key
            if best is not None and SLACK > 0:
                lim = bkey[0] + SLACK
                bb = None
                for (st, i) in cands:
                    if st <= lim:
                        k2 = (-bl[i], st, i)
                        if bb is None or k2 < bb:
                            bb = k2
                best = bb[2]
                bkey = (bb[1], best)
            if best is None:
                best = min(ready)
                bkey = (max(efree[ops[best].eng], rstart[best]), best)
            o = ops[best]
            ready.remove(best)
            done[best] = True
            order.append(o)
            f = bkey[0] + o.dur
            if o.is_dma:
                efree[o.eng] = bkey[0] + 0.1
            else:
                efree[o.eng] = f
            fin[best] = f
            for j in succ[best]:
                indeg[j] -= 1
                lat = 0.0 if ops[j].eng == o.eng else 0.2
                rstart[j] = max(rstart.get(j, 0.0), f + lat)
                if indeg[j] == 0:
                    ready.append(j)
        assert len(order) == n, (len(order), n)
        self.ops = order
        self.est_total = max(fin) if fin else 0.0

    def emit(self):
        cnt = {e: 0 for e in self.sem}
        for o in self.ops:
            if (not o.is_dma) and o.signal:
                cnt[o.eng] += 1
                o.count = cnt[o.eng]
        waited = {}
        last_dma_on = {}
        for o in self.ops:
            eng = self.h[o.eng]
            waits = {}
            for d in o.deps:
                if d.is_dma:
                    key = ("d", d.semi)
                    waits[key] = max(waits.get(key, 0), d.target)
                else:
                    key = ("e", d.eng)
                    waits[key] = max(waits.get(key, 0), d.count)
            if o.is_dma and o.target > 16:
                key = ("d", o.semi)
                waits[key] = max(waits.get(key, 0), o.target - 16)
            for key, val in waits.items():
                wk = (o.eng, key)
                if waited.get(wk, 0) >= val:
                    continue
                waited[wk] = val
                s = self.dsem[key[1]] if key[0] == "d" else self.sem[key[1]]
                eng.wait_ge(s, val)
            ins = o.fn(eng)
            if o.is_dma:
                ins.then_inc(self.dsem[o.semi], 16)
                last_dma_on[o.semi] = o.target
            elif o.signal:
                ins.then_inc(self.sem[o.eng], 1)
        for semi, tgt in last_dma_on.items():
            self.h["sp"].wait_ge(self.dsem[semi], tgt)
        for e, c in cnt.items():
            if c > 0:
                self.h["sp"].wait_ge(self.sem[e], c)


def build_program():
    nc = bass.Bass("TRN2", target_bir_lowering=False)
    dr = {}

    def din(name, shape, dt=F32):
        dr[name] = nc.dram_tensor(name, list(shape), dt, kind="ExternalInput").ap()
        return dr[name]

    x_d = din("x", [SEQ, DM])
    ctx_d = din("ctx", [CTX, DM])
    wmod_d = din("w_mod", [DM, 3 * DM])
    bmod_d = din("b_mod2", [2, 3 * DM])
    win_d = din("w_in", [DM, D_IN])
    wout_d = din("w_out", [DM, DM])
    vecs_d = din("vecs", [128, 192])
    ident_d = din("ident", [128, 128])
    masks_d = din("masks", [128, 256], F32)
    sel_d = din("sel", [2, 256])
    fg_d = din("fg_rep", [128, DM])
    rmask_d = din("rmask", [128, 512])
    out_d = nc.dram_tensor("out", [SEQ, DM], F32, kind="ExternalOutput").ap()
    if DEBUG:
        dbg_d = nc.dram_tensor("dbg", [128, 16 * 4096], BF16, kind="ExternalOutput").ap()

    es = ExitStack()
    arena = es.enter_context(nc.sbuf_tensor("arena", [128, ARENA_BYTES // 4], F32))
    psum = es.enter_context(nc.psum_tensor("ps", [128, 4096], F32))
    S = Sched(nc, es)

    def A(off, nbytes, dt, parts=(0, 128)):
        assert off % 4 == 0 and nbytes % 4 == 0 and off + nbytes <= ARENA_BYTES, (off, nbytes)
        v = arena[parts[0]:parts[1], off // 4:(off + nbytes) // 4]
        if dt != F32:
            v = v.bitcast(dt)
        return v

    def bank(i):
        return psum[:, i * 512:(i + 1) * 512]

    def _n(ap):
        n = 1
        for d in list(ap.shape)[1:]:
            n *= int(d)
        return n

    def dma(out, in_):
        nb = _n(out) * _isz(out.dtype) * int(out.shape[0])
        return S.op("sp", lambda e: e.dma_start(out=out, in_=in_), reads=[in_], writes=[out], dma=True,
                    dur=2.0 + nb / 150e3)

    def mm(out, lhsT, rhs, start=True, stop=True):
        return S.op("pe", lambda e: e.matmul(out, lhsT, rhs, start=start, stop=stop),
                    reads=[lhsT, rhs], writes=[out],
                    dur=(0.06 + max(_n(rhs), 64) * 0.0005) * (4 if rhs.dtype == F32 else 1))

    def tr(out, in_, ident):
        return S.op("pe", lambda e: e.transpose(out, in_, ident), reads=[in_, ident], writes=[out], dur=0.1)

    def act(out, in_, func, bias=None, scale=None, accum_out=None):
        kw = {}
        if (bias is not None and not isinstance(bias, (int, float))
                and (scale is None or (isinstance(scale, (int, float)) and float(scale) == 1.0))
                and func != AF.Ln):
            scale = onecol
        if bias is not None:
            kw["bias"] = bias
        if scale is not None:
            kw["scale"] = scale
        if accum_out is not None:
            kw["accum_out"] = accum_out
        rd = [in_] + [a for a in (bias, scale) if a is not None and not isinstance(a, (int, float))]
        return S.op("act", lambda e: e.activation(out, in_, func, **kw), reads=rd,
                    writes=[out] + ([accum_out] if accum_out is not None else []), dur=0.22 + _n(in_) / 1400.0)

    def ts(eng, out, in0, s1, s2, op0, op1=None):
        rd = [in0] + [a for a in (s1, s2) if a is not None and not isinstance(a, (int, float))]
        du = (0.07 + _n(in0) / 960.0) if eng == "dve" else (0.3 + _n(in0) / 450.0)
        if op1 is None:
            return S.op(eng, lambda e: e.tensor_scalar(out, in0, s1, None, op0), reads=rd, writes=[out], dur=du)
        return S.op(eng, lambda e: e.tensor_scalar(out, in0, s1, s2, op0, op1), reads=rd, writes=[out], dur=du)

    def tt(eng, out, in0, in1, op):
        du = (0.07 + _n(in0) / 960.0) if eng == "dve" else (0.3 + _n(in0) / 450.0)
        return S.op(eng, lambda e: e.tensor_tensor(out, in0, in1, op), reads=[in0, in1], writes=[out], dur=du)

    def stt(out, in0, scalar, in1, op0, op1):
        rd = [in0, in1] + ([scalar] if not isinstance(scalar, (int, float)) else [])
        return S.op("dve", lambda e: e.scalar_tensor_tensor(out, in0, scalar, in1, op0, op1),
                    reads=rd, writes=[out], dur=0.07 + _n(in0) / 960.0)

    def cp(eng, out, in_):
        du = (0.07 + _n(in_) / 960.0) if eng == "dve" else (0.3 + _n(in_) / 450.0)
        return S.op(eng, lambda e: e.tensor_copy(out, in_), reads=[in_], writes=[out], dur=du)

    def memset(eng, ap, val):
        du = (0.07 + _n(ap) / 960.0) if eng == "dve" else (0.3 + _n(ap) / 450.0)
        return S.op(eng, lambda e: e.memset(ap, val), reads=[], writes=[ap], dur=du)

    AT0 = 0
    BR0 = 65536
    P0 = 131072
    aT = A(AT0, 65536, BF16).rearrange("p (k t) -> p k t", k=8)
    br = A(BR0, 65536, BF16).rearrange("p (k t) -> p k t", k=8)
    o = P0
    ident = A(o, 512, F32); o += 512
    ident_bf = A(o, 256, BF16); o += 256
    ones_bf = A(o, 256, BF16); o += 256
    masks = A(o, 1024, F32); o += 1024
    maskF, maskB = masks[:, 0:128], masks[:, 128:256]
    vecs = A(o, 768, F32); o += 768
    dv = A(o, 1024, F32); o += 1024
    gt_rep = A(o, 4096, F32); o += 4096
    aTc = A(o, 4096, BF16).rearrange("p (k t) -> p k t", k=8); o += 4096
    rmask = A(o, 2048, F32); o += 2048
    small = A(o, 1024, F32); o += 1024
    onecol = small[:, 255:256]
    PH0 = o
    PH_BYTES = ARENA_BYTES - PH0

    cv = vecs[:, 0:16].rearrange("p (k c) -> p k c", c=2)
    ng = vecs[:, 16:24]
    lbl = vecs[:, 24:40]
    cw = vecs[:, 44:168].rearrange("p (s j) -> p s j", j=NTAP)
    convb = vecs[:, 168:172]
    lng = vecs[:, 172:176]
    lnb = vecs[:, 176:180]
    gs_l, sh_l, gs_c, sh_c = dv[:, 0:8], dv[:, 8:16], dv[:, 16:24], dv[:, 24:32]
    lbv, l1mlb = dv[:, 32:40], dv[:, 40:48]
    cwh = dv[:, 48:172].rearrange("p (s j) -> p s j", j=NTAP)
    tmp8 = dv[:, 172:236]

    dma(vecs, vecs_d)
    dma(ident, ident_d)
    dma(masks, masks_d)
    dma(rmask, rmask_d)
    cp("dve", ident_bf, ident)
    memset("pool", ones_bf, 1.0)
    memset("pool", onecol, 1.0)

    o = PH0
    mod_sb = A(o, 12288, F32, parts=(0, 2)); o += 12288
    bmod = A(o, 12288, F32, parts=(0, 2)); o += 12288
    wm = []
    for i in range(2):
        wm.append(A(o, 8192, F32).rearrange("p (k n) -> p k n", k=8)); o += 8192
    xs = []
    for i in range(3):
        xs.append(A(o, 4096, F32)); o += 4096
    junk = A(o, 2048, BF16); o += 2048
    sel = A(o, 1024, F32, parts=(0, 2)); o += 1024
    assert o <= ARENA_BYTES, o
    dma(sel, sel_d)

    dma(bmod, bmod_d)
    act(vecs[:, 0:16], vecs[:, 0:16], AF.Silu)
    wmod_v = wmod_d.rearrange("(k p) n -> p k n", p=128)
    for blk in range(12):
        w = wm[blk % 2]
        dma(w, wmod_v[:, :, blk * 256:(blk + 1) * 256])
        pb = bank(blk % 2)[0:2, 0:256]
        for k in range(8):
            mm(pb, cv[:, k, :], w[:, k, :], start=(k == 0), stop=(k == 7))
        tt("dve", mod_sb[:, blk * 256:(blk + 1) * 256], pb, bmod[:, blk * 256:(blk + 1) * 256], ALU.add)

    def extract(dst8, cb0, selv):
        for half in range(2):
            pb = bank(2 + half)
            mm(pb, selv, mod_sb[:, (cb0 + half) * 512:(cb0 + half + 1) * 512])
            for kk in range(4):
                k = half * 4 + kk
                scr = junk.bitcast(F32)[:, 0:128]
                tt("dve", scr, pb[:, kk * 128:(kk + 1) * 128], ident, ALU.mult)
                S.op("dve", lambda e, scr=scr, d=dst8[:, k:k + 1]: e.tensor_reduce(d, scr, AX.X, ALU.add),
                     reads=[scr], writes=[dst8[:, k:k + 1]])

    sel_l, sel_c = sel[:, 0:128], sel[:, 128:256]
    extract(sh_l, 0, sel_l)
    extract(gs_l, 2, sel_l)
    extract(sh_c, 0, sel_c)
    extract(gs_c, 2, sel_c)
    for half in range(2):
        pb = bank(2 + half)
        mm(pb, sel_l, mod_sb[:, (4 + half) * 512:(5 + half) * 512])
        act(gt_rep[:, half * 512:(half + 1) * 512], pb, AF.Copy)
    for g in (gs_l, gs_c):
        stt(g, g, 1.0, ng, ALU.add, ALU.mult)
    dlt = small[:, 0:8]
    tt("dve", dlt, lbl[:, 0:8], lbl[:, 8:16], ALU.subtract)
    act(dlt, dlt, AF.Exp, scale=-1.0)
    ts("dve", dlt, dlt, 1.0, None, ALU.add)
    S.op("dve", lambda e: e.reciprocal(lbv, dlt), reads=[dlt], writes=[lbv])
    act(l1mlb, lbv, AF.Ln, scale=-1.0, bias=1.0)
    ts("dve", dv[:, 48:172], vecs[:, 44:168], 0.5, None, ALU.mult)

    if MAXPHASE >= 1:
        ssv = small[:, 16:80]
        tile_ctr = [0]

        def norm_tile(src_rows, dst_fn, gs, sh):
            i = tile_ctr[0]
            tile_ctr[0] += 1
            xt = xs[i % 3]
            ss = ssv[:, (i % 16) * 2:(i % 16) * 2 + 1]
            rs = ssv[:, (i % 16) * 2 + 1:(i % 16) * 2 + 2]
            dma(xt, src_rows)
            act(junk, xt, AF.Square, accum_out=ss)
            act(rs, ss, AF.Ln, scale=1.0 / DM, bias=EPS)
            act(rs, rs, AF.Exp, scale=-0.5)
            act(xt, xt, AF.Copy, scale=rs)
            for k in range(8):
                pb = bank(4 + (i % 2) * 2 + k // 4)
                tr(pb[:, (k % 4) * 128:(k % 4 + 1) * 128], xt[:, k * 128:(k + 1) * 128], ident)
            for k in range(8):
                pb = bank(4 + (i % 2) * 2 + k // 4)
                ts("dve", dst_fn(k), pb[:, (k % 4) * 128:(k % 4 + 1) * 128], gs[:, k:k + 1], sh[:, k:k + 1],
                   ALU.mult, ALU.add)

        for t in range(2):
            norm_tile(ctx_d[t * 128:(t + 1) * 128, :], lambda k, t=t: aTc[:, k, t * 128:(t + 1) * 128], gs_c, sh_c)
        for t in range(32):
            norm_tile(x_d[t * 128:(t + 1) * 128, :], lambda k, t=t: aT[:, k, t * 128:(t + 1) * 128], gs_l, sh_l)

    win_v = win_d.rearrange("(k p) n -> p k n", p=128)

    def load_w(dst_bf, stage, col0, ncols=128):
        dma(stage, win_v[:, :, col0:col0 + ncols])
        act(dst_bf, stage, AF.Copy)

    if MAXPHASE >= 2:
        o = PH0
        wst = []
        for i in range(2):
            wst.append(A(o, 4096, F32).rearrange("p (k n) -> p k n", k=8)); o += 4096
        wgb = []
        for i in range(4):
            wgb.append(A(o, 2048, BF16).rearrange("p (k n) -> p k n", k=8)); o += 2048
        cB0 = o
        wu = []
        for i in range(4):
            wu.append(A(o, 2048, BF16).rearrange("p (k n) -> p k n", k=8)); o += 2048
        gin = A(o, 12288, BF16); o += 12288
        diag = []
        for i in range(2):
            diag.append(A(o, NTAP * 256, BF16).rearrange("p (j n) -> p j n", j=NTAP)); o += NTAP * 256
        th = []
        for i in range(2):
            th.append(A(o, 2048, F32)); o += 2048
        assert o <= ARENA_BYTES, o

        for s in (range(4) if SUB > 10 else ([2] if SUB == 6 else [0])):
            rowmode = s < 2
            wU, wG = wu[(s % 2) * 2], wu[(s % 2) * 2 + 1]
            load_w(wU, wst[0], 2560 + s * 128)
            load_w(wG, wst[1], 3072 + s * 128)
            dg = diag[s % 2]
            for j in range(NTAP if SUB >= 2 else 0):
                ts("dve", dg[:, j, :], ident_bf, cwh[:, s, j:j + 1], None, ALU.mult)
            if (s == 0 or s == 2) and SUB >= 3:
                memset("pool", gin, 0.0)
            for blk in range(8 if SUB >= 4 else 0):
                pu, pg = bank(0 + (blk % 2) * 2), bank(1 + (blk % 2) * 2)
                for k in range(8):
                    mm(pu, wU[:, k, :], aT[:, k, blk * 512:(blk + 1) * 512], start=(k == 0), stop=(k == 7))
                for k in range(8):
                    mm(pg, wG[:, k, :], aT[:, k, blk * 512:(blk + 1) * 512], start=(k == 0), stop=(k == 7))
                tb = th[blk % 2]
                act(tb, pg, AF.Tanh, scale=0.5)
                if rowmode:
                    r0 = blk * 8
                    dst = gin[:, r0 * 80 + 15:r0 * 80 + 15 + 640].rearrange("p (r c) -> p r c", c=80)[:, :, 0:64]
                else:
                    dst = gin[:, (blk * 8 + 15) * 64:(blk * 8 + 15) * 64 + 512].rearrange("p (r c) -> p r c", c=64)
                stt(dst, tb.rearrange("p (r c) -> p r c", c=64), 1.0, pu.rearrange("p (r c) -> p r c", c=64),
                    ALU.add, ALU.mult)
            for blk in range(8 if SUB >= 5 else 0):
                py = bank(4 + blk % 2)
                for j in range(NTAP):
                    if rowmode:
                        r0 = blk * 8
                        src = gin[:, r0 * 80 + j:r0 * 80 + j + 640].rearrange("p (r c) -> p r c", c=80)[:, :, 0:64]
                        dstp = py.rearrange("p (r c) -> p r c", c=64)
                    else:
                        src = gin[:, (blk * 8 + j) * 64:(blk * 8 + j) * 64 + 512]
                        dstp = py
                    mm(dstp, dg[:, j, :], src, start=(j == 0), stop=(j == NTAP - 1))
                if VAR == 0:
                    act(br[:, 4 + s, blk * 512:(blk + 1) * 512], py, AF.Identity, bias=convb[:, s:s + 1])
                elif VAR == 1:
                    ts("dve", br[:, 4 + s, blk * 512:(blk + 1) * 512], py, convb[:, s:s + 1], None, ALU.add)
                elif VAR == 2:
                    act(br[:, 4 + s, blk * 512:(blk + 1) * 512], py, AF.Copy)

    if MAXPHASE >= 3:
        o = cB0
        sqb = []
        for i in range(4):
            sqb.append(A(o, 1024, BF16)); o += 1024
        mean = A(o, 2048, F32); o += 2048
        msq = A(o, 2048, F32); o += 2048
        rstdc = A(o, 2048, F32); o += 2048
        mhalf = A(o, 2048, F32); o += 2048
        tbuf = []
        for i in range(2):
            tbuf.append(A(o, 2048, F32)); o += 2048
        sgb = []
        for i in range(2):
            sgb.append(A(o, 2048, F32)); o += 2048
        assert o <= ARENA_BYTES, o
        memset("pool", mhalf, -0.5)
        for s in range(4):
            load_w(wgb[s], wst[s % 2], 3584 + s * 128)
        for blk in range(8):
            sl = slice(blk * 512, (blk + 1) * 512)
            p1, p2 = bank(0 + (blk % 2) * 4), bank(1 + (blk % 2) * 4)
            for s in range(4):
                act(sqb[s], br[:, 4 + s, sl], AF.Square)
            for s in range(4):
                mm(p1, ones_bf, br[:, 4 + s, sl], start=(s == 0), stop=(s == 3))
            for s in range(4):
                mm(p2, ones_bf, sqb[s], start=(s == 0), stop=(s == 3))
            act(mean, p1, AF.Copy, scale=1.0 / 512)
            tt("dve", msq, mean, mean, ALU.mult)
            ts("dve", rstdc, p2, 1.0 / 512, EPS, ALU.mult, ALU.add)
            tt("dve", rstdc, rstdc, msq, ALU.subtract)
            act(rstdc, rstdc, AF.Ln)
            act(rstdc, rstdc, AF.Exp, scale=-0.5)
            for s in range(4):
                tb = tbuf[s % 2]
                tt("dve", tb, br[:, 4 + s, sl], mean, ALU.subtract)
                tt("dve", tb, tb, rstdc, ALU.mult)
                act(tb, tb, AF.Silu, scale=lng[:, s:s + 1], bias=lnb[:, s:s + 1])
                pgb = bank(2 + s % 2)
                for k in range(8):
                    mm(pgb, wgb[s][:, k, :], aT[:, k, sl], start=(k == 0), stop=(k == 7))
                sg = sgb[s % 2]
                act(sg, pgb, AF.Silu)
                tt("dve", br[:, 4 + s, sl], tb, sg, ALU.mult)

    if MAXPHASE >= 4:
        o = PH0
        wst3 = A(o, 4096, F32).rearrange("p (k n) -> p k n", k=8); o += 4096
        wAs, wBs = [], []
        for i in range(2):
            wAs.append(A(o, 2048, BF16).rearrange("p (k n) -> p k n", k=8)); o += 2048
            wBs.append(A(o, 2048, BF16).rearrange("p (k n) -> p k n", k=8)); o += 2048
        wC = A(o, 2048, BF16).rearrange("p (k n) -> p k n", k=8); o += 2048
        vh = A(o, 34 * 256, BF16).rearrange("p (c e) -> p c e", e=128); o += 34 * 256
        qbf = A(o, 8192, BF16); o += 8192
        e2b, l1b, l2b, bb, bxb = [], [], [], [], []
        for lst in (e2b, l1b, l2b, bb):
            for i in range(2):
                lst.append(A(o, 2048, F32)); o += 2048
        bxb.append(A(o, 2048, F32)); o += 2048
        bxb.append(bxb[0])
        qtb, ktb = [], []
        for lst in (qtb, ktb):
            for i in range(2):
                lst.append(A(o, 1024, BF16)); o += 1024
        kTall = [A(o, 1024, BF16)]; o += 1024
        kTall.append(kTall[0])
        osb1 = A(o, 2048, F32); o += 2048
        etmp = A(o, 2048, F32); o += 2048
        PFall = A(o, 1024, BF16); o += 1024
        PBall = A(o, 1024, BF16); o += 1024
        Zb = []
        for i in range(2):
            Zb.append(A(o, 256, BF16)); o += 256
        Yst = A(o, 512, F32); o += 512
        sc3 = A(o, 512, F32); o += 512
        sq3 = A(o, 1024, BF16); o += 1024
        Uall = A(o, 2048, F32); o += 2048
        assert o <= ARENA_BYTES, o

        memset("pool", PFall, 0.0)
        memset("pool", PBall, 0.0)

        blkctr = [0]
        chctr = [0]

        items = []
        sweep_no = [0]

        def make_vproj(h):
            def vproj():
                for c in range(34):
                    src = aTc[:, :, c * 128:(c + 1) * 128] if c < 2 else aT[:, :, (c - 2) * 128:(c - 1) * 128]
                    pv = bank(4 + c % 2)[:, 0:128]
                    for k in range(8):
                        mm(pv, src[:, k, :], wC[:, k, :], start=(k == 0), stop=(k == 7))
                    if c % 2 == 0:
                        act(vh[:, c, :], pv, AF.Copy)
                    else:
                        cp("dve", vh[:, c, :], pv)
                if h + 1 < NH:
                    load_w(wC, wst3, 1536 + (h + 1) * 128)
            return vproj

        NH = 4 if S3 > 10 else 1
        load_w(wC, wst3, 1536)
        for h in range(NH):
            head_init = make_vproj(h)

            for dirn in range(2):
                fwd = dirn == 0
                lb_ap = lbv[:, dirn * 4 + h:dirn * 4 + h + 1]
                l1m_ap = l1mlb[:, dirn * 4 + h:dirn * 4 + h + 1]
                sw = sweep_no[0]
                sweep_no[0] += 1
                wA, wB = wAs[sw % 2], wBs[sw % 2]

                def wload(wA=wA, wB=wB, fwd=fwd, h=h):
                    load_w(wA, wst3, (512 if fwd else 1024) + h * 128)
                    load_w(wB, wst3, (0 if fwd else 2048) + h * 128)

                def sweep_init():
                    memset("pool", Yst, 0.0)
                blocks = [("ctx", 0)] + [("lat", j) for j in (range(8) if fwd else range(7, -1, -1))]
                msk = maskF if fwd else maskB

                def stage1(kind, j, fwd=fwd, lb_ap=lb_ap, l1m_ap=l1m_ap, wA=wA, wB=wB):
                    bi = blkctr[0]
                    blkctr[0] += 1
                    par = bi % 2
                    lat = kind == "lat"
                    nt = 512 if lat else 256
                    nch = nt // 128
                    src = aT[:, :, j * 512:(j + 1) * 512] if lat else aTc
                    tsl = slice(j * 512, (j + 1) * 512)
                    pz = bank(0 + par * 2)[:, 0:nt]
                    pq = bank(1 + par * 2)[:, 0:nt]
                    for k in range(8):
                        mm(pz, wA[:, k, :], src[:, k, :], start=(k == 0), stop=(k == 7))
                    if lat and fwd:
                        for k in range(8):
                            mm(pq, wB[:, k, :], src[:, k, :], start=(k == 0), stop=(k == 7))
                    e2, l1, l2, bq, bx = (e2b[par][:, 0:nt], l1b[par][:, 0:nt], l2b[par][:, 0:nt],
                                          bb[par][:, 0:nt], bxb[par][:, 0:nt])
                    qt, kt = qtb[par][:, 0:nt], ktb[par][:, 0:nt]
                    act(e2, pz, AF.Exp)
                    act(l1, e2, AF.Ln, bias=lb_ap)
                    act(l2, e2, AF.Ln, bias=1.0)
                    tt("dve", l1, l1, l2, ALU.subtract)
                    S.op("dve", lambda e, bq=bq, l1=l1, nt=nt: e.tensor_tensor_scan(
                        bq, rmask[:, 0:nt], l1, 0.0, ALU.mult, ALU.add), reads=[rmask[:, 0:nt], l1], writes=[bq],
                        dur=0.1 + nt / 480.0)
                    bq3 = bq.rearrange("p (c t) -> p c t", t=128)
                    m4b = bq3[:, :, 63:64].to_broadcast([128, nch, 128])
                    bx3 = bx.rearrange("p (c t) -> p c t", t=128)
                    if fwd:
                        tt("dve", bx3, bq3, m4b, ALU.subtract)
                        tt("dve", l2, l2, bx, ALU.add)
                    else:
                        tt("dve", bx, bq, l1, ALU.subtract)
                        tt("dve", bx3, bx3, m4b, ALU.subtract)
                        tt("dve", l2, bx, l2, ALU.subtract)
                    base = 8 + par * 40
                    m4 = bq3[:, :, 63]
                    tot4 = bq3[:, :, 127]
                    d4 = sc3[:, base:base + nch]
                    cA4 = sc3[:, base + 12:base + 12 + nch]
                    cB4 = sc3[:, base + 16:base + 16 + nch]
                    tt("dve", d4, tot4, m4, ALU.subtract)
                    cAB4 = sc3[:, base + 20:base + 20 + nch]
                    if fwd:
                        act(cA4, m4, AF.Exp)
                        act(cB4, d4, AF.Exp)
                    else:
                        act(cA4, d4, AF.Exp)
                        act(cB4, m4, AF.Exp)
                    act(cAB4, tot4, AF.Exp)
                    if lat:
                        act(e2, bx, AF.Exp, scale=(1.0 if fwd else -1.0))
                    act(kt, l2, AF.Exp, scale=(-1.0 if fwd else 1.0), bias=l1m_ap)
                    if lat:
                        if fwd:
                            tt("dve", qt, pq, e2, ALU.mult)
                            act(qbf[:, tsl], pq, AF.Copy)
                        else:
                            tt("dve", qt, qbf[:, tsl], e2, ALU.mult)
                    return dict(par=par, lat=lat, nt=nt, nch=nch, src=src, tsl=tsl, pz=pz, pq=pq, qt=qt, kt=kt,
                                cA4=cA4, cB4=cB4, cAB4=cAB4, j=j)

                def stage1b(st):
                    par, nch, kt = st["par"], st["nch"], st["kt"]
                    pzb = bank(0 + par * 2).bitcast(BF16)
                    for c in range(nch):
                        tr(pzb[:, c * 128:(c + 1) * 128], kt[:, c * 128:(c + 1) * 128], ident_bf)
                    act(kTall[par][:, 0:nch * 128], pzb[:, 0:nch * 128], AF.Copy)

                def stage2(st, fwd=fwd, msk=msk, h=h, wB=wB):
                    par, lat, nch, tsl, qt, kt = st["par"], st["lat"], st["nch"], st["tsl"], st["qt"], st["kt"]
                    cA4, cB4, j, pq, src = st["cA4"], st["cB4"], st["j"], st["pq"], st["src"]
                    cAB4 = st["cAB4"]
                    po = bank(4 + par)
                    bD, bS = bank(6), bank(7)
                    Pall = PFall if fwd else PBall
                    kTa = kTall[par]
                    corder = list(range(nch)) if fwd else list(range(nch - 1, -1, -1))
                    for c in corder:
                        gch = (j * 4 + c + 2) if lat else c
                        cs = slice(c * 128, (c + 1) * 128)
                        mm(bD[:, cs], kTa[:, cs], vh[:, gch, :])
                    nn = nch * 128
                    tt("dve", Uall[:, 0:nn].rearrange("p (c t) -> p c t", t=128),
                       bD[:, 0:nn].rearrange("p (c t) -> p c t", t=128),
                       cB4.unsqueeze(2).to_broadcast([128, nch, 128]), ALU.mult)
                    if lat:
                        if fwd:
                            pieces = [(slice(0, 128), slice(64, 128)), (slice(0, 64), slice(0, 64))]
                        else:
                            pieces = [(slice(0, 128), slice(0, 64)), (slice(64, 128), slice(64, 128))]
                        for c in corder:
                            for (sp, tp) in pieces:
                                tq = slice(c * 128 + tp.start, c * 128 + tp.stop)
                                mm(bS[sp, tq], kt[:, c * 128 + sp.start:c * 128 + sp.stop], qt[:, tq])
                        for (sp, tp) in pieces:
                            w = tp.stop - tp.start
                            np_ = sp.stop - sp.start
                            P3 = Pall[sp, :].rearrange("p (c t) -> p c t", t=128)[:, :, tp]
                            S3v = bS[sp, :].rearrange("p (c t) -> p c t", t=128)[:, :, tp]
                            M3 = msk[sp, tp].unsqueeze(1).to_broadcast([np_, nch, w])
                            tt("dve", P3, S3v, M3, ALU.mult)
                    for c in corder:
                        ci = chctr[0]
                        chctr[0] += 1
                        cpar = ci % 2
                        cs = slice(c * 128, (c + 1) * 128)
                        gch = (j * 4 + c + 2) if lat else c
                        if lat:
                            ts("pool", Zb[cpar], Yst, cA4[:, c:c + 1], 1.0, ALU.mult, ALU.mult)
                            mm(po[:, cs], vh[:, gch, :], Pall[:, cs], start=True, stop=False)
                            mm(po[:, cs], Zb[cpar], qt[:, cs], start=False, stop=True)
                        stt(Yst, Yst, cAB4[:, c:c + 1], Uall[:, cs], ALU.mult, ALU.add)
                    if lat and fwd:
                        act(br[:, h, tsl], po, AF.Copy)
                    if lat and not fwd and S3 == 7:
                        act(br[:, h, tsl], po, AF.Copy)
                    elif lat and not fwd:
                        osb = osb1
                        tt("dve", osb, po, br[:, h, tsl], ALU.add)
                        act(sq3, osb, AF.Square)
                        pn = bank(0 + par * 2)
                        mm(pn, ones_bf, sq3)
                        for k in range(8):
                            mm(pq, wB[:, k, :], src[:, k, :], start=(k == 0), stop=(k == 7))
                        act(etmp, pn, AF.Ln, scale=1.0 / 128, bias=EPS)
                        act(etmp, etmp, AF.Exp, scale=-0.5)
                        tt("pool", osb, osb, etmp, ALU.mult)
                        act(etmp, pq, AF.Exp, scale=-1.0)
                        act(etmp, etmp, AF.Ln, bias=1.0)
                        act(etmp, etmp, AF.Exp, scale=-1.0)
                        tt("dve", etmp, pq, etmp, ALU.mult)
                        sg3 = etmp
                        tt("pool", br[:, h, tsl], osb, sg3, ALU.mult)

                for bi_, blk in enumerate(blocks):
                    items.append(dict(s1=stage1, s1b=stage1b, s2=stage2, blk=blk, first_sweep=(bi_ == 0),
                                      first_head=(bi_ == 0 and dirn == 0), wload=wload, sweep_init=sweep_init,
                                      head_init=head_init))

        sweeps_first = [it for it in items if it["first_sweep"]]
        sweeps_first[0]["wload"]()
        nxt_sweep = 1
        prev, prev_st = None, None
        for it in items:
            cur_st = it["s1"](*it["blk"])
            if prev is not None:
                if prev["first_head"]:
                    prev["head_init"]()
                if prev["first_sweep"]:
                    prev["sweep_init"]()
                prev["s2"](prev_st)
            if it["first_sweep"] and nxt_sweep < len(sweeps_first):
                sweeps_first[nxt_sweep]["wload"]()
                nxt_sweep += 1
            it["s1b"](cur_st)
            prev, prev_st = it, cur_st
        if prev["first_head"]:
            prev["head_init"]()
        if prev["first_sweep"]:
            prev["sweep_init"]()
        prev["s2"](prev_st)

    if MAXPHASE >= 5:
        o = PH0
        wo = A(o, 16384, BF16).rearrange("p (k n) -> p k n", k=8); o += 16384
        wos = A(o, 4096, F32); o += 4096
        xr = []
        for i in range(3):
            xr.append(A(o, 4096, F32)); o += 4096
        ht = []
        for i in range(2):
            ht.append(A(o, 4096, F32)); o += 4096
        junk4 = A(o, 2048, BF16); o += 2048
        fs = A(o, 512, F32); o += 512
        fg_rep = A(o, 4096, F32); o += 4096
        dma(fg_rep, fg_d)
        assert o <= ARENA_BYTES, o
        hgn = vecs[:, 40:44]
        wout_v = wout_d.rearrange("(k p) n -> p k n", p=128)
        for k in range(8):
            dma(wos, wout_v[:, k, :])
            if k < 4:
                stt(wo[:, k, :], wos, hgn[:, k:k + 1], gt_rep, ALU.mult, ALU.mult)
            else:
                tt("dve", wo[:, k, :], wos, gt_rep, ALU.mult)
        for t in range(2 if P5 >= 2 else 0):
            dma(xr[t % 3], x_d[t * 128:(t + 1) * 128, :])
        for t in range(32 if P5 >= 2 else 0):
            rows = slice(t * 128, (t + 1) * 128)
            xt = xr[t % 3]
            if t + 2 < 32:
                dma(xr[(t + 2) % 3], x_d[(t + 2) * 128:(t + 3) * 128, :])
            hb = ht[t % 2]
            for half in range(2):
                pb = bank((t % 2) * 2 + half)
                for g in range(8):
                    mm(pb, br[:, g, rows], wo[:, g, half * 512:(half + 1) * 512], start=(g == 0), stop=(g == 7))
                tt("dve", hb[:, half * 512:(half + 1) * 512], pb, xt[:, half * 512:(half + 1) * 512], ALU.add)
            ss = fs[:, (t % 16) * 2:(t % 16) * 2 + 1]
            rs = fs[:, (t % 16) * 2 + 1:(t % 16) * 2 + 2]
            if P5 >= 3:
                act(junk4, hb, AF.Square, accum_out=ss)
                act(rs, ss, AF.Ln, scale=1.0 / DM, bias=EPS)
                act(rs, rs, AF.Exp, scale=-0.5)
                stt(hb, hb, rs, fg_rep, ALU.mult, ALU.mult)
            if P5 >= 4:
                dma(out_d[rows, :], hb)

    if DEBUG:
        if MAXPHASE >= 1:
            dma(dbg_d[:, 0:8 * 4096], A(AT0, 65536, BF16))
        if MAXPHASE >= 2 and SUB > 10:
            dma(dbg_d[:, 12 * 4096:16 * 4096], A(BR0 + 32768, 32768, BF16))
        if MAXPHASE >= 4 and S3 > 10:
            dma(dbg_d[:, 8 * 4096:12 * 4096], A(BR0, 32768, BF16))
        if MAXPHASE >= 4 and S3 in (5, 7):
            dma(dbg_d[:, 8 * 4096:9 * 4096], A(BR0, 8192, BF16))

    if REORDER:
        S.reorder()
    S.emit()
    es.close()
    return nc


def _fm(v, k):
    return np.ascontiguousarray(np.asarray(v, np.float32).reshape(k, 128).T)


_NC_CACHE = {}


def kernel(x, c, ctx, c_ctx, norm_g, w_mod, b_mod, w_in, lb_logits, hgrn_norm_g,
           conv_w, conv_b, conv_ln_g, conv_ln_b, w_out, final_norm_g):
    f32 = np.float32
    x = np.asarray(x, f32)
    ctx = np.asarray(ctx, f32)
    c = np.asarray(c, f32)
    c_ctx = np.asarray(c_ctx, f32)
    w_mod0 = np.ascontiguousarray(np.asarray(w_mod, f32)[0])
    w_in0 = np.ascontiguousarray(np.asarray(w_in, f32)[0])
    w_out0 = np.ascontiguousarray(np.asarray(w_out, f32)[0])
    b_mod2 = np.ascontiguousarray(np.tile(np.asarray(b_mod, f32)[0][None, :], (2, 1)))
    fg_rep = np.ascontiguousarray(np.tile(np.asarray(final_norm_g, f32)[None, :], (128, 1)))
    ident = np.eye(128, dtype=f32)
    s_idx = np.arange(128)[:, None]
    t_idx = np.arange(128)[None, :]
    masks = np.concatenate([(s_idx <= t_idx), (s_idx >= t_idx)], axis=1).astype(np.float32)
    sel = np.zeros((2, 256), f32)
    sel[0, 0:128] = 1.0
    sel[1, 128:256] = 1.0
    rmask = np.ones((128, 512), f32)
    rmask[:, 0::128] = 0.0

    lbl = np.asarray(lb_logits, f32)
    lbl_fm = lbl.reshape(2, 2, 4, 128).transpose(3, 0, 1, 2).reshape(128, 16)
    cwv = np.asarray(conv_w, f32)[0]
    cw_fm = cwv.reshape(NTAP, 4, 128).transpose(2, 1, 0).reshape(128, 4 * NTAP)

    in_maps = []
    for b in range(N_CORES):
        vecs = np.zeros((128, 192), f32)
        cvv = np.stack([_fm(c[b], 8), _fm(c_ctx, 8)], axis=-1)
        vecs[:, 0:16] = cvv.reshape(128, 16)
        vecs[:, 16:24] = _fm(np.asarray(norm_g, f32)[0], 8)
        vecs[:, 24:40] = lbl_fm
        vecs[:, 40:44] = _fm(np.asarray(hgrn_norm_g, f32)[0], 4)
        vecs[:, 44:168] = cw_fm
        vecs[:, 168:172] = _fm(np.asarray(conv_b, f32)[0], 4)
        vecs[:, 172:176] = _fm(np.asarray(conv_ln_g, f32)[0], 4)
        vecs[:, 176:180] = _fm(np.asarray(conv_ln_b, f32)[0], 4)
        in_maps.append({
            "x": np.ascontiguousarray(x[b]),
            "ctx": np.ascontiguousarray(ctx[b]),
            "w_mod": w_mod0, "b_mod2": b_mod2, "w_in": w_in0, "w_out": w_out0,
            "vecs": vecs, "ident": ident, "masks": masks, "sel": sel,
            "fg_rep": fg_rep, "rmask": rmask,
        })
    if "nc" not in _NC_CACHE:
        _NC_CACHE["nc"] = build_program()
    nc = _NC_CACHE["nc"]
    res = run_bass_kernel_spmd(nc, in_maps, core_ids=list(range(N_CORES)))
    out = np.stack([np.asarray(r["out"], f32) for r in res.results], axis=0)
    if DEBUG:
        kernel.dbg = [np.asarray(r["dbg"]) for r in res.results]
    return out
```

```python
import numpy as np
from contextlib import ExitStack

import concourse.bass as bass
import concourse.mybir as mybir
from concourse.bass_utils import run_bass_kernel_spmd

F32 = mybir.dt.float32
BF16 = mybir.dt.bfloat16
U32 = mybir.dt.uint32
AF = mybir.ActivationFunctionType
ALU = mybir.AluOpType
AX = mybir.AxisListType

N_CORES = 8
SEQ = 4096
DM = 1024
CTX = 256
D_IN = 4096
EPS = 1e-6
NTAP = 31
ARENA_BYTES = 207 * 1024

DEBUG = False
MAXPHASE = 99
SUB = 99
VAR = 1
LV = 99
LQ = 31
P5 = 99
S3 = 99
REORDER = True
REORDER_PE = True
SLACK = 0.25


def _isz(dt):
    return mybir.dt.size(dt)


class _Reg:
    __slots__ = ("name", "p0", "p1", "ivs", "lo", "hi")

    def __init__(self, name, p0, p1, ivs):
        self.name, self.p0, self.p1, self.ivs = name, p0, p1, ivs
        self.lo = ivs[0][0]
        self.hi = ivs[-1][1]


def _region(ap):
    t = ap.tensor
    if type(t).__name__.startswith("DRam"):
        return None
    isz = _isz(ap.dtype)
    dims = [tuple(d) for d in ap.ap]
    row = int(t.shape[-1]) * _isz(t.dtype) if len(t.shape) == 2 else None
    if row is None:
        n = 1
        for s in t.shape[1:]:
            n *= int(s)
        row = n * _isz(t.dtype)
    off = int(ap.offset) * isz
    p0 = off // row
    fo = off % row
    pn = dims[0][1]
    free = [(s * isz, c) for (s, c) in dims[1:] if c > 1]
    span = isz
    for s, c in free:
        span += abs(s) * (c - 1)
    ivs = [(fo, fo + span)]
    if len(free) == 2:
        (s0, c0), (s1, c1) = free
        if s1 == isz and c0 <= 64 and c1 * isz < s0:
            ivs = [(fo + i * s0, fo + i * s0 + c1 * isz) for i in range(c0)]
    elif len(free) == 3:
        (s0, c0), (s1, c1), (s2, c2) = free
        if s2 == isz and c0 * c1 <= 64 and c2 * isz < s1 and s1 * c1 <= s0:
            ivs = [(fo + i * s0 + j * s1, fo + i * s0 + j * s1 + c2 * isz)
                   for i in range(c0) for j in range(c1)]
    return _Reg(t.name, p0, p0 + pn, ivs)


def _overlap(a, b):
    if a.p1 <= b.p0 or b.p1 <= a.p0 or a.hi <= b.lo or b.hi <= a.lo:
        return False
    if len(a.ivs) == 1 and len(b.ivs) == 1:
        return True
    for (l0, h0) in a.ivs:
        for (l1, h1) in b.ivs:
            if l0 < h1 and l1 < h0:
                return True
    return False


def _covers(w, r):
    if w.p0 > r.p0 or w.p1 < r.p1:
        return False
    for (l, h) in r.ivs:
        ok = False
        for (wl, wh) in w.ivs:
            if wl <= l and h <= wh:
                ok = True
                break
        if not ok:
            return False
    return True


class _Op:
    __slots__ = ("eng", "fn", "deps", "raw", "signal", "count", "is_dma", "semi", "target", "idx", "dur", "sdeps")


class Sched:
    NDMA = 8

    def __init__(self, nc, es):
        self.nc = nc
        self.h = {"pe": nc.tensor, "act": nc.scalar, "dve": nc.vector, "pool": nc.gpsimd, "sp": nc.sync}
        self.sem = {e: es.enter_context(nc.semaphore("s_" + e)) for e in ("pe", "act", "dve", "pool")}
        self.dsem = [es.enter_context(nc.semaphore("s_dma%d" % i)) for i in range(self.NDMA)]
        self.ops = []
        self.rec = {}
        self.ndma = 0
        self.last_dma = None
        self.last_on = {}

    def op(self, eng, fn, reads=(), writes=(), dma=False, dur=0.3):
        o = _Op()
        o.dur = dur
        o.eng, o.fn, o.is_dma = eng, fn, dma
        o.signal, o.count, o.idx = False, 0, len(self.ops)
        deps = {}

        def add(d, raw):
            k = d.idx
            if k in deps:
                deps[k] = (d, deps[k][1] or raw)
            else:
                deps[k] = (d, raw)

        rregs = [r for r in (_region(a) for a in reads if a is not None) if r is not None]
        wregs = [r for r in (_region(a) for a in writes if a is not None) if r is not None]
        def _bankify(w):
            if w.name != "ps":
                return w
            return _Reg(w.name, 0, 128, [((w.lo >> 11) << 11, (((w.hi - 1) >> 11) << 11) + 2048)])
        wregs = [_bankify(w) for w in wregs]
        rregs = [_bankify(r) for r in rregs]
        def keys(r):
            return [(r.name, b) for b in range(r.lo >> 13, ((r.hi - 1) >> 13) + 1)]

        for r in rregs:
            for key in keys(r):
                for (reg, d, isw) in self.rec.get(key, ()):
                    if (isw or (r.name == "ps" and d.eng != eng)) and _overlap(reg, r):
                        add(d, True)
        for w in wregs:
            for key in keys(w):
                for (reg, d, isw) in self.rec.get(key, ()):
                    if _overlap(reg, w):
                        add(d, False)
        o.deps = []
        o.sdeps = [d for (d, raw) in deps.values()]
        if eng in (("sp",) if REORDER_PE else ("pe", "sp")) and self.last_on.get(eng) is not None:
            o.sdeps.append(self.last_on[eng])
        self.last_on[eng] = o
        for (d, raw) in deps.values():
            if d.is_dma:
                o.deps.append(d)
                continue
            if d.eng == eng and eng == "pe":
                continue
            o.deps.append(d)
            d.signal = True
        for w in wregs:
            for key in keys(w):
                lst = self.rec.setdefault(key, [])
                lst[:] = [x for x in lst if not _covers(w, x[0])]
                lst.append((w, o, True))
        for r in rregs:
            for key in keys(r):
                lst = self.rec.setdefault(key, [])
                if eng == "pe" and not REORDER_PE:
                    lst[:] = [x for x in lst if not ((not x[2]) and x[1].eng == eng and _covers(r, x[0]))]
                lst.append((r, o, False))
        if dma:
            o.semi = self.ndma % self.NDMA
            o.target = 16 * (self.ndma // self.NDMA + 1)
            self.ndma += 1
            self.last_dma = o
        self.ops.append(o)
        return o

    def reorder(self, window=600):
        ops = self.ops
        n = len(ops)
        succ = [[] for _ in range(n)]
        indeg = [0] * n
        for o in ops:
            seen = set()
            for d in o.sdeps:
                if d.idx in seen:
                    continue
                seen.add(d.idx)
                succ[d.idx].append(o.idx)
                indeg[o.idx] += 1
        bl = [0.0] * n
        for i in range(n - 1, -1, -1):
            m = 0.0
            for j in succ[i]:
                if bl[j] > m:
                    m = bl[j]
            bl[i] = ops[i].dur + m
        fin = [0.0] * n
        efree = {e: 0.0 for e in self.h}
        ready = [i for i in range(n) if indeg[i] == 0]
        rstart = {i: 0.0 for i in ready}
        done = [False] * n
        lo = 0
        order = []
        while ready:
            while lo < n and done[lo]:
                lo += 1
            best, bkey = None, None
            cands = []
            for i in ready:
                if i > lo + window:
                    continue
                st = max(efree[ops[i].eng], rstart[i])
                cands.append((st, i))
                key = (st, i)
                if bkey is None or key < bkey:
                    best, bkey = i, key
            if best is not None and SLACK > 0:
                lim = bkey[0] + SLACK
                bb = None
                for (st, i) in cands:
                    if st <= lim:
                        k2 = (-bl[i], st, i)
                        if bb is None or k2 < bb:
                            bb = k2
                best = bb[2]
                bkey = (bb[1], best)
            if best is None:
                best = min(ready)
                bkey = (max(efree[ops[best].eng], rstart[best]), best)
            o = ops[best]
            ready.remove(best)
            done[best] = True
            order.append(o)
            f = bkey[0] + o.dur
            if o.is_dma:
                efree[o.eng] = bkey[0] + 0.1
            else:
                efree[o.eng] = f
            fin[best] = f
            for j in succ[best]:
                indeg[j] -= 1
                lat = 0.05 if ops[j].eng == o.eng else 0.4
                rstart[j] = max(rstart.get(j, 0.0), f + lat)
                if indeg[j] == 0:
                    ready.append(j)
        assert len(order) == n, (len(order), n)
        self.ops = order
        self.est_total = max(fin) if fin else 0.0

    def emit(self):
        cnt = {e: 0 for e in self.sem}
        for o in self.ops:
            if (not o.is_dma) and o.signal:
                cnt[o.eng] += 1
                o.count = cnt[o.eng]
        waited = {}
        last_dma_on = {}
        for o in self.ops:
            eng = self.h[o.eng]
            waits = {}
            for d in o.deps:
                if d.is_dma:
                    key = ("d", d.semi)
                    waits[key] = max(waits.get(key, 0), d.target)
                else:
                    key = ("e", d.eng)
                    waits[key] = max(waits.get(key, 0), d.count)
            if o.is_dma and o.target > 16:
                key = ("d", o.semi)
                waits[key] = max(waits.get(key, 0), o.target - 16)
            for key, val in waits.items():
                wk = (o.eng, key)
                if waited.get(wk, 0) >= val:
                    continue
                waited[wk] = val
                s = self.dsem[key[1]] if key[0] == "d" else self.sem[key[1]]
                eng.wait_ge(s, val)
            ins = o.fn(eng)
            if o.is_dma:
                ins.then_inc(self.dsem[o.semi], 16)
                last_dma_on[o.semi] = o.target
            elif o.signal:
                ins.then_inc(self.sem[o.eng], 1)
        for semi, tgt in last_dma_on.items():
            self.h["sp"].wait_ge(self.dsem[semi], tgt)
        for e, c in cnt.items():
            if c > 0:
                self.h["sp"].wait_ge(self.sem[e], c)


def build_program():
    nc = bass.Bass("TRN2", target_bir_lowering=False)
    dr = {}

    def din(name, shape, dt=F32):
        dr[name] = nc.dram_tensor(name, list(shape), dt, kind="ExternalInput").ap()
        return dr[name]

    x_d = din("x", [SEQ, DM])
    ctx_d = din("ctx", [CTX, DM])
    wmod_d = din("w_mod", [DM, 3 * DM])
    bmod_d = din("b_mod2", [2, 3 * DM])
    win_d = din("w_in", [DM, D_IN])
    wout_d = din("w_out", [DM, DM])
    vecs_d = din("vecs", [128, 192])
    ident_d = din("ident", [128, 128])
    masks_d = din("masks", [128, 256], F32)
    sel_d = din("sel", [2, 256])
    fg_d = din("fg_rep", [128, DM])
    rmask_d = din("rmask", [128, 512])
    out_d = nc.dram_tensor("out", [SEQ, DM], F32, kind="ExternalOutput").ap()
    if DEBUG:
        dbg_d = nc.dram_tensor("dbg", [128, 16 * 4096], BF16, kind="ExternalOutput").ap()

    es = ExitStack()
    arena = es.enter_context(nc.sbuf_tensor("arena", [128, ARENA_BYTES // 4], F32))
    psum = es.enter_context(nc.psum_tensor("ps", [128, 4096], F32))
    S = Sched(nc, es)

    def A(off, nbytes, dt, parts=(0, 128)):
        assert off % 4 == 0 and nbytes % 4 == 0 and off + nbytes <= ARENA_BYTES, (off, nbytes)
        v = arena[parts[0]:parts[1], off // 4:(off + nbytes) // 4]
        if dt != F32:
            v = v.bitcast(dt)
        return v

    def bank(i):
        return psum[:, i * 512:(i + 1) * 512]

    def _n(ap):
        n = 1
        for d in list(ap.shape)[1:]:
            n *= int(d)
        return n

    def dma(out, in_):
        nb = _n(out) * _isz(out.dtype) * int(out.shape[0])
        return S.op("sp", lambda e: e.dma_start(out=out, in_=in_), reads=[in_], writes=[out], dma=True,
                    dur=2.0 + nb / 150e3)

    def mm(out, lhsT, rhs, start=True, stop=True):
        return S.op("pe", lambda e: e.matmul(out, lhsT, rhs, start=start, stop=stop),
                    reads=[lhsT, rhs], writes=[out],
                    dur=(0.06 + max(_n(rhs), 64) * 0.0005) * (4 if rhs.dtype == F32 else 1))

    def tr(out, in_, ident):
        return S.op("pe", lambda e: e.transpose(out, in_, ident), reads=[in_, ident], writes=[out], dur=0.1)

    def act(out, in_, func, bias=None, scale=None, accum_out=None):
        kw = {}
        if (bias is not None and not isinstance(bias, (int, float))
                and (scale is None or (isinstance(scale, (int, float)) and float(scale) == 1.0))
                and func != AF.Ln):
            scale = onecol
        if bias is not None:
            kw["bias"] = bias
        if scale is not None:
            kw["scale"] = scale
        if accum_out is not None:
            kw["accum_out"] = accum_out
        rd = [in_] + [a for a in (bias, scale) if a is not None and not isinstance(a, (int, float))]
        return S.op("act", lambda e: e.activation(out, in_, func, **kw), reads=rd,
                    writes=[out] + ([accum_out] if accum_out is not None else []), dur=0.22 + _n(in_) / 1400.0)

    def ts(eng, out, in0, s1, s2, op0, op1=None):
        rd = [in0] + [a for a in (s1, s2) if a is not None and not isinstance(a, (int, float))]
        du = (0.07 + _n(in0) / 960.0) if eng == "dve" else (0.3 + _n(in0) / 450.0)
        if op1 is None:
            return S.op(eng, lambda e: e.tensor_scalar(out, in0, s1, None, op0), reads=rd, writes=[out], dur=du)
        return S.op(eng, lambda e: e.tensor_scalar(out, in0, s1, s2, op0, op1), reads=rd, writes=[out], dur=du)

    def tt(eng, out, in0, in1, op):
        du = (0.07 + _n(in0) / 960.0) if eng == "dve" else (0.3 + _n(in0) / 450.0)
        return S.op(eng, lambda e: e.tensor_tensor(out, in0, in1, op), reads=[in0, in1], writes=[out], dur=du)

    def stt(out, in0, scalar, in1, op0, op1):
        rd = [in0, in1] + ([scalar] if not isinstance(scalar, (int, float)) else [])
        return S.op("dve", lambda e: e.scalar_tensor_tensor(out, in0, scalar, in1, op0, op1),
                    reads=rd, writes=[out], dur=0.07 + _n(in0) / 960.0)

    def cp(eng, out, in_):
        du = (0.07 + _n(in_) / 960.0) if eng == "dve" else (0.3 + _n(in_) / 450.0)
        return S.op(eng, lambda e: e.tensor_copy(out, in_), reads=[in_], writes=[out], dur=du)

    def memset(eng, ap, val):
        du = (0.07 + _n(ap) / 960.0) if eng == "dve" else (0.3 + _n(ap) / 450.0)
        return S.op(eng, lambda e: e.memset(ap, val), reads=[], writes=[ap], dur=du)

    AT0 = 0
    BR0 = 65536
    P0 = 131072
    aT = A(AT0, 65536, BF16).rearrange("p (k t) -> p k t", k=8)
    br = A(BR0, 65536, BF16).rearrange("p (k t) -> p k t", k=8)
    o = P0
    ident = A(o, 512, F32); o += 512
    ident_bf = A(o, 256, BF16); o += 256
    ones_bf = A(o, 256, BF16); o += 256
    masks = A(o, 1024, F32); o += 1024
    maskF, maskB = masks[:, 0:128], masks[:, 128:256]
    vecs = A(o, 768, F32); o += 768
    dv = A(o, 1024, F32); o += 1024
    gt_rep = A(o, 4096, F32); o += 4096
    aTc = A(o, 4096, BF16).rearrange("p (k t) -> p k t", k=8); o += 4096
    rmask = A(o, 2048, F32); o += 2048
    small = A(o, 1024, F32); o += 1024
    onecol = small[:, 255:256]
    PH0 = o
    PH_BYTES = ARENA_BYTES - PH0

    cv = vecs[:, 0:16].rearrange("p (k c) -> p k c", c=2)
    ng = vecs[:, 16:24]
    lbl = vecs[:, 24:40]
    cw = vecs[:, 44:168].rearrange("p (s j) -> p s j", j=NTAP)
    convb = vecs[:, 168:172]
    lng = vecs[:, 172:176]
    lnb = vecs[:, 176:180]
    gs_l, sh_l, gs_c, sh_c = dv[:, 0:8], dv[:, 8:16], dv[:, 16:24], dv[:, 24:32]
    lbv, l1mlb = dv[:, 32:40], dv[:, 40:48]
    cwh = dv[:, 48:172].rearrange("p (s j) -> p s j", j=NTAP)
    tmp8 = dv[:, 172:236]

    dma(vecs, vecs_d)
    dma(ident, ident_d)
    dma(masks, masks_d)
    dma(rmask, rmask_d)
    cp("dve", ident_bf, ident)
    memset("pool", ones_bf, 1.0)
    memset("pool", onecol, 1.0)

    o = PH0
    mod_sb = A(o, 12288, F32, parts=(0, 2)); o += 12288
    bmod = A(o, 12288, F32, parts=(0, 2)); o += 12288
    wm = []
    for i in range(2):
        wm.append(A(o, 8192, F32).rearrange("p (k n) -> p k n", k=8)); o += 8192
    xs = []
    for i in range(3):
        xs.append(A(o, 4096, F32)); o += 4096
    junk = A(o, 2048, BF16); o += 2048
    sel = A(o, 1024, F32, parts=(0, 2)); o += 1024
    assert o <= ARENA_BYTES, o
    dma(sel, sel_d)

    dma(bmod, bmod_d)
    act(vecs[:, 0:16], vecs[:, 0:16], AF.Silu)
    wmod_v = wmod_d.rearrange("(k p) n -> p k n", p=128)
    for blk in range(12):
        w = wm[blk % 2]
        dma(w, wmod_v[:, :, blk * 256:(blk + 1) * 256])
        pb = bank(blk % 2)[0:2, 0:256]
        for k in range(8):
            mm(pb, cv[:, k, :], w[:, k, :], start=(k == 0), stop=(k == 7))
        tt("dve", mod_sb[:, blk * 256:(blk + 1) * 256], pb, bmod[:, blk * 256:(blk + 1) * 256], ALU.add)

    def extract(dst8, cb0, selv):
        for half in range(2):
            pb = bank(2 + half)
            mm(pb, selv, mod_sb[:, (cb0 + half) * 512:(cb0 + half + 1) * 512])
            for kk in range(4):
                k = half * 4 + kk
                scr = junk.bitcast(F32)[:, 0:128]
                tt("dve", scr, pb[:, kk * 128:(kk + 1) * 128], ident, ALU.mult)
                S.op("dve", lambda e, scr=scr, d=dst8[:, k:k + 1]: e.tensor_reduce(d, scr, AX.X, ALU.add),
                     reads=[scr], writes=[dst8[:, k:k + 1]])

    sel_l, sel_c = sel[:, 0:128], sel[:, 128:256]
    extract(sh_l, 0, sel_l)
    extract(gs_l, 2, sel_l)
    extract(sh_c, 0, sel_c)
    extract(gs_c, 2, sel_c)
    for half in range(2):
        pb = bank(2 + half)
        mm(pb, sel_l, mod_sb[:, (4 + half) * 512:(5 + half) * 512])
        act(gt_rep[:, half * 512:(half + 1) * 512], pb, AF.Copy)
    for g in (gs_l, gs_c):
        stt(g, g, 1.0, ng, ALU.add, ALU.mult)
    dlt = small[:, 0:8]
    tt("dve", dlt, lbl[:, 0:8], lbl[:, 8:16], ALU.subtract)
    act(dlt, dlt, AF.Exp, scale=-1.0)
    ts("dve", dlt, dlt, 1.0, None, ALU.add)
    S.op("dve", lambda e: e.reciprocal(lbv, dlt), reads=[dlt], writes=[lbv])
    act(l1mlb, lbv, AF.Ln, scale=-1.0, bias=1.0)
    ts("dve", dv[:, 48:172], vecs[:, 44:168], 0.5, None, ALU.mult)

    if MAXPHASE >= 1:
        ssv = small[:, 16:80]
        tile_ctr = [0]

        def norm_tile(src_rows, dst_fn, gs, sh):
            i = tile_ctr[0]
            tile_ctr[0] += 1
            xt = xs[i % 3]
            ss = ssv[:, (i % 16) * 2:(i % 16) * 2 + 1]
            rs = ssv[:, (i % 16) * 2 + 1:(i % 16) * 2 + 2]
            dma(xt, src_rows)
            act(junk, xt, AF.Square, accum_out=ss)
            act(rs, ss, AF.Ln, scale=1.0 / DM, bias=EPS)
            act(rs, rs, AF.Exp, scale=-0.5)
            act(xt, xt, AF.Copy, scale=rs)
            for k in range(8):
                pb = bank(4 + (i % 2) * 2 + k // 4)
                tr(pb[:, (k % 4) * 128:(k % 4 + 1) * 128], xt[:, k * 128:(k + 1) * 128], ident)
            for k in range(8):
                pb = bank(4 + (i % 2) * 2 + k // 4)
                ts("dve", dst_fn(k), pb[:, (k % 4) * 128:(k % 4 + 1) * 128], gs[:, k:k + 1], sh[:, k:k + 1],
                   ALU.mult, ALU.add)

        for t in range(2):
            norm_tile(ctx_d[t * 128:(t + 1) * 128, :], lambda k, t=t: aTc[:, k, t * 128:(t + 1) * 128], gs_c, sh_c)
        for t in range(32):
            norm_tile(x_d[t * 128:(t + 1) * 128, :], lambda k, t=t: aT[:, k, t * 128:(t + 1) * 128], gs_l, sh_l)

    win_v = win_d.rearrange("(k p) n -> p k n", p=128)

    def load_w(dst_bf, stage, col0, ncols=128):
        dma(stage, win_v[:, :, col0:col0 + ncols])
        act(dst_bf, stage, AF.Copy)

    if MAXPHASE >= 2:
        o = PH0
        wst = []
        for i in range(2):
            wst.append(A(o, 4096, F32).rearrange("p (k n) -> p k n", k=8)); o += 4096
        wgb = []
        for i in range(4):
            wgb.append(A(o, 2048, BF16).rearrange("p (k n) -> p k n", k=8)); o += 2048
        cB0 = o
        wu = []
        for i in range(4):
            wu.append(A(o, 2048, BF16).rearrange("p (k n) -> p k n", k=8)); o += 2048
        gin = A(o, 12288, BF16); o += 12288
        diag = []
        for i in range(2):
            diag.append(A(o, NTAP * 256, BF16).rearrange("p (j n) -> p j n", j=NTAP)); o += NTAP * 256
        th = []
        for i in range(2):
            th.append(A(o, 2048, F32)); o += 2048
        assert o <= ARENA_BYTES, o

        for s in (range(4) if SUB > 10 else ([2] if SUB == 6 else [0])):
            rowmode = s < 2
            wU, wG = wu[(s % 2) * 2], wu[(s % 2) * 2 + 1]
            load_w(wU, wst[0], 2560 + s * 128)
            load_w(wG, wst[1], 3072 + s * 128)
            dg = diag[s % 2]
            for j in range(NTAP if SUB >= 2 else 0):
                ts("dve", dg[:, j, :], ident_bf, cwh[:, s, j:j + 1], None, ALU.mult)
            if (s == 0 or s == 2) and SUB >= 3:
                memset("pool", gin, 0.0)
            for blk in range(8 if SUB >= 4 else 0):
                pu, pg = bank(0 + (blk % 2) * 2), bank(1 + (blk % 2) * 2)
                for k in range(8):
                    mm(pu, wU[:, k, :], aT[:, k, blk * 512:(blk + 1) * 512], start=(k == 0), stop=(k == 7))
                for k in range(8):
                    mm(pg, wG[:, k, :], aT[:, k, blk * 512:(blk + 1) * 512], start=(k == 0), stop=(k == 7))
                tb = th[blk % 2]
                act(tb, pg, AF.Tanh, scale=0.5)
                if rowmode:
                    r0 = blk * 8
                    dst = gin[:, r0 * 80 + 15:r0 * 80 + 15 + 640].rearrange("p (r c) -> p r c", c=80)[:, :, 0:64]
                else:
                    dst = gin[:, (blk * 8 + 15) * 64:(blk * 8 + 15) * 64 + 512].rearrange("p (r c) -> p r c", c=64)
                stt(dst, tb.rearrange("p (r c) -> p r c", c=64), 1.0, pu.rearrange("p (r c) -> p r c", c=64),
                    ALU.add, ALU.mult)
            for blk in range(8 if SUB >= 5 else 0):
                py = bank(4 + blk % 2)
                for j in range(NTAP):
                    if rowmode:
                        r0 = blk * 8
                        src = gin[:, r0 * 80 + j:r0 * 80 + j + 640].rearrange("p (r c) -> p r c", c=80)[:, :, 0:64]
                        dstp = py.rearrange("p (r c) -> p r c", c=64)
                    else:
                        src = gin[:, (blk * 8 + j) * 64:(blk * 8 + j) * 64 + 512]
                        dstp = py
                    mm(dstp, dg[:, j, :], src, start=(j == 0), stop=(j == NTAP - 1))
                if VAR == 0:
                    act(br[:, 4 + s, blk * 512:(blk + 1) * 512], py, AF.Identity, bias=convb[:, s:s + 1])
                elif VAR == 1:
                    ts("dve", br[:, 4 + s, blk * 512:(blk + 1) * 512], py, convb[:, s:s + 1], None, ALU.add)
                elif VAR == 2:
                    act(br[:, 4 + s, blk * 512:(blk + 1) * 512], py, AF.Copy)

    if MAXPHASE >= 3:
        o = cB0
        sqb = []
        for i in range(4):
            sqb.append(A(o, 1024, BF16)); o += 1024
        mean = A(o, 2048, F32); o += 2048
        msq = A(o, 2048, F32); o += 2048
        rstdc = A(o, 2048, F32); o += 2048
        mhalf = A(o, 2048, F32); o += 2048
        tbuf = []
        for i in range(2):
            tbuf.append(A(o, 2048, F32)); o += 2048
        sgb = []
        for i in range(2):
            sgb.append(A(o, 2048, F32)); o += 2048
        assert o <= ARENA_BYTES, o
        memset("pool", mhalf, -0.5)
        for s in range(4):
            load_w(wgb[s], wst[s % 2], 3584 + s * 128)
        for blk in range(8):
            sl = slice(blk * 512, (blk + 1) * 512)
            p1, p2 = bank(0 + (blk % 2) * 4), bank(1 + (blk % 2) * 4)
            for s in range(4):
                act(sqb[s], br[:, 4 + s, sl], AF.Square)
            for s in range(4):
                mm(p1, ones_bf, br[:, 4 + s, sl], start=(s == 0), stop=(s == 3))
            for s in range(4):
                mm(p2, ones_bf, sqb[s], start=(s == 0), stop=(s == 3))
            act(mean, p1, AF.Copy, scale=1.0 / 512)
            tt("dve", msq, mean, mean, ALU.mult)
            ts("dve", rstdc, p2, 1.0 / 512, EPS, ALU.mult, ALU.add)
            tt("dve", rstdc, rstdc, msq, ALU.subtract)
            act(rstdc, rstdc, AF.Ln)
            act(rstdc, rstdc, AF.Exp, scale=-0.5)
            for s in range(4):
                tb = tbuf[s % 2]
                tt("dve", tb, br[:, 4 + s, sl], mean, ALU.subtract)
                tt("dve", tb, tb, rstdc, ALU.mult)
                act(tb, tb, AF.Silu, scale=lng[:, s:s + 1], bias=lnb[:, s:s + 1])
                pgb = bank(2 + s % 2)
                for k in range(8):
                    mm(pgb, wgb[s][:, k, :], aT[:, k, sl], start=(k == 0), stop=(k == 7))
                sg = sgb[s % 2]
                act(sg, pgb, AF.Silu)
                tt("dve", br[:, 4 + s, sl], tb, sg, ALU.mult)

    if MAXPHASE >= 4:
        o = PH0
        wst3 = A(o, 4096, F32).rearrange("p (k n) -> p k n", k=8); o += 4096
        wAs, wBs = [], []
        for i in range(2):
            wAs.append(A(o, 2048, BF16).rearrange("p (k n) -> p k n", k=8)); o += 2048
            wBs.append(A(o, 2048, BF16).rearrange("p (k n) -> p k n", k=8)); o += 2048
        wC = A(o, 2048, BF16).rearrange("p (k n) -> p k n", k=8); o += 2048
        vh = A(o, 34 * 256, BF16).rearrange("p (c e) -> p c e", e=128); o += 34 * 256
        qbf = A(o, 8192, BF16); o += 8192
        e2b, l1b, l2b, bb, bxb = [], [], [], [], []
        for lst in (e2b, l1b, l2b, bb):
            for i in range(2):
                lst.append(A(o, 2048, F32)); o += 2048
        bxb.append(A(o, 2048, F32)); o += 2048
        bxb.append(bxb[0])
        qtb, ktb = [], []
        for lst in (qtb, ktb):
            for i in range(2):
                lst.append(A(o, 1024, BF16)); o += 1024
        kTall = [A(o, 1024, BF16)]; o += 1024
        kTall.append(kTall[0])
        osb1 = A(o, 2048, F32); o += 2048
        etmp = A(o, 2048, F32); o += 2048
        PFall = A(o, 1024, BF16); o += 1024
        PBall = A(o, 1024, BF16); o += 1024
        Zb = []
        for i in range(2):
            Zb.append(A(o, 256, BF16)); o += 256
        Yst = A(o, 512, F32); o += 512
        sc3 = A(o, 512, F32); o += 512
        sq3 = A(o, 1024, BF16); o += 1024
        Uall = A(o, 2048, F32); o += 2048
        assert o <= ARENA_BYTES, o

        memset("pool", PFall, 0.0)
        memset("pool", PBall, 0.0)

        blkctr = [0]
        chctr = [0]

        items = []
        sweep_no = [0]

        def make_vproj(h):
            def vproj():
                for c in range(34):
                    src = aTc[:, :, c * 128:(c + 1) * 128] if c < 2 else aT[:, :, (c - 2) * 128:(c - 1) * 128]
                    pv = bank(4 + c % 2)[:, 0:128]
                    for k in range(8):
                        mm(pv, src[:, k, :], wC[:, k, :], start=(k == 0), stop=(k == 7))
                    if c % 2 == 0:
                        act(vh[:, c, :], pv, AF.Copy)
                    else:
                        cp("dve", vh[:, c, :], pv)
                if h + 1 < NH:
                    load_w(wC, wst3, 1536 + (h + 1) * 128)
            return vproj

        NH = 4 if S3 > 10 else 1
        load_w(wC, wst3, 1536)
        for h in range(NH):
            head_init = make_vproj(h)

            for dirn in range(2):
                fwd = dirn == 0
                lb_ap = lbv[:, dirn * 4 + h:dirn * 4 + h + 1]
                l1m_ap = l1mlb[:, dirn * 4 + h:dirn * 4 + h + 1]
                sw = sweep_no[0]
                sweep_no[0] += 1
                wA, wB = wAs[sw % 2], wBs[sw % 2]

                def wload(wA=wA, wB=wB, fwd=fwd, h=h):
                    load_w(wA, wst3, (512 if fwd else 1024) + h * 128)
                    load_w(wB, wst3, (0 if fwd else 2048) + h * 128)

                def sweep_init():
                    memset("pool", Yst, 0.0)
                blocks = [("ctx", 0)] + [("lat", j) for j in (range(8) if fwd else range(7, -1, -1))]
                msk = maskF if fwd else maskB

                def stage1(kind, j, fwd=fwd, lb_ap=lb_ap, l1m_ap=l1m_ap, wA=wA, wB=wB):
                    bi = blkctr[0]
                    blkctr[0] += 1
                    par = bi % 2
                    lat = kind == "lat"
                    nt = 512 if lat else 256
                    nch = nt // 128
                    src = aT[:, :, j * 512:(j + 1) * 512] if lat else aTc
                    tsl = slice(j * 512, (j + 1) * 512)
                    pz = bank(0 + par * 2)[:, 0:nt]
                    pq = bank(1 + par * 2)[:, 0:nt]
                    for k in range(8):
                        mm(pz, wA[:, k, :], src[:, k, :], start=(k == 0), stop=(k == 7))
                    if lat and fwd:
                        for k in range(8):
                            mm(pq, wB[:, k, :], src[:, k, :], start=(k == 0), stop=(k == 7))
                    e2, l1, l2, bq, bx = (e2b[par][:, 0:nt], l1b[par][:, 0:nt], l2b[par][:, 0:nt],
                                          bb[par][:, 0:nt], bxb[par][:, 0:nt])
                    qt, kt = qtb[par][:, 0:nt], ktb[par][:, 0:nt]
                    act(e2, pz, AF.Exp)
                    act(l1, e2, AF.Ln, bias=lb_ap)
                    act(l2, e2, AF.Ln, bias=1.0)
                    tt("dve", l1, l1, l2, ALU.subtract)
                    S.op("dve", lambda e, bq=bq, l1=l1, nt=nt: e.tensor_tensor_scan(
                        bq, rmask[:, 0:nt], l1, 0.0, ALU.mult, ALU.add), reads=[rmask[:, 0:nt], l1], writes=[bq],
                        dur=0.1 + nt / 480.0)
                    bq3 = bq.rearrange("p (c t) -> p c t", t=128)
                    m4b = bq3[:, :, 63:64].to_broadcast([128, nch, 128])
                    bx3 = bx.rearrange("p (c t) -> p c t", t=128)
                    if fwd:
                        tt("dve", bx3, bq3, m4b, ALU.subtract)
                        tt("dve", l2, l2, bx, ALU.add)
                    else:
                        tt("dve", bx, bq, l1, ALU.subtract)
                        tt("dve", bx3, bx3, m4b, ALU.subtract)
                        tt("dve", l2, bx, l2, ALU.subtract)
                    base = 8 + par * 40
                    m4 = bq3[:, :, 63]
                    tot4 = bq3[:, :, 127]
                    d4 = sc3[:, base:base + nch]
                    cA4 = sc3[:, base + 12:base + 12 + nch]
                    cB4 = sc3[:, base + 16:base + 16 + nch]
                    tt("dve", d4, tot4, m4, ALU.subtract)
                    cAB4 = sc3[:, base + 20:base + 20 + nch]
                    if fwd:
                        act(cA4, m4, AF.Exp)
                        act(cB4, d4, AF.Exp)
                    else:
                        act(cA4, d4, AF.Exp)
                        act(cB4, m4, AF.Exp)
                    act(cAB4, tot4, AF.Exp)
                    if lat:
                        act(e2, bx, AF.Exp, scale=(1.0 if fwd else -1.0))
                    act(kt, l2, AF.Exp, scale=(-1.0 if fwd else 1.0), bias=l1m_ap)
                    if lat:
                        if fwd:
                            tt("dve", qt, pq, e2, ALU.mult)
                            act(qbf[:, tsl], pq, AF.Copy)
                        else:
                            tt("dve", qt, qbf[:, tsl], e2, ALU.mult)
                    return dict(par=par, lat=lat, nt=nt, nch=nch, src=src, tsl=tsl, pz=pz, pq=pq, qt=qt, kt=kt,
                                cA4=cA4, cB4=cB4, cAB4=cAB4, j=j)

                def stage1b(st):
                    par, nch, kt = st["par"], st["nch"], st["kt"]
                    pzb = bank(0 + par * 2).bitcast(BF16)
                    for c in range(nch):
                        tr(pzb[:, c * 128:(c + 1) * 128], kt[:, c * 128:(c + 1) * 128], ident_bf)
                    act(kTall[par][:, 0:nch * 128], pzb[:, 0:nch * 128], AF.Copy)

                def stage2(st, fwd=fwd, msk=msk, h=h, wB=wB):
                    par, lat, nch, tsl, qt, kt = st["par"], st["lat"], st["nch"], st["tsl"], st["qt"], st["kt"]
                    cA4, cB4, j, pq, src = st["cA4"], st["cB4"], st["j"], st["pq"], st["src"]
                    cAB4 = st["cAB4"]
                    po = bank(4 + par)
                    bD, bS = bank(6), bank(7)
                    Pall = PFall if fwd else PBall
                    kTa = kTall[par]
                    corder = list(range(nch)) if fwd else list(range(nch - 1, -1, -1))
                    for c in corder:
                        gch = (j * 4 + c + 2) if lat else c
                        cs = slice(c * 128, (c + 1) * 128)
                        mm(bD[:, cs], kTa[:, cs], vh[:, gch, :])
                    for c in corder:
                        cs = slice(c * 128, (c + 1) * 128)
                        ts("dve", Uall[:, cs], bD[:, cs], cB4[:, c:c + 1], None, ALU.mult)
                    if lat:
                        if fwd:
                            pieces = [(slice(0, 128), slice(64, 128)), (slice(0, 64), slice(0, 64))]
                        else:
                            pieces = [(slice(0, 128), slice(0, 64)), (slice(64, 128), slice(64, 128))]
                        for c in corder:
                            for (sp, tp) in pieces:
                                tq = slice(c * 128 + tp.start, c * 128 + tp.stop)
                                mm(bS[sp, tq], kt[:, c * 128 + sp.start:c * 128 + sp.stop], qt[:, tq])
                        for (sp, tp) in pieces:
                            w = tp.stop - tp.start
                            np_ = sp.stop - sp.start
                            P3 = Pall[sp, :].rearrange("p (c t) -> p c t", t=128)[:, :, tp]
                            S3v = bS[sp, :].rearrange("p (c t) -> p c t", t=128)[:, :, tp]
                            M3 = msk[sp, tp].unsqueeze(1).to_broadcast([np_, nch, w])
                            tt("dve", P3, S3v, M3, ALU.mult)
                    for c in corder:
                        ci = chctr[0]
                        chctr[0] += 1
                        cpar = ci % 2
                        cs = slice(c * 128, (c + 1) * 128)
                        gch = (j * 4 + c + 2) if lat else c
                        if lat:
                            ts("pool", Zb[cpar], Yst, cA4[:, c:c + 1], 1.0, ALU.mult, ALU.mult)
                            mm(po[:, cs], vh[:, gch, :], Pall[:, cs], start=True, stop=False)
                            mm(po[:, cs], Zb[cpar], qt[:, cs], start=False, stop=True)
                        stt(Yst, Yst, cAB4[:, c:c + 1], Uall[:, cs], ALU.mult, ALU.add)
                    if lat and fwd:
                        act(br[:, h, tsl], po, AF.Copy)
                    if lat and not fwd and S3 == 7:
                        act(br[:, h, tsl], po, AF.Copy)
                    elif lat and not fwd:
                        osb = osb1
                        tt("dve", osb, po, br[:, h, tsl], ALU.add)
                        act(sq3, osb, AF.Square)
                        pn = bank(0 + par * 2)
                        mm(pn, ones_bf, sq3)
                        for k in range(8):
                            mm(pq, wB[:, k, :], src[:, k, :], start=(k == 0), stop=(k == 7))
                        act(etmp, pn, AF.Ln, scale=1.0 / 128, bias=EPS)
                        act(etmp, etmp, AF.Exp, scale=-0.5)
                        tt("pool", osb, osb, etmp, ALU.mult)
                        act(etmp, pq, AF.Exp, scale=-1.0)
                        act(etmp, etmp, AF.Ln, bias=1.0)
                        act(etmp, etmp, AF.Exp, scale=-1.0)
                        tt("dve", etmp, pq, etmp, ALU.mult)
                        sg3 = etmp
                        tt("pool", br[:, h, tsl], osb, sg3, ALU.mult)

                for bi_, blk in enumerate(blocks):
                    items.append(dict(s1=stage1, s1b=stage1b, s2=stage2, blk=blk, first_sweep=(bi_ == 0),
                                      first_head=(bi_ == 0 and dirn == 0), wload=wload, sweep_init=sweep_init,
                                      head_init=head_init))

        sweeps_first = [it for it in items if it["first_sweep"]]
        sweeps_first[0]["wload"]()
        nxt_sweep = 1
        prev, prev_st = None, None
        for it in items:
            cur_st = it["s1"](*it["blk"])
            if prev is not None:
                if prev["first_head"]:
                    prev["head_init"]()
                if prev["first_sweep"]:
                    prev["sweep_init"]()
                prev["s2"](prev_st)
            if it["first_sweep"] and nxt_sweep < len(sweeps_first):
                sweeps_first[nxt_sweep]["wload"]()
                nxt_sweep += 1
            it["s1b"](cur_st)
            prev, prev_st = it, cur_st
        if prev["first_head"]:
            prev["head_init"]()
        if prev["first_sweep"]:
            prev["sweep_init"]()
        prev["s2"](prev_st)

    if MAXPHASE >= 5:
        o = PH0
        wo = A(o, 16384, BF16).rearrange("p (k n) -> p k n", k=8); o += 16384
        wos = A(o, 4096, F32); o += 4096
        xr = []
        for i in range(3):
            xr.append(A(o, 4096, F32)); o += 4096
        ht = []
        for i in range(2):
            ht.append(A(o, 4096, F32)); o += 4096
        junk4 = A(o, 2048, BF16); o += 2048
        fs = A(o, 512, F32); o += 512
        fg_rep = A(o, 4096, F32); o += 4096
        dma(fg_rep, fg_d)
        assert o <= ARENA_BYTES, o
        hgn = vecs[:, 40:44]
        wout_v = wout_d.rearrange("(k p) n -> p k n", p=128)
        for k in range(8):
            dma(wos, wout_v[:, k, :])
            if k < 4:
                stt(wo[:, k, :], wos, hgn[:, k:k + 1], gt_rep, ALU.mult, ALU.mult)
            else:
                tt("dve", wo[:, k, :], wos, gt_rep, ALU.mult)
        for t in range(2 if P5 >= 2 else 0):
            dma(xr[t % 3], x_d[t * 128:(t + 1) * 128, :])
        for t in range(32 if P5 >= 2 else 0):
            rows = slice(t * 128, (t + 1) * 128)
            xt = xr[t % 3]
            if t + 2 < 32:
                dma(xr[(t + 2) % 3], x_d[(t + 2) * 128:(t + 3) * 128, :])
            hb = ht[t % 2]
            for half in range(2):
                pb = bank((t % 2) * 2 + half)
                for g in range(8):
                    mm(pb, br[:, g, rows], wo[:, g, half * 512:(half + 1) * 512], start=(g == 0), stop=(g == 7))
                tt("dve", hb[:, half * 512:(half + 1) * 512], pb, xt[:, half * 512:(half + 1) * 512], ALU.add)
            ss = fs[:, (t % 16) * 2:(t % 16) * 2 + 1]
            rs = fs[:, (t % 16) * 2 + 1:(t % 16) * 2 + 2]
            if P5 >= 3:
                act(junk4, hb, AF.Square, accum_out=ss)
                act(rs, ss, AF.Ln, scale=1.0 / DM, bias=EPS)
                act(rs, rs, AF.Exp, scale=-0.5)
                stt(hb, hb, rs, fg_rep, ALU.mult, ALU.mult)
            if P5 >= 4:
                dma(out_d[rows, :], hb)

    if DEBUG:
        if MAXPHASE >= 1:
            dma(dbg_d[:, 0:8 * 4096], A(AT0, 65536, BF16))
        if MAXPHASE >= 2 and SUB > 10:
            dma(dbg_d[:, 12 * 4096:16 * 4096], A(BR0 + 32768, 32768, BF16))
        if MAXPHASE >= 4 and S3 > 10:
            dma(dbg_d[:, 8 * 4096:12 * 4096], A(BR0, 32768, BF16))
        if MAXPHASE >= 4 and S3 in (5, 7):
            dma(dbg_d[:, 8 * 4096:9 * 4096], A(BR0, 8192, BF16))

    if REORDER:
        S.reorder()
    S.emit()
    es.close()
    return nc


def _fm(v, k):
    return np.ascontiguousarray(np.asarray(v, np.float32).reshape(k, 128).T)


_NC_CACHE = {}


def kernel(x, c, ctx, c_ctx, norm_g, w_mod, b_mod, w_in, lb_logits, hgrn_norm_g,
           conv_w, conv_b, conv_ln_g, conv_ln_b, w_out, final_norm_g):
    f32 = np.float32
    x = np.asarray(x, f32)
    ctx = np.asarray(ctx, f32)
    c = np.asarray(c, f32)
    c_ctx = np.asarray(c_ctx, f32)
    w_mod0 = np.ascontiguousarray(np.asarray(w_mod, f32)[0])
    w_in0 = np.ascontiguousarray(np.asarray(w_in, f32)[0])
    w_out0 = np.ascontiguousarray(np.asarray(w_out, f32)[0])
    b_mod2 = np.ascontiguousarray(np.tile(np.asarray(b_mod, f32)[0][None, :], (2, 1)))
    fg_rep = np.ascontiguousarray(np.tile(np.asarray(final_norm_g, f32)[None, :], (128, 1)))
    ident = np.eye(128, dtype=f32)
    s_idx = np.arange(128)[:, None]
    t_idx = np.arange(128)[None, :]
    masks = np.concatenate([(s_idx <= t_idx), (s_idx >= t_idx)], axis=1).astype(np.float32)
    sel = np.zeros((2, 256), f32)
    sel[0, 0:128] = 1.0
    sel[1, 128:256] = 1.0
    rmask = np.ones((128, 512), f32)
    rmask[:, 0::128] = 0.0

    lbl = np.asarray(lb_logits, f32)
    lbl_fm = lbl.reshape(2, 2, 4, 128).transpose(3, 0, 1, 2).reshape(128, 16)
    cwv = np.asarray(conv_w, f32)[0]
    cw_fm = cwv.reshape(NTAP, 4, 128).transpose(2, 1, 0).reshape(128, 4 * NTAP)

    in_maps = []
    for b in range(N_CORES):
        vecs = np.zeros((128, 192), f32)
        cvv = np.stack([_fm(c[b], 8), _fm(c_ctx, 8)], axis=-1)
        vecs[:, 0:16] = cvv.reshape(128, 16)
        vecs[:, 16:24] = _fm(np.asarray(norm_g, f32)[0], 8)
        vecs[:, 24:40] = lbl_fm
        vecs[:, 40:44] = _fm(np.asarray(hgrn_norm_g, f32)[0], 4)
        vecs[:, 44:168] = cw_fm
        vecs[:, 168:172] = _fm(np.asarray(conv_b, f32)[0], 4)
        vecs[:, 172:176] = _fm(np.asarray(conv_ln_g, f32)[0], 4)
        vecs[:, 176:180] = _fm(np.asarray(conv_ln_b, f32)[0], 4)
        in_maps.append({
            "x": np.ascontiguousarray(x[b]),
            "ctx": np.ascontiguousarray(ctx[b]),
            "w_mod": w_mod0, "b_mod2": b_mod2, "w_in": w_in0, "w_out": w_out0,
            "vecs": vecs, "ident": ident, "masks": masks, "sel": sel,
            "fg_rep": fg_rep, "rmask": rmask,
        })
    if "nc" not in _NC_CACHE:
        _NC_CACHE["nc"] = build_program()
    nc = _NC_CACHE["nc"]
    res = run_bass_kernel_spmd(nc, in_maps, core_ids=list(range(N_CORES)))
    out = np.stack([np.asarray(r["out"], f32) for r in res.results], axis=0)
    if DEBUG:
        kernel.dbg = [np.asarray(r["dbg"]) for r in res.results]
    return out
```

```python
import numpy as np
from contextlib import ExitStack

import concourse.bass as bass
import concourse.mybir as mybir
from concourse.bass_utils import run_bass_kernel_spmd

F32 = mybir.dt.float32
BF16 = mybir.dt.bfloat16
U32 = mybir.dt.uint32
AF = mybir.ActivationFunctionType
ALU = mybir.AluOpType
AX = mybir.AxisListType

N_CORES = 8
SEQ = 4096
DM = 1024
CTX = 256
D_IN = 4096
EPS = 1e-6
NTAP = 31
ARENA_BYTES = 207 * 1024

DEBUG = False
MAXPHASE = 99
SUB = 99
VAR = 1
LV = 99
LQ = 31
P5 = 99
S3 = 99
REORDER = True
REORDER_PE = True
SLACK = 0.25


def _isz(dt):
    return mybir.dt.size(dt)


class _Reg:
    __slots__ = ("name", "p0", "p1", "ivs", "lo", "hi")

    def __init__(self, name, p0, p1, ivs):
        self.name, self.p0, self.p1, self.ivs = name, p0, p1, ivs
        self.lo = ivs[0][0]
        self.hi = ivs[-1][1]


def _region(ap):
    t = ap.tensor
    if type(t).__name__.startswith("DRam"):
        return None
    isz = _isz(ap.dtype)
    dims = [tuple(d) for d in ap.ap]
    row = int(t.shape[-1]) * _isz(t.dtype) if len(t.shape) == 2 else None
    if row is None:
        n = 1
        for s in t.shape[1:]:
            n *= int(s)
        row = n * _isz(t.dtype)
    off = int(ap.offset) * isz
    p0 = off // row
    fo = off % row
    pn = dims[0][1]
    free = [(s * isz, c) for (s, c) in dims[1:] if c > 1]
    span = isz
    for s, c in free:
        span += abs(s) * (c - 1)
    ivs = [(fo, fo + span)]
    if len(free) == 2:
        (s0, c0), (s1, c1) = free
        if s1 == isz and c0 <= 64 and c1 * isz < s0:
            ivs = [(fo + i * s0, fo + i * s0 + c1 * isz) for i in range(c0)]
    elif len(free) == 3:
        (s0, c0), (s1, c1), (s2, c2) = free
        if s2 == isz and c0 * c1 <= 64 and c2 * isz < s1 and s1 * c1 <= s0:
            ivs = [(fo + i * s0 + j * s1, fo + i * s0 + j * s1 + c2 * isz)
                   for i in range(c0) for j in range(c1)]
    return _Reg(t.name, p0, p0 + pn, ivs)


def _overlap(a, b):
    if a.p1 <= b.p0 or b.p1 <= a.p0 or a.hi <= b.lo or b.hi <= a.lo:
        return False
    if len(a.ivs) == 1 and len(b.ivs) == 1:
        return True
    for (l0, h0) in a.ivs:
        for (l1, h1) in b.ivs:
            if l0 < h1 and l1 < h0:
                return True
    return False


def _covers(w, r):
    if w.p0 > r.p0 or w.p1 < r.p1:
        return False
    for (l, h) in r.ivs:
        ok = False
        for (wl, wh) in w.ivs:
            if wl <= l and h <= wh:
                ok = True
                break
        if not ok:
            return False
    return True


class _Op:
    __slots__ = ("eng", "fn", "deps", "raw", "signal", "count", "is_dma", "semi", "target", "idx", "dur", "sdeps")


class Sched:
    NDMA = 8

    def __init__(self, nc, es):
        self.nc = nc
        self.h = {"pe": nc.tensor, "act": nc.scalar, "dve": nc.vector, "pool": nc.gpsimd, "sp": nc.sync}
        self.sem = {e: es.enter_context(nc.semaphore("s_" + e)) for e in ("pe", "act", "dve", "pool")}
        self.dsem = [es.enter_context(nc.semaphore("s_dma%d" % i)) for i in range(self.NDMA)]
        self.ops = []
        self.rec = {}
        self.ndma = 0
        self.last_dma = None
        self.last_on = {}

    def op(self, eng, fn, reads=(), writes=(), dma=False, dur=0.3):
        o = _Op()
        o.dur = dur
        o.eng, o.fn, o.is_dma = eng, fn, dma
        o.signal, o.count, o.idx = False, 0, len(self.ops)
        deps = {}

        def add(d, raw):
            k = d.idx
            if k in deps:
                deps[k] = (d, deps[k][1] or raw)
            else:
                deps[k] = (d, raw)

        rregs = [r for r in (_region(a) for a in reads if a is not None) if r is not None]
        wregs = [r for r in (_region(a) for a in writes if a is not None) if r is not None]
        def _bankify(w):
            if w.name != "ps":
                return w
            return _Reg(w.name, 0, 128, [((w.lo >> 11) << 11, (((w.hi - 1) >> 11) << 11) + 2048)])
        wregs = [_bankify(w) for w in wregs]
        rregs = [_bankify(r) for r in rregs]
        def keys(r):
            return [(r.name, b) for b in range(r.lo >> 13, ((r.hi - 1) >> 13) + 1)]

        for r in rregs:
            for key in keys(r):
                for (reg, d, isw) in self.rec.get(key, ()):
                    if (isw or (r.name == "ps" and d.eng != eng)) and _overlap(reg, r):
                        add(d, True)
        for w in wregs:
            for key in keys(w):
                for (reg, d, isw) in self.rec.get(key, ()):
                    if _overlap(reg, w):
                        add(d, False)
        o.deps = []
        o.sdeps = [d for (d, raw) in deps.values()]
        if eng in (("sp",) if REORDER_PE else ("pe", "sp")) and self.last_on.get(eng) is not None:
            o.sdeps.append(self.last_on[eng])
        self.last_on[eng] = o
        for (d, raw) in deps.values():
            if d.is_dma:
                o.deps.append(d)
                continue
            if d.eng == eng and eng == "pe":
                continue
            o.deps.append(d)
            d.signal = True
        for w in wregs:
            for key in keys(w):
                lst = self.rec.setdefault(key, [])
                lst[:] = [x for x in lst if not _covers(w, x[0])]
                lst.append((w, o, True))
        for r in rregs:
            for key in keys(r):
                lst = self.rec.setdefault(key, [])
                if eng == "pe" and not REORDER_PE:
                    lst[:] = [x for x in lst if not ((not x[2]) and x[1].eng == eng and _covers(r, x[0]))]
                lst.append((r, o, False))
        if dma:
            o.semi = self.ndma % self.NDMA
            o.target = 16 * (self.ndma // self.NDMA + 1)
            self.ndma += 1
            self.last_dma = o
        self.ops.append(o)
        return o

    def reorder(self, window=600):
        ops = self.ops
        n = len(ops)
        succ = [[] for _ in range(n)]
        indeg = [0] * n
        for o in ops:
            seen = set()
            for d in o.sdeps:
                if d.idx in seen:
                    continue
                seen.add(d.idx)
                succ[d.idx].append(o.idx)
                indeg[o.idx] += 1
        bl = [0.0] * n
        for i in range(n - 1, -1, -1):
            m = 0.0
            for j in succ[i]:
                if bl[j] > m:
                    m = bl[j]
            bl[i] = ops[i].dur + m
        fin = [0.0] * n
        efree = {e: 0.0 for e in self.h}
        ready = [i for i in range(n) if indeg[i] == 0]
        rstart = {i: 0.0 for i in ready}
        done = [False] * n
        lo = 0
        order = []
        while ready:
            while lo < n and done[lo]:
                lo += 1
            best, bkey = None, None
            cands = []
            for i in ready:
                if i > lo + window:
                    continue
                st = max(efree[ops[i].eng], rstart[i])
                cands.append((st, i))
                key = (st, i)
                if bkey is None or key < bkey:
                    best, bkey = i, key
            if best is not None and SLACK > 0:
                lim = bkey[0] + SLACK
                bb = None
                for (st, i) in cands:
                    if st <= lim:
                        k2 = (-bl[i], st, i)
                        if bb is None or k2 < bb:
                            bb = k2
                best = bb[2]
                bkey = (bb[1], best)
            if best is None:
                best = min(ready)
                bkey = (max(efree[ops[best].eng], rstart[best]), best)
            o = ops[best]
            ready.remove(best)
            done[best] = True
            order.append(o)
            f = bkey[0] + o.dur
            if o.is_dma:
                efree[o.eng] = bkey[0] + 0.1
            else:
                efree[o.eng] = f
            fin[best] = f
            for j in succ[best]:
                indeg[j] -= 1
                lat = 0.1 if ops[j].eng == o.eng else 0.8
                rstart[j] = max(rstart.get(j, 0.0), f + lat)
                if indeg[j] == 0:
                    ready.append(j)
        assert len(order) == n, (len(order), n)
        self.ops = order
        self.est_total = max(fin) if fin else 0.0

    def emit(self):
        cnt = {e: 0 for e in self.sem}
        for o in self.ops:
            if (not o.is_dma) and o.signal:
                cnt[o.eng] += 1
                o.count = cnt[o.eng]
        waited = {}
        last_dma_on = {}
        for o in self.ops:
            eng = self.h[o.eng]
            waits = {}
            for d in o.deps:
                if d.is_dma:
                    key = ("d", d.semi)
                    waits[key] = max(waits.get(key, 0), d.target)
                else:
                    key = ("e", d.eng)
                    waits[key] = max(waits.get(key, 0), d.count)
            if o.is_dma and o.target > 16:
                key = ("d", o.semi)
                waits[key] = max(waits.get(key, 0), o.target - 16)
            for key, val in waits.items():
                wk = (o.eng, key)
                if waited.get(wk, 0) >= val:
                    continue
                waited[wk] = val
                s = self.dsem[key[1]] if key[0] == "d" else self.sem[key[1]]
                eng.wait_ge(s, val)
            ins = o.fn(eng)
            if o.is_dma:
                ins.then_inc(self.dsem[o.semi], 16)
                last_dma_on[o.semi] = o.target
            elif o.signal:
                ins.then_inc(self.sem[o.eng], 1)
        for semi, tgt in last_dma_on.items():
            self.h["sp"].wait_ge(self.dsem[semi], tgt)
        for e, c in cnt.items():
            if c > 0:
                self.h["sp"].wait_ge(self.sem[e], c)


def build_program():
    nc = bass.Bass("TRN2", target_bir_lowering=False)
    dr = {}

    def din(name, shape, dt=F32):
        dr[name] = nc.dram_tensor(name, list(shape), dt, kind="ExternalInput").ap()
        return dr[name]

    x_d = din("x", [SEQ, DM])
    ctx_d = din("ctx", [CTX, DM])
    wmod_d = din("w_mod", [DM, 3 * DM])
    bmod_d = din("b_mod2", [2, 3 * DM])
    win_d = din("w_in", [DM, D_IN])
    wout_d = din("w_out", [DM, DM])
    vecs_d = din("vecs", [128, 192])
    ident_d = din("ident", [128, 128])
    masks_d = din("masks", [128, 256], F32)
    sel_d = din("sel", [2, 256])
    fg_d = din("fg_rep", [128, DM])
    rmask_d = din("rmask", [128, 512])
    out_d = nc.dram_tensor("out", [SEQ, DM], F32, kind="ExternalOutput").ap()
    if DEBUG:
        dbg_d = nc.dram_tensor("dbg", [128, 16 * 4096], BF16, kind="ExternalOutput").ap()

    es = ExitStack()
    arena = es.enter_context(nc.sbuf_tensor("arena", [128, ARENA_BYTES // 4], F32))
    psum = es.enter_context(nc.psum_tensor("ps", [128, 4096], F32))
    S = Sched(nc, es)

    def A(off, nbytes, dt, parts=(0, 128)):
        assert off % 4 == 0 and nbytes % 4 == 0 and off + nbytes <= ARENA_BYTES, (off, nbytes)
        v = arena[parts[0]:parts[1], off // 4:(off + nbytes) // 4]
        if dt != F32:
            v = v.bitcast(dt)
        return v

    def bank(i):
        return psum[:, i * 512:(i + 1) * 512]

    def _n(ap):
        n = 1
        for d in list(ap.shape)[1:]:
            n *= int(d)
        return n

    def dma(out, in_):
        nb = _n(out) * _isz(out.dtype) * int(out.shape[0])
        return S.op("sp", lambda e: e.dma_start(out=out, in_=in_), reads=[in_], writes=[out], dma=True,
                    dur=2.0 + nb / 150e3)

    def mm(out, lhsT, rhs, start=True, stop=True):
        return S.op("pe", lambda e: e.matmul(out, lhsT, rhs, start=start, stop=stop),
                    reads=[lhsT, rhs], writes=[out],
                    dur=(0.06 + max(_n(rhs), 64) * 0.0005) * (4 if rhs.dtype == F32 else 1))

    def tr(out, in_, ident):
        return S.op("pe", lambda e: e.transpose(out, in_, ident), reads=[in_, ident], writes=[out], dur=0.1)

    def act(out, in_, func, bias=None, scale=None, accum_out=None):
        kw = {}
        if (bias is not None and not isinstance(bias, (int, float))
                and (scale is None or (isinstance(scale, (int, float)) and float(scale) == 1.0))
                and func != AF.Ln):
            scale = onecol
        if bias is not None:
            kw["bias"] = bias
        if scale is not None:
            kw["scale"] = scale
        if accum_out is not None:
            kw["accum_out"] = accum_out
        rd = [in_] + [a for a in (bias, scale) if a is not None and not isinstance(a, (int, float))]
        return S.op("act", lambda e: e.activation(out, in_, func, **kw), reads=rd,
                    writes=[out] + ([accum_out] if accum_out is not None else []), dur=0.22 + _n(in_) / 1400.0)

    def ts(eng, out, in0, s1, s2, op0, op1=None):
        rd = [in0] + [a for a in (s1, s2) if a is not None and not isinstance(a, (int, float))]
        du = (0.07 + _n(in0) / 960.0) if eng == "dve" else (0.3 + _n(in0) / 450.0)
        if op1 is None:
            return S.op(eng, lambda e: e.tensor_scalar(out, in0, s1, None, op0), reads=rd, writes=[out], dur=du)
        return S.op(eng, lambda e: e.tensor_scalar(out, in0, s1, s2, op0, op1), reads=rd, writes=[out], dur=du)

    def tt(eng, out, in0, in1, op):
        du = (0.07 + _n(in0) / 960.0) if eng == "dve" else (0.3 + _n(in0) / 450.0)
        return S.op(eng, lambda e: e.tensor_tensor(out, in0, in1, op), reads=[in0, in1], writes=[out], dur=du)

    def stt(out, in0, scalar, in1, op0, op1):
        rd = [in0, in1] + ([scalar] if not isinstance(scalar, (int, float)) else [])
        return S.op("dve", lambda e: e.scalar_tensor_tensor(out, in0, scalar, in1, op0, op1),
                    reads=rd, writes=[out], dur=0.07 + _n(in0) / 960.0)

    def cp(eng, out, in_):
        du = (0.07 + _n(in_) / 960.0) if eng == "dve" else (0.3 + _n(in_) / 450.0)
        return S.op(eng, lambda e: e.tensor_copy(out, in_), reads=[in_], writes=[out], dur=du)

    def memset(eng, ap, val):
        du = (0.07 + _n(ap) / 960.0) if eng == "dve" else (0.3 + _n(ap) / 450.0)
        return S.op(eng, lambda e: e.memset(ap, val), reads=[], writes=[ap], dur=du)

    AT0 = 0
    BR0 = 65536
    P0 = 131072
    aT = A(AT0, 65536, BF16).rearrange("p (k t) -> p k t", k=8)
    br = A(BR0, 65536, BF16).rearrange("p (k t) -> p k t", k=8)
    o = P0
    ident = A(o, 512, F32); o += 512
    ident_bf = A(o, 256, BF16); o += 256
    ones_bf = A(o, 256, BF16); o += 256
    masks = A(o, 1024, F32); o += 1024
    maskF, maskB = masks[:, 0:128], masks[:, 128:256]
    vecs = A(o, 768, F32); o += 768
    dv = A(o, 1024, F32); o += 1024
    gt_rep = A(o, 4096, F32); o += 4096
    aTc = A(o, 4096, BF16).rearrange("p (k t) -> p k t", k=8); o += 4096
    rmask = A(o, 2048, F32); o += 2048
    small = A(o, 1024, F32); o += 1024
    onecol = small[:, 255:256]
    PH0 = o
    PH_BYTES = ARENA_BYTES - PH0

    cv = vecs[:, 0:16].rearrange("p (k c) -> p k c", c=2)
    ng = vecs[:, 16:24]
    lbl = vecs[:, 24:40]
    cw = vecs[:, 44:168].rearrange("p (s j) -> p s j", j=NTAP)
    convb = vecs[:, 168:172]
    lng = vecs[:, 172:176]
    lnb = vecs[:, 176:180]
    gs_l, sh_l, gs_c, sh_c = dv[:, 0:8], dv[:, 8:16], dv[:, 16:24], dv[:, 24:32]
    lbv, l1mlb = dv[:, 32:40], dv[:, 40:48]
    cwh = dv[:, 48:172].rearrange("p (s j) -> p s j", j=NTAP)
    tmp8 = dv[:, 172:236]

    dma(vecs, vecs_d)
    dma(ident, ident_d)
    dma(masks, masks_d)
    dma(rmask, rmask_d)
    cp("dve", ident_bf, ident)
    memset("pool", ones_bf, 1.0)
    memset("pool", onecol, 1.0)

    o = PH0
    mod_sb = A(o, 12288, F32, parts=(0, 2)); o += 12288
    bmod = A(o, 12288, F32, parts=(0, 2)); o += 12288
    wm = []
    for i in range(2):
        wm.append(A(o, 8192, F32).rearrange("p (k n) -> p k n", k=8)); o += 8192
    xs = []
    for i in range(3):
        xs.append(A(o, 4096, F32)); o += 4096
    junk = A(o, 2048, BF16); o += 2048
    sel = A(o, 1024, F32, parts=(0, 2)); o += 1024
    assert o <= ARENA_BYTES, o
    dma(sel, sel_d)

    dma(bmod, bmod_d)
    act(vecs[:, 0:16], vecs[:, 0:16], AF.Silu)
    wmod_v = wmod_d.rearrange("(k p) n -> p k n", p=128)
    for blk in range(12):
        w = wm[blk % 2]
        dma(w, wmod_v[:, :, blk * 256:(blk + 1) * 256])
        pb = bank(blk % 2)[0:2, 0:256]
        for k in range(8):
            mm(pb, cv[:, k, :], w[:, k, :], start=(k == 0), stop=(k == 7))
        tt("dve", mod_sb[:, blk * 256:(blk + 1) * 256], pb, bmod[:, blk * 256:(blk + 1) * 256], ALU.add)

    def extract(dst8, cb0, selv):
        for half in range(2):
            pb = bank(2 + half)
            mm(pb, selv, mod_sb[:, (cb0 + half) * 512:(cb0 + half + 1) * 512])
            for kk in range(4):
                k = half * 4 + kk
                scr = junk.bitcast(F32)[:, 0:128]
                tt("dve", scr, pb[:, kk * 128:(kk + 1) * 128], ident, ALU.mult)
                S.op("dve", lambda e, scr=scr, d=dst8[:, k:k + 1]: e.tensor_reduce(d, scr, AX.X, ALU.add),
                     reads=[scr], writes=[dst8[:, k:k + 1]])

    sel_l, sel_c = sel[:, 0:128], sel[:, 128:256]
    extract(sh_l, 0, sel_l)
    extract(gs_l, 2, sel_l)
    extract(sh_c, 0, sel_c)
    extract(gs_c, 2, sel_c)
    for half in range(2):
        pb = bank(2 + half)
        mm(pb, sel_l, mod_sb[:, (4 + half) * 512:(5 + half) * 512])
        act(gt_rep[:, half * 512:(half + 1) * 512], pb, AF.Copy)
    for g in (gs_l, gs_c):
        stt(g, g, 1.0, ng, ALU.add, ALU.mult)
    dlt = small[:, 0:8]
    tt("dve", dlt, lbl[:, 0:8], lbl[:, 8:16], ALU.subtract)
    act(dlt, dlt, AF.Exp, scale=-1.0)
    ts("dve", dlt, dlt, 1.0, None, ALU.add)
    S.op("dve", lambda e: e.reciprocal(lbv, dlt), reads=[dlt], writes=[lbv])
    act(l1mlb, lbv, AF.Ln, scale=-1.0, bias=1.0)
    ts("dve", dv[:, 48:172], vecs[:, 44:168], 0.5, None, ALU.mult)

    if MAXPHASE >= 1:
        ssv = small[:, 16:80]
        tile_ctr = [0]

        def norm_tile(src_rows, dst_fn, gs, sh):
            i = tile_ctr[0]
            tile_ctr[0] += 1
            xt = xs[i % 3]
            ss = ssv[:, (i % 16) * 2:(i % 16) * 2 + 1]
            rs = ssv[:, (i % 16) * 2 + 1:(i % 16) * 2 + 2]
            dma(xt, src_rows)
            act(junk, xt, AF.Square, accum_out=ss)
            act(rs, ss, AF.Ln, scale=1.0 / DM, bias=EPS)
            act(rs, rs, AF.Exp, scale=-0.5)
            act(xt, xt, AF.Copy, scale=rs)
            for k in range(8):
                pb = bank(4 + (i % 2) * 2 + k // 4)
                tr(pb[:, (k % 4) * 128:(k % 4 + 1) * 128], xt[:, k * 128:(k + 1) * 128], ident)
            for k in range(8):
                pb = bank(4 + (i % 2) * 2 + k // 4)
                ts("dve", dst_fn(k), pb[:, (k % 4) * 128:(k % 4 + 1) * 128], gs[:, k:k + 1], sh[:, k:k + 1],
                   ALU.mult, ALU.add)

        for t in range(2):
            norm_tile(ctx_d[t * 128:(t + 1) * 128, :], lambda k, t=t: aTc[:, k, t * 128:(t + 1) * 128], gs_c, sh_c)
        for t in range(32):
            norm_tile(x_d[t * 128:(t + 1) * 128, :], lambda k, t=t: aT[:, k, t * 128:(t + 1) * 128], gs_l, sh_l)

    win_v = win_d.rearrange("(k p) n -> p k n", p=128)

    def load_w(dst_bf, stage, col0, ncols=128):
        dma(stage, win_v[:, :, col0:col0 + ncols])
        act(dst_bf, stage, AF.Copy)

    if MAXPHASE >= 2:
        o = PH0
        wst = []
        for i in range(2):
            wst.append(A(o, 4096, F32).rearrange("p (k n) -> p k n", k=8)); o += 4096
        wgb = []
        for i in range(4):
            wgb.append(A(o, 2048, BF16).rearrange("p (k n) -> p k n", k=8)); o += 2048
        cB0 = o
        wu = []
        for i in range(4):
            wu.append(A(o, 2048, BF16).rearrange("p (k n) -> p k n", k=8)); o += 2048
        gin = A(o, 12288, BF16); o += 12288
        diag = []
        for i in range(2):
            diag.append(A(o, NTAP * 256, BF16).rearrange("p (j n) -> p j n", j=NTAP)); o += NTAP * 256
        th = []
        for i in range(2):
            th.append(A(o, 2048, F32)); o += 2048
        assert o <= ARENA_BYTES, o

        for s in (range(4) if SUB > 10 else ([2] if SUB == 6 else [0])):
            rowmode = s < 2
            wU, wG = wu[(s % 2) * 2], wu[(s % 2) * 2 + 1]
            load_w(wU, wst[0], 2560 + s * 128)
            load_w(wG, wst[1], 3072 + s * 128)
            dg = diag[s % 2]
            for j in range(NTAP if SUB >= 2 else 0):
                ts("dve", dg[:, j, :], ident_bf, cwh[:, s, j:j + 1], None, ALU.mult)
            if (s == 0 or s == 2) and SUB >= 3:
                memset("pool", gin, 0.0)
            for blk in range(8 if SUB >= 4 else 0):
                pu, pg = bank(0 + (blk % 2) * 2), bank(1 + (blk % 2) * 2)
                for k in range(8):
                    mm(pu, wU[:, k, :], aT[:, k, blk * 512:(blk + 1) * 512], start=(k == 0), stop=(k == 7))
                for k in range(8):
                    mm(pg, wG[:, k, :], aT[:, k, blk * 512:(blk + 1) * 512], start=(k == 0), stop=(k == 7))
                tb = th[blk % 2]
                act(tb, pg, AF.Tanh, scale=0.5)
                if rowmode:
                    r0 = blk * 8
                    dst = gin[:, r0 * 80 + 15:r0 * 80 + 15 + 640].rearrange("p (r c) -> p r c", c=80)[:, :, 0:64]
                else:
                    dst = gin[:, (blk * 8 + 15) * 64:(blk * 8 + 15) * 64 + 512].rearrange("p (r c) -> p r c", c=64)
                stt(dst, tb.rearrange("p (r c) -> p r c", c=64), 1.0, pu.rearrange("p (r c) -> p r c", c=64),
                    ALU.add, ALU.mult)
            for blk in range(8 if SUB >= 5 else 0):
                py = bank(4 + blk % 2)
                for j in range(NTAP):
                    if rowmode:
                        r0 = blk * 8
                        src = gin[:, r0 * 80 + j:r0 * 80 + j + 640].rearrange("p (r c) -> p r c", c=80)[:, :, 0:64]
                        dstp = py.rearrange("p (r c) -> p r c", c=64)
                    else:
                        src = gin[:, (blk * 8 + j) * 64:(blk * 8 + j) * 64 + 512]
                        dstp = py
                    mm(dstp, dg[:, j, :], src, start=(j == 0), stop=(j == NTAP - 1))
                if VAR == 0:
                    act(br[:, 4 + s, blk * 512:(blk + 1) * 512], py, AF.Identity, bias=convb[:, s:s + 1])
                elif VAR == 1:
                    ts("dve", br[:, 4 + s, blk * 512:(blk + 1) * 512], py, convb[:, s:s + 1], None, ALU.add)
                elif VAR == 2:
                    act(br[:, 4 + s, blk * 512:(blk + 1) * 512], py, AF.Copy)

    if MAXPHASE >= 3:
        o = cB0
        sqb = []
        for i in range(4):
            sqb.append(A(o, 1024, BF16)); o += 1024
        mean = A(o, 2048, F32); o += 2048
        msq = A(o, 2048, F32); o += 2048
        rstdc = A(o, 2048, F32); o += 2048
        mhalf = A(o, 2048, F32); o += 2048
        tbuf = []
        for i in range(2):
            tbuf.append(A(o, 2048, F32)); o += 2048
        sgb = []
        for i in range(2):
            sgb.append(A(o, 2048, F32)); o += 2048
        assert o <= ARENA_BYTES, o
        memset("pool", mhalf, -0.5)
        for s in range(4):
            load_w(wgb[s], wst[s % 2], 3584 + s * 128)
        for blk in range(8):
            sl = slice(blk * 512, (blk + 1) * 512)
            p1, p2 = bank(0 + (blk % 2) * 4), bank(1 + (blk % 2) * 4)
            for s in range(4):
                act(sqb[s], br[:, 4 + s, sl], AF.Square)
            for s in range(4):
                mm(p1, ones_bf, br[:, 4 + s, sl], start=(s == 0), stop=(s == 3))
            for s in range(4):
                mm(p2, ones_bf, sqb[s], start=(s == 0), stop=(s == 3))
            act(mean, p1, AF.Copy, scale=1.0 / 512)
            tt("dve", msq, mean, mean, ALU.mult)
            ts("dve", rstdc, p2, 1.0 / 512, EPS, ALU.mult, ALU.add)
            tt("dve", rstdc, rstdc, msq, ALU.subtract)
            act(rstdc, rstdc, AF.Ln)
            act(rstdc, rstdc, AF.Exp, scale=-0.5)
            for s in range(4):
                tb = tbuf[s % 2]
                tt("dve", tb, br[:, 4 + s, sl], mean, ALU.subtract)
                tt("dve", tb, tb, rstdc, ALU.mult)
                act(tb, tb, AF.Silu, scale=lng[:, s:s + 1], bias=lnb[:, s:s + 1])
                pgb = bank(2 + s % 2)
                for k in range(8):
                    mm(pgb, wgb[s][:, k, :], aT[:, k, sl], start=(k == 0), stop=(k == 7))
                sg = sgb[s % 2]
                act(sg, pgb, AF.Silu)
                tt("dve", br[:, 4 + s, sl], tb, sg, ALU.mult)

    if MAXPHASE >= 4:
        o = PH0
        wst3 = A(o, 4096, F32).rearrange("p (k n) -> p k n", k=8); o += 4096
        wAs, wBs = [], []
        for i in range(2):
            wAs.append(A(o, 2048, BF16).rearrange("p (k n) -> p k n", k=8)); o += 2048
            wBs.append(A(o, 2048, BF16).rearrange("p (k n) -> p k n", k=8)); o += 2048
        wC = A(o, 2048, BF16).rearrange("p (k n) -> p k n", k=8); o += 2048
        vh = A(o, 34 * 256, BF16).rearrange("p (c e) -> p c e", e=128); o += 34 * 256
        qbf = A(o, 8192, BF16); o += 8192
        e2b, l1b, l2b, bb, bxb = [], [], [], [], []
        for lst in (e2b, l1b, l2b, bb):
            for i in range(2):
                lst.append(A(o, 2048, F32)); o += 2048
        bxb.append(A(o, 2048, F32)); o += 2048
        bxb.append(bxb[0])
        qtb, ktb = [], []
        for lst in (qtb, ktb):
            for i in range(2):
                lst.append(A(o, 1024, BF16)); o += 1024
        kTall = [A(o, 1024, BF16)]; o += 1024
        kTall.append(kTall[0])
        osb1 = A(o, 2048, F32); o += 2048
        etmp = A(o, 2048, F32); o += 2048
        PFall = A(o, 1024, BF16); o += 1024
        PBall = A(o, 1024, BF16); o += 1024
        Zb = []
        for i in range(2):
            Zb.append(A(o, 256, BF16)); o += 256
        Yst = A(o, 512, F32); o += 512
        sc3 = A(o, 512, F32); o += 512
        sq3 = A(o, 1024, BF16); o += 1024
        Uall = A(o, 2048, F32); o += 2048
        assert o <= ARENA_BYTES, o

        memset("pool", PFall, 0.0)
        memset("pool", PBall, 0.0)

        blkctr = [0]
        chctr = [0]

        items = []
        sweep_no = [0]

        def make_vproj(h):
            def vproj():
                for c in range(34):
                    src = aTc[:, :, c * 128:(c + 1) * 128] if c < 2 else aT[:, :, (c - 2) * 128:(c - 1) * 128]
                    pv = bank(4 + c % 2)[:, 0:128]
                    for k in range(8):
                        mm(pv, src[:, k, :], wC[:, k, :], start=(k == 0), stop=(k == 7))
                    if c % 2 == 0:
                        act(vh[:, c, :], pv, AF.Copy)
                    else:
                        cp("dve", vh[:, c, :], pv)
                if h + 1 < NH:
                    load_w(wC, wst3, 1536 + (h + 1) * 128)
            return vproj

        NH = 4 if S3 > 10 else 1
        load_w(wC, wst3, 1536)
        for h in range(NH):
            head_init = make_vproj(h)

            for dirn in range(2):
                fwd = dirn == 0
                lb_ap = lbv[:, dirn * 4 + h:dirn * 4 + h + 1]
                l1m_ap = l1mlb[:, dirn * 4 + h:dirn * 4 + h + 1]
                sw = sweep_no[0]
                sweep_no[0] += 1
                wA, wB = wAs[sw % 2], wBs[sw % 2]

                def wload(wA=wA, wB=wB, fwd=fwd, h=h):
                    load_w(wA, wst3, (512 if fwd else 1024) + h * 128)
                    load_w(wB, wst3, (0 if fwd else 2048) + h * 128)

                def sweep_init():
                    memset("pool", Yst, 0.0)
                blocks = [("ctx", 0)] + [("lat", j) for j in (range(8) if fwd else range(7, -1, -1))]
                msk = maskF if fwd else maskB

                def stage1(kind, j, fwd=fwd, lb_ap=lb_ap, l1m_ap=l1m_ap, wA=wA, wB=wB):
                    bi = blkctr[0]
                    blkctr[0] += 1
                    par = bi % 2
                    lat = kind == "lat"
                    nt = 512 if lat else 256
                    nch = nt // 128
                    src = aT[:, :, j * 512:(j + 1) * 512] if lat else aTc
                    tsl = slice(j * 512, (j + 1) * 512)
                    pz = bank(0 + par * 2)[:, 0:nt]
                    pq = bank(1 + par * 2)[:, 0:nt]
                    for k in range(8):
                        mm(pz, wA[:, k, :], src[:, k, :], start=(k == 0), stop=(k == 7))
                    if lat and fwd:
                        for k in range(8):
                            mm(pq, wB[:, k, :], src[:, k, :], start=(k == 0), stop=(k == 7))
                    e2, l1, l2, bq, bx = (e2b[par][:, 0:nt], l1b[par][:, 0:nt], l2b[par][:, 0:nt],
                                          bb[par][:, 0:nt], bxb[par][:, 0:nt])
                    qt, kt = qtb[par][:, 0:nt], ktb[par][:, 0:nt]
                    act(e2, pz, AF.Exp)
                    act(l1, e2, AF.Ln, bias=lb_ap)
                    act(l2, e2, AF.Ln, bias=1.0)
                    tt("dve", l1, l1, l2, ALU.subtract)
                    S.op("dve", lambda e, bq=bq, l1=l1, nt=nt: e.tensor_tensor_scan(
                        bq, rmask[:, 0:nt], l1, 0.0, ALU.mult, ALU.add), reads=[rmask[:, 0:nt], l1], writes=[bq],
                        dur=0.1 + nt / 480.0)
                    bq3 = bq.rearrange("p (c t) -> p c t", t=128)
                    m4b = bq3[:, :, 63:64].to_broadcast([128, nch, 128])
                    bx3 = bx.rearrange("p (c t) -> p c t", t=128)
                    if fwd:
                        tt("dve", bx3, bq3, m4b, ALU.subtract)
                        tt("dve", l2, l2, bx, ALU.add)
                    else:
                        tt("dve", bx, bq, l1, ALU.subtract)
                        tt("dve", bx3, bx3, m4b, ALU.subtract)
                        tt("dve", l2, bx, l2, ALU.subtract)
                    base = 8 + par * 40
                    m4 = bq3[:, :, 63]
                    tot4 = bq3[:, :, 127]
                    d4 = sc3[:, base:base + nch]
                    cA4 = sc3[:, base + 12:base + 12 + nch]
                    cB4 = sc3[:, base + 16:base + 16 + nch]
                    tt("dve", d4, tot4, m4, ALU.subtract)
                    cAB4 = sc3[:, base + 20:base + 20 + nch]
                    if fwd:
                        act(cA4, m4, AF.Exp)
                        act(cB4, d4, AF.Exp)
                    else:
                        act(cA4, d4, AF.Exp)
                        act(cB4, m4, AF.Exp)
                    act(cAB4, tot4, AF.Exp)
                    if lat:
                        act(e2, bx, AF.Exp, scale=(1.0 if fwd else -1.0))
                    act(kt, l2, AF.Exp, scale=(-1.0 if fwd else 1.0), bias=l1m_ap)
                    if lat:
                        if fwd:
                            tt("dve", qt, pq, e2, ALU.mult)
                            act(qbf[:, tsl], pq, AF.Copy)
                        else:
                            tt("dve", qt, qbf[:, tsl], e2, ALU.mult)
                    return dict(par=par, lat=lat, nt=nt, nch=nch, src=src, tsl=tsl, pz=pz, pq=pq, qt=qt, kt=kt,
                                cA4=cA4, cB4=cB4, cAB4=cAB4, j=j)

                def stage1b(st):
                    par, nch, kt = st["par"], st["nch"], st["kt"]
                    pzb = bank(0 + par * 2).bitcast(BF16)
                    for c in range(nch):
                        tr(pzb[:, c * 128:(c + 1) * 128], kt[:, c * 128:(c + 1) * 128], ident_bf)
                    act(kTall[par][:, 0:nch * 128], pzb[:, 0:nch * 128], AF.Copy)

                def stage2(st, fwd=fwd, msk=msk, h=h, wB=wB):
                    par, lat, nch, tsl, qt, kt = st["par"], st["lat"], st["nch"], st["tsl"], st["qt"], st["kt"]
                    cA4, cB4, j, pq, src = st["cA4"], st["cB4"], st["j"], st["pq"], st["src"]
                    cAB4 = st["cAB4"]
                    po = bank(4 + par)
                    bD, bS = bank(6), bank(7)
                    Pall = PFall if fwd else PBall
                    kTa = kTall[par]
                    corder = list(range(nch)) if fwd else list(range(nch - 1, -1, -1))
                    for c in corder:
                        gch = (j * 4 + c + 2) if lat else c
                        cs = slice(c * 128, (c + 1) * 128)
                        mm(bD[:, cs], kTa[:, cs], vh[:, gch, :])
                    for c in corder:
                        cs = slice(c * 128, (c + 1) * 128)
                        ts("dve", Uall[:, cs], bD[:, cs], cB4[:, c:c + 1], None, ALU.mult)
                    if lat:
                        if fwd:
                            pieces = [(slice(0, 128), slice(64, 128)), (slice(0, 64), slice(0, 64))]
                        else:
                            pieces = [(slice(0, 128), slice(0, 64)), (slice(64, 128), slice(64, 128))]
                        for c in corder:
                            for (sp, tp) in pieces:
                                tq = slice(c * 128 + tp.start, c * 128 + tp.stop)
                                mm(bS[sp, tq], kt[:, c * 128 + sp.start:c * 128 + sp.stop], qt[:, tq])
                        for (sp, tp) in pieces:
                            w = tp.stop - tp.start
                            np_ = sp.stop - sp.start
                            P3 = Pall[sp, :].rearrange("p (c t) -> p c t", t=128)[:, :, tp]
                            S3v = bS[sp, :].rearrange("p (c t) -> p c t", t=128)[:, :, tp]
                            M3 = msk[sp, tp].unsqueeze(1).to_broadcast([np_, nch, w])
                            tt("dve", P3, S3v, M3, ALU.mult)
                    for c in corder:
                        ci = chctr[0]
                        chctr[0] += 1
                        cpar = ci % 2
                        cs = slice(c * 128, (c + 1) * 128)
                        gch = (j * 4 + c + 2) if lat else c
                        if lat:
                            ts("pool", Zb[cpar], Yst, cA4[:, c:c + 1], 1.0, ALU.mult, ALU.mult)
                            mm(po[:, cs], vh[:, gch, :], Pall[:, cs], start=True, stop=False)
                            mm(po[:, cs], Zb[cpar], qt[:, cs], start=False, stop=True)
                        stt(Yst, Yst, cAB4[:, c:c + 1], Uall[:, cs], ALU.mult, ALU.add)
                    if lat and fwd:
                        act(br[:, h, tsl], po, AF.Copy)
                    if lat and not fwd and S3 == 7:
                        act(br[:, h, tsl], po, AF.Copy)
                    elif lat and not fwd:
                        osb = osb1
                        tt("dve", osb, po, br[:, h, tsl], ALU.add)
                        act(sq3, osb, AF.Square)
                        pn = bank(0 + par * 2)
                        mm(pn, ones_bf, sq3)
                        for k in range(8):
                            mm(pq, wB[:, k, :], src[:, k, :], start=(k == 0), stop=(k == 7))
                        act(etmp, pn, AF.Ln, scale=1.0 / 128, bias=EPS)
                        act(etmp, etmp, AF.Exp, scale=-0.5)
                        tt("pool", osb, osb, etmp, ALU.mult)
                        act(etmp, pq, AF.Exp, scale=-1.0)
                        act(etmp, etmp, AF.Ln, bias=1.0)
                        act(etmp, etmp, AF.Exp, scale=-1.0)
                        tt("dve", etmp, pq, etmp, ALU.mult)
                        sg3 = etmp
                        tt("pool", br[:, h, tsl], osb, sg3, ALU.mult)

                for bi_, blk in enumerate(blocks):
                    items.append(dict(s1=stage1, s1b=stage1b, s2=stage2, blk=blk, first_sweep=(bi_ == 0),
                                      first_head=(bi_ == 0 and dirn == 0), wload=wload, sweep_init=sweep_init,
                                      head_init=head_init))

        sweeps_first = [it for it in items if it["first_sweep"]]
        sweeps_first[0]["wload"]()
        nxt_sweep = 1
        prev, prev_st = None, None
        for it in items:
            cur_st = it["s1"](*it["blk"])
            if prev is not None:
                if prev["first_head"]:
                    prev["head_init"]()
                if prev["first_sweep"]:
                    prev["sweep_init"]()
                prev["s2"](prev_st)
            if it["first_sweep"] and nxt_sweep < len(sweeps_first):
                sweeps_first[nxt_sweep]["wload"]()
                nxt_sweep += 1
            it["s1b"](cur_st)
            prev, prev_st = it, cur_st
        if prev["first_head"]:
            prev["head_init"]()
        if prev["first_sweep"]:
            prev["sweep_init"]()
        prev["s2"](prev_st)

    if MAXPHASE >= 5:
        o = PH0
        wo = A(o, 16384, BF16).rearrange("p (k n) -> p k n", k=8); o += 16384
        wos = A(o, 4096, F32); o += 4096
        xr = []
        for i in range(3):
            xr.append(A(o, 4096, F32)); o += 4096
        ht = []
        for i in range(2):
            ht.append(A(o, 4096, F32)); o += 4096
        junk4 = A(o, 2048, BF16); o += 2048
        fs = A(o, 512, F32); o += 512
        fg_rep = A(o, 4096, F32); o += 4096
        dma(fg_rep, fg_d)
        assert o <= ARENA_BYTES, o
        hgn = vecs[:, 40:44]
        wout_v = wout_d.rearrange("(k p) n -> p k n", p=128)
        for k in range(8):
            dma(wos, wout_v[:, k, :])
            if k < 4:
                stt(wo[:, k, :], wos, hgn[:, k:k + 1], gt_rep, ALU.mult, ALU.mult)
            else:
                tt("dve", wo[:, k, :], wos, gt_rep, ALU.mult)
        for t in range(2 if P5 >= 2 else 0):
            dma(xr[t % 3], x_d[t * 128:(t + 1) * 128, :])
        for t in range(32 if P5 >= 2 else 0):
            rows = slice(t * 128, (t + 1) * 128)
            xt = xr[t % 3]
            if t + 2 < 32:
                dma(xr[(t + 2) % 3], x_d[(t + 2) * 128:(t + 3) * 128, :])
            hb = ht[t % 2]
            for half in range(2):
                pb = bank((t % 2) * 2 + half)
                for g in range(8):
                    mm(pb, br[:, g, rows], wo[:, g, half * 512:(half + 1) * 512], start=(g == 0), stop=(g == 7))
                tt("dve", hb[:, half * 512:(half + 1) * 512], pb, xt[:, half * 512:(half + 1) * 512], ALU.add)
            ss = fs[:, (t % 16) * 2:(t % 16) * 2 + 1]
            rs = fs[:, (t % 16) * 2 + 1:(t % 16) * 2 + 2]
            if P5 >= 3:
                act(junk4, hb, AF.Square, accum_out=ss)
                act(rs, ss, AF.Ln, scale=1.0 / DM, bias=EPS)
                act(rs, rs, AF.Exp, scale=-0.5)
                stt(hb, hb, rs, fg_rep, ALU.mult, ALU.mult)
            if P5 >= 4:
                dma(out_d[rows, :], hb)

    if DEBUG:
        if MAXPHASE >= 1:
            dma(dbg_d[:, 0:8 * 4096], A(AT0, 65536, BF16))
        if MAXPHASE >= 2 and SUB > 10:
            dma(dbg_d[:, 12 * 4096:16 * 4096], A(BR0 + 32768, 32768, BF16))
        if MAXPHASE >= 4 and S3 > 10:
            dma(dbg_d[:, 8 * 4096:12 * 4096], A(BR0, 32768, BF16))
        if MAXPHASE >= 4 and S3 in (5, 7):
            dma(dbg_d[:, 8 * 4096:9 * 4096], A(BR0, 8192, BF16))

    if REORDER:
        S.reorder()
    S.emit()
    es.close()
    return nc


def _fm(v, k):
    return np.ascontiguousarray(np.asarray(v, np.float32).reshape(k, 128).T)


_NC_CACHE = {}


def kernel(x, c, ctx, c_ctx, norm_g, w_mod, b_mod, w_in, lb_logits, hgrn_norm_g,
           conv_w, conv_b, conv_ln_g, conv_ln_b, w_out, final_norm_g):
    f32 = np.float32
    x = np.asarray(x, f32)
    ctx = np.asarray(ctx, f32)
    c = np.asarray(c, f32)
    c_ctx = np.asarray(c_ctx, f32)
    w_mod0 = np.ascontiguousarray(np.asarray(w_mod, f32)[0])
    w_in0 = np.ascontiguousarray(np.asarray(w_in, f32)[0])
    w_out0 = np.ascontiguousarray(np.asarray(w_out, f32)[0])
    b_mod2 = np.ascontiguousarray(np.tile(np.asarray(b_mod, f32)[0][None, :], (2, 1)))
    fg_rep = np.ascontiguousarray(np.tile(np.asarray(final_norm_g, f32)[None, :], (128, 1)))
    ident = np.eye(128, dtype=f32)
    s_idx = np.arange(128)[:, None]
    t_idx = np.arange(128)[None, :]
    masks = np.concatenate([(s_idx <= t_idx), (s_idx >= t_idx)], axis=1).astype(np.float32)
    sel = np.zeros((2, 256), f32)
    sel[0, 0:128] = 1.0
    sel[1, 128:256] = 1.0
    rmask = np.ones((128, 512), f32)
    rmask[:, 0::128] = 0.0

    lbl = np.asarray(lb_logits, f32)
    lbl_fm = lbl.reshape(2, 2, 4, 128).transpose(3, 0, 1, 2).reshape(128, 16)
    cwv = np.asarray(conv_w, f32)[0]
    cw_fm = cwv.reshape(NTAP, 4, 128).transpose(2, 1, 0).reshape(128, 4 * NTAP)

    in_maps = []
    for b in range(N_CORES):
        vecs = np.zeros((128, 192), f32)
        cvv = np.stack([_fm(c[b], 8), _fm(c_ctx, 8)], axis=-1)
        vecs[:, 0:16] = cvv.reshape(128, 16)
        vecs[:, 16:24] = _fm(np.asarray(norm_g, f32)[0], 8)
        vecs[:, 24:40] = lbl_fm
        vecs[:, 40:44] = _fm(np.asarray(hgrn_norm_g, f32)[0], 4)
        vecs[:, 44:168] = cw_fm
        vecs[:, 168:172] = _fm(np.asarray(conv_b, f32)[0], 4)
        vecs[:, 172:176] = _fm(np.asarray(conv_ln_g, f32)[0], 4)
        vecs[:, 176:180] = _fm(np.asarray(conv_ln_b, f32)[0], 4)
        in_maps.append({
            "x": np.ascontiguousarray(x[b]),
            "ctx": np.ascontiguousarray(ctx[b]),
            "w_mod": w_mod0, "b_mod2": b_mod2, "w_in": w_in0, "w_out": w_out0,
            "vecs": vecs, "ident": ident, "masks": masks, "sel": sel,
            "fg_rep": fg_rep, "rmask": rmask,
        })
    if "nc" not in _NC_CACHE:
        _NC_CACHE["nc"] = build_program()
    nc = _NC_CACHE["nc"]
    res = run_bass_kernel_spmd(nc, in_maps, core_ids=list(range(N_CORES)))
    out = np.stack([np.asarray(r["out"], f32) for r in res.results], axis=0)
    if DEBUG:
        kernel.dbg = [np.asarray(r["dbg"]) for r in res.results]
    return out
```

```python
import numpy as np
from contextlib import ExitStack

import concourse.bass as bass
import concourse.mybir as mybir
from concourse.bass_utils import run_bass_kernel_spmd

F32 = mybir.dt.float32
BF16 = mybir.dt.bfloat16
U32 = mybir.dt.uint32
AF = mybir.ActivationFunctionType
ALU = mybir.AluOpType
AX = mybir.AxisListType

N_CORES = 8
SEQ = 4096
DM = 1024
CTX = 256
D_IN = 4096
EPS = 1e-6
NTAP = 31
ARENA_BYTES = 207 * 1024

DEBUG = False
MAXPHASE = 99
SUB = 99
VAR = 1
LV = 99
LQ = 31
P5 = 99
S3 = 99
REORDER = True
REORDER_PE = True
SLACK = 0.25


def _isz(dt):
    return mybir.dt.size(dt)


class _Reg:
    __slots__ = ("name", "p0", "p1", "ivs", "lo", "hi")

    def __init__(self, name, p0, p1, ivs):
        self.name, self.p0, self.p1, self.ivs = name, p0, p1, ivs
        self.lo = ivs[0][0]
        self.hi = ivs[-1][1]


def _region(ap):
    t = ap.tensor
    if type(t).__name__.startswith("DRam"):
        return None
    isz = _isz(ap.dtype)
    dims = [tuple(d) for d in ap.ap]
    row = int(t.shape[-1]) * _isz(t.dtype) if len(t.shape) == 2 else None
    if row is None:
        n = 1
        for s in t.shape[1:]:
            n *= int(s)
        row = n * _isz(t.dtype)
    off = int(ap.offset) * isz
    p0 = off // row
    fo = off % row
    pn = dims[0][1]
    free = [(s * isz, c) for (s, c) in dims[1:] if c > 1]
    span = isz
    for s, c in free:
        span += abs(s) * (c - 1)
    ivs = [(fo, fo + span)]
    if len(free) == 2:
        (s0, c0), (s1, c1) = free
        if s1 == isz and c0 <= 64 and c1 * isz < s0:
            ivs = [(fo + i * s0, fo + i * s0 + c1 * isz) for i in range(c0)]
    elif len(free) == 3:
        (s0, c0), (s1, c1), (s2, c2) = free
        if s2 == isz and c0 * c1 <= 64 and c2 * isz < s1 and s1 * c1 <= s0:
            ivs = [(fo + i * s0 + j * s1, fo + i * s0 + j * s1 + c2 * isz)
                   for i in range(c0) for j in range(c1)]
    return _Reg(t.name, p0, p0 + pn, ivs)


def _overlap(a, b):
    if a.p1 <= b.p0 or b.p1 <= a.p0 or a.hi <= b.lo or b.hi <= a.lo:
        return False
    if len(a.ivs) == 1 and len(b.ivs) == 1:
        return True
    for (l0, h0) in a.ivs:
        for (l1, h1) in b.ivs:
            if l0 < h1 and l1 < h0:
                return True
    return False


def _covers(w, r):
    if w.p0 > r.p0 or w.p1 < r.p1:
        return False
    for (l, h) in r.ivs:
        ok = False
        for (wl, wh) in w.ivs:
            if wl <= l and h <= wh:
                ok = True
                break
        if not ok:
            return False
    return True


class _Op:
    __slots__ = ("eng", "fn", "deps", "raw", "signal", "count", "is_dma", "semi", "target", "idx", "dur", "sdeps")


class Sched:
    NDMA = 8

    def __init__(self, nc, es):
        self.nc = nc
        self.h = {"pe": nc.tensor, "act": nc.scalar, "dve": nc.vector, "pool": nc.gpsimd, "sp": nc.sync}
        self.sem = {e: es.enter_context(nc.semaphore("s_" + e)) for e in ("pe", "act", "dve", "pool")}
        self.dsem = [es.enter_context(nc.semaphore("s_dma%d" % i)) for i in range(self.NDMA)]
        self.ops = []
        self.rec = {}
        self.ndma = 0
        self.last_dma = None
        self.last_on = {}

    def op(self, eng, fn, reads=(), writes=(), dma=False, dur=0.3):
        o = _Op()
        o.dur = dur
        o.eng, o.fn, o.is_dma = eng, fn, dma
        o.signal, o.count, o.idx = False, 0, len(self.ops)
        deps = {}

        def add(d, raw):
            k = d.idx
            if k in deps:
                deps[k] = (d, deps[k][1] or raw)
            else:
                deps[k] = (d, raw)

        rregs = [r for r in (_region(a) for a in reads if a is not None) if r is not None]
        wregs = [r for r in (_region(a) for a in writes if a is not None) if r is not None]
        def _bankify(w):
            if w.name != "ps":
                return w
            return _Reg(w.name, 0, 128, [((w.lo >> 11) << 11, (((w.hi - 1) >> 11) << 11) + 2048)])
        wregs = [_bankify(w) for w in wregs]
        rregs = [_bankify(r) for r in rregs]
        def keys(r):
            return [(r.name, b) for b in range(r.lo >> 13, ((r.hi - 1) >> 13) + 1)]

        for r in rregs:
            for key in keys(r):
                for (reg, d, isw) in self.rec.get(key, ()):
                    if (isw or (r.name == "ps" and d.eng != eng)) and _overlap(reg, r):
                        add(d, True)
        for w in wregs:
            for key in keys(w):
                for (reg, d, isw) in self.rec.get(key, ()):
                    if _overlap(reg, w):
                        add(d, False)
        o.deps = []
        o.sdeps = [d for (d, raw) in deps.values()]
        if eng in (("sp",) if REORDER_PE else ("pe", "sp")) and self.last_on.get(eng) is not None:
            o.sdeps.append(self.last_on[eng])
        self.last_on[eng] = o
        for (d, raw) in deps.values():
            if d.is_dma:
                o.deps.append(d)
                continue
            if d.eng == eng and eng == "pe":
                continue
            o.deps.append(d)
            d.signal = True
        for w in wregs:
            for key in keys(w):
                lst = self.rec.setdefault(key, [])
                lst[:] = [x for x in lst if not _covers(w, x[0])]
                lst.append((w, o, True))
        for r in rregs:
            for key in keys(r):
                lst = self.rec.setdefault(key, [])
                if eng == "pe" and not REORDER_PE:
                    lst[:] = [x for x in lst if not ((not x[2]) and x[1].eng == eng and _covers(r, x[0]))]
                lst.append((r, o, False))
        if dma:
            o.semi = self.ndma % self.NDMA
            o.target = 16 * (self.ndma // self.NDMA + 1)
            self.ndma += 1
            self.last_dma = o
        self.ops.append(o)
        return o

    def reorder(self, window=600):
        ops = self.ops
        n = len(ops)
        succ = [[] for _ in range(n)]
        indeg = [0] * n
        for o in ops:
            seen = set()
            for d in o.sdeps:
                if d.idx in seen:
                    continue
                seen.add(d.idx)
                succ[d.idx].append(o.idx)
                indeg[o.idx] += 1
        bl = [0.0] * n
        for i in range(n - 1, -1, -1):
            m = 0.0
            for j in succ[i]:
                if bl[j] > m:
                    m = bl[j]
            bl[i] = ops[i].dur + m
        fin = [0.0] * n
        efree = {e: 0.0 for e in self.h}
        ready = [i for i in range(n) if indeg[i] == 0]
        rstart = {i: 0.0 for i in ready}
        done = [False] * n
        lo = 0
        order = []
        while ready:
            while lo < n and done[lo]:
                lo += 1
            best, bkey = None, None
            cands = []
            for i in ready:
                if i > lo + window:
                    continue
                st = max(efree[ops[i].eng], rstart[i])
                cands.append((st, i))
                key = (st, i)
                if bkey is None or key < bkey:
                    best, bkey = i, key
            if best is not None and SLACK > 0:
                lim = bkey[0] + SLACK
                bb = None
                for (st, i) in cands:
                    if st <= lim:
                        k2 = (-bl[i], st, i)
                        if bb is None or k2 < bb:
                            bb = k2
                best = bb[2]
                bkey = (bb[1], best)
            if best is None:
                best = min(ready)
                bkey = (max(efree[ops[best].eng], rstart[best]), best)
            o = ops[best]
            ready.remove(best)
            done[best] = True
            order.append(o)
            f = bkey[0] + o.dur
            if o.is_dma:
                efree[o.eng] = bkey[0] + 0.1
            else:
                efree[o.eng] = f
            fin[best] = f
            for j in succ[best]:
                indeg[j] -= 1
                lat = 0.1 if ops[j].eng == o.eng else 0.8
                rstart[j] = max(rstart.get(j, 0.0), f + lat)
                if indeg[j] == 0:
                    ready.append(j)
        assert len(order) == n, (len(order), n)
        self.ops = order
        self.est_total = max(fin) if fin else 0.0

    def emit(self):
        cnt = {e: 0 for e in self.sem}
        for o in self.ops:
            if (not o.is_dma) and o.signal:
                cnt[o.eng] += 1
                o.count = cnt[o.eng]
        waited = {}
        last_dma_on = {}
        for o in self.ops:
            eng = self.h[o.eng]
            waits = {}
            for d in o.deps:
                if d.is_dma:
                    key = ("d", d.semi)
                    waits[key] = max(waits.get(key, 0), d.target)
                else:
                    key = ("e", d.eng)
                    waits[key] = max(waits.get(key, 0), d.count)
            if o.is_dma and o.target > 16:
                key = ("d", o.semi)
                waits[key] = max(waits.get(key, 0), o.target - 16)
            for key, val in waits.items():
                wk = (o.eng, key)
                if waited.get(wk, 0) >= val:
                    continue
                waited[wk] = val
                s = self.dsem[key[1]] if key[0] == "d" else self.sem[key[1]]
                eng.wait_ge(s, val)
            ins = o.fn(eng)
            if o.is_dma:
                ins.then_inc(self.dsem[o.semi], 16)
                last_dma_on[o.semi] = o.target
            elif o.signal:
                ins.then_inc(self.sem[o.eng], 1)
        for semi, tgt in last_dma_on.items():
            self.h["sp"].wait_ge(self.dsem[semi], tgt)
        for e, c in cnt.items():
            if c > 0:
                self.h["sp"].wait_ge(self.sem[e], c)


def build_program():
    nc = bass.Bass("TRN2", target_bir_lowering=False)
    dr = {}

    def din(name, shape, dt=F32):
        dr[name] = nc.dram_tensor(name, list(shape), dt, kind="ExternalInput").ap()
        return dr[name]

    x_d = din("x", [SEQ, DM])
    ctx_d = din("ctx", [CTX, DM])
    wmod_d = din("w_mod", [DM, 3 * DM])
    bmod_d = din("b_mod2", [2, 3 * DM])
    win_d = din("w_in", [DM, D_IN])
    wout_d = din("w_out", [DM, DM])
    vecs_d = din("vecs", [128, 192])
    ident_d = din("ident", [128, 128])
    masks_d = din("masks", [128, 256], F32)
    sel_d = din("sel", [2, 256])
    fg_d = din("fg_rep", [128, DM])
    rmask_d = din("rmask", [128, 512])
    out_d = nc.dram_tensor("out", [SEQ, DM], F32, kind="ExternalOutput").ap()
    if DEBUG:
        dbg_d = nc.dram_tensor("dbg", [128, 16 * 4096], BF16, kind="ExternalOutput").ap()

    es = ExitStack()
    arena = es.enter_context(nc.sbuf_tensor("arena", [128, ARENA_BYTES // 4], F32))
    psum = es.enter_context(nc.psum_tensor("ps", [128, 4096], F32))
    S = Sched(nc, es)

    def A(off, nbytes, dt, parts=(0, 128)):
        assert off % 4 == 0 and nbytes % 4 == 0 and off + nbytes <= ARENA_BYTES, (off, nbytes)
        v = arena[parts[0]:parts[1], off // 4:(off + nbytes) // 4]
        if dt != F32:
            v = v.bitcast(dt)
        return v

    def bank(i):
        return psum[:, i * 512:(i + 1) * 512]

    def _n(ap):
        n = 1
        for d in list(ap.shape)[1:]:
            n *= int(d)
        return n

    def dma(out, in_):
        nb = _n(out) * _isz(out.dtype) * int(out.shape[0])
        return S.op("sp", lambda e: e.dma_start(out=out, in_=in_), reads=[in_], writes=[out], dma=True,
                    dur=2.0 + nb / 150e3)

    def mm(out, lhsT, rhs, start=True, stop=True):
        return S.op("pe", lambda e: e.matmul(out, lhsT, rhs, start=start, stop=stop),
                    reads=[lhsT, rhs], writes=[out],
                    dur=(0.06 + max(_n(rhs), 64) * 0.0005) * (4 if rhs.dtype == F32 else 1))

    def tr(out, in_, ident):
        return S.op("pe", lambda e: e.transpose(out, in_, ident), reads=[in_, ident], writes=[out], dur=0.1)

    def act(out, in_, func, bias=None, scale=None, accum_out=None):
        kw = {}
        if (bias is not None and not isinstance(bias, (int, float))
                and (scale is None or (isinstance(scale, (int, float)) and float(scale) == 1.0))
                and func != AF.Ln):
            scale = onecol
        if bias is not None:
            kw["bias"] = bias
        if scale is not None:
            kw["scale"] = scale
        if accum_out is not None:
            kw["accum_out"] = accum_out
        rd = [in_] + [a for a in (bias, scale) if a is not None and not isinstance(a, (int, float))]
        return S.op("act", lambda e: e.activation(out, in_, func, **kw), reads=rd,
                    writes=[out] + ([accum_out] if accum_out is not None else []), dur=0.22 + _n(in_) / 1400.0)

    def ts(eng, out, in0, s1, s2, op0, op1=None):
        rd = [in0] + [a for a in (s1, s2) if a is not None and not isinstance(a, (int, float))]
        du = (0.07 + _n(in0) / 960.0) if eng == "dve" else (0.3 + _n(in0) / 450.0)
        if op1 is None:
            return S.op(eng, lambda e: e.tensor_scalar(out, in0, s1, None, op0), reads=rd, writes=[out], dur=du)
        return S.op(eng, lambda e: e.tensor_scalar(out, in0, s1, s2, op0, op1), reads=rd, writes=[out], dur=du)

    def tt(eng, out, in0, in1, op):
        du = (0.07 + _n(in0) / 960.0) if eng == "dve" else (0.3 + _n(in0) / 450.0)
        return S.op(eng, lambda e: e.tensor_tensor(out, in0, in1, op), reads=[in0, in1], writes=[out], dur=du)

    def stt(out, in0, scalar, in1, op0, op1):
        rd = [in0, in1] + ([scalar] if not isinstance(scalar, (int, float)) else [])
        return S.op("dve", lambda e: e.scalar_tensor_tensor(out, in0, scalar, in1, op0, op1),
                    reads=rd, writes=[out], dur=0.07 + _n(in0) / 960.0)

    def cp(eng, out, in_):
        du = (0.07 + _n(in_) / 960.0) if eng == "dve" else (0.3 + _n(in_) / 450.0)
        return S.op(eng, lambda e: e.tensor_copy(out, in_), reads=[in_], writes=[out], dur=du)

    def memset(eng, ap, val):
        du = (0.07 + _n(ap) / 960.0) if eng == "dve" else (0.3 + _n(ap) / 450.0)
        return S.op(eng, lambda e: e.memset(ap, val), reads=[], writes=[ap], dur=du)

    AT0 = 0
    BR0 = 65536
    P0 = 131072
    aT = A(AT0, 65536, BF16).rearrange("p (k t) -> p k t", k=8)
    br = A(BR0, 65536, BF16).rearrange("p (k t) -> p k t", k=8)
    o = P0
    ident = A(o, 512, F32); o += 512
    ident_bf = A(o, 256, BF16); o += 256
    ones_bf = A(o, 256, BF16); o += 256
    masks = A(o, 1024, F32); o += 1024
    maskF, maskB = masks[:, 0:128], masks[:, 128:256]
    vecs = A(o, 768, F32); o += 768
    dv = A(o, 1024, F32); o += 1024
    gt_rep = A(o, 4096, F32); o += 4096
    aTc = A(o, 4096, BF16).rearrange("p (k t) -> p k t", k=8); o += 4096
    rmask = A(o, 2048, F32); o += 2048
    small = A(o, 1024, F32); o += 1024
    onecol = small[:, 255:256]
    PH0 = o
    PH_BYTES = ARENA_BYTES - PH0

    cv = vecs[:, 0:16].rearrange("p (k c) -> p k c", c=2)
    ng = vecs[:, 16:24]
    lbl = vecs[:, 24:40]
    cw = vecs[:, 44:168].rearrange("p (s j) -> p s j", j=NTAP)
    convb = vecs[:, 168:172]
    lng = vecs[:, 172:176]
    lnb = vecs[:, 176:180]
    gs_l, sh_l, gs_c, sh_c = dv[:, 0:8], dv[:, 8:16], dv[:, 16:24], dv[:, 24:32]
    lbv, l1mlb = dv[:, 32:40], dv[:, 40:48]
    cwh = dv[:, 48:172].rearrange("p (s j) -> p s j", j=NTAP)
    tmp8 = dv[:, 172:236]

    dma(vecs, vecs_d)
    dma(ident, ident_d)
    dma(masks, masks_d)
    dma(rmask, rmask_d)
    cp("dve", ident_bf, ident)
    memset("pool", ones_bf, 1.0)
    memset("pool", onecol, 1.0)

    o = PH0
    mod_sb = A(o, 12288, F32, parts=(0, 2)); o += 12288
    bmod = A(o, 12288, F32, parts=(0, 2)); o += 12288
    wm = []
    for i in range(2):
        wm.append(A(o, 8192, F32).rearrange("p (k n) -> p k n", k=8)); o += 8192
    xs = []
    for i in range(3):
        xs.append(A(o, 4096, F32)); o += 4096
    junk = A(o, 2048, BF16); o += 2048
    sel = A(o, 1024, F32, parts=(0, 2)); o += 1024
    assert o <= ARENA_BYTES, o
    dma(sel, sel_d)

    dma(bmod, bmod_d)
    act(vecs[:, 0:16], vecs[:, 0:16], AF.Silu)
    wmod_v = wmod_d.rearrange("(k p) n -> p k n", p=128)
    for blk in range(12):
        w = wm[blk % 2]
        dma(w, wmod_v[:, :, blk * 256:(blk + 1) * 256])
        pb = bank(blk % 2)[0:2, 0:256]
        for k in range(8):
            mm(pb, cv[:, k, :], w[:, k, :], start=(k == 0), stop=(k == 7))
        tt("dve", mod_sb[:, blk * 256:(blk + 1) * 256], pb, bmod[:, blk * 256:(blk + 1) * 256], ALU.add)

    def extract(dst8, cb0, selv):
        for half in range(2):
            pb = bank(2 + half)
            mm(pb, selv, mod_sb[:, (cb0 + half) * 512:(cb0 + half + 1) * 512])
            for kk in range(4):
                k = half * 4 + kk
                scr = junk.bitcast(F32)[:, 0:128]
                tt("dve", scr, pb[:, kk * 128:(kk + 1) * 128], ident, ALU.mult)
                S.op("dve", lambda e, scr=scr, d=dst8[:, k:k + 1]: e.tensor_reduce(d, scr, AX.X, ALU.add),
                     reads=[scr], writes=[dst8[:, k:k + 1]])

    sel_l, sel_c = sel[:, 0:128], sel[:, 128:256]
    extract(sh_l, 0, sel_l)
    extract(gs_l, 2, sel_l)
    extract(sh_c, 0, sel_c)
    extract(gs_c, 2, sel_c)
    for half in range(2):
        pb = bank(2 + half)
        mm(pb, sel_l, mod_sb[:, (4 + half) * 512:(5 + half) * 512])
        act(gt_rep[:, half * 512:(half + 1) * 512], pb, AF.Copy)
    for g in (gs_l, gs_c):
        stt(g, g, 1.0, ng, ALU.add, ALU.mult)
    dlt = small[:, 0:8]
    tt("dve", dlt, lbl[:, 0:8], lbl[:, 8:16], ALU.subtract)
    act(dlt, dlt, AF.Exp, scale=-1.0)
    ts("dve", dlt, dlt, 1.0, None, ALU.add)
    S.op("dve", lambda e: e.reciprocal(lbv, dlt), reads=[dlt], writes=[lbv])
    act(l1mlb, lbv, AF.Ln, scale=-1.0, bias=1.0)
    ts("dve", dv[:, 48:172], vecs[:, 44:168], 0.5, None, ALU.mult)

    if MAXPHASE >= 1:
        ssv = small[:, 16:80]
        tile_ctr = [0]

        def norm_tile(src_rows, dst_fn, gs, sh):
            i = tile_ctr[0]
            tile_ctr[0] += 1
            xt = xs[i % 3]
            ss = ssv[:, (i % 16) * 2:(i % 16) * 2 + 1]
            rs = ssv[:, (i % 16) * 2 + 1:(i % 16) * 2 + 2]
            dma(xt, src_rows)
            act(junk, xt, AF.Square, accum_out=ss)
            act(rs, ss, AF.Ln, scale=1.0 / DM, bias=EPS)
            act(rs, rs, AF.Exp, scale=-0.5)
            act(xt, xt, AF.Copy, scale=rs)
            for k in range(8):
                pb = bank(4 + (i % 2) * 2 + k // 4)
                tr(pb[:, (k % 4) * 128:(k % 4 + 1) * 128], xt[:, k * 128:(k + 1) * 128], ident)
            for k in range(8):
                pb = bank(4 + (i % 2) * 2 + k // 4)
                ts("dve", dst_fn(k), pb[:, (k % 4) * 128:(k % 4 + 1) * 128], gs[:, k:k + 1], sh[:, k:k + 1],
                   ALU.mult, ALU.add)

        for t in range(2):
            norm_tile(ctx_d[t * 128:(t + 1) * 128, :], lambda k, t=t: aTc[:, k, t * 128:(t + 1) * 128], gs_c, sh_c)
        for t in range(32):
            norm_tile(x_d[t * 128:(t + 1) * 128, :], lambda k, t=t: aT[:, k, t * 128:(t + 1) * 128], gs_l, sh_l)

    win_v = win_d.rearrange("(k p) n -> p k n", p=128)

    def load_w(dst_bf, stage, col0, ncols=128):
        dma(stage, win_v[:, :, col0:col0 + ncols])
        act(dst_bf, stage, AF.Copy)

    if MAXPHASE >= 2:
        o = PH0
        wst = []
        for i in range(2):
            wst.append(A(o, 4096, F32).rearrange("p (k n) -> p k n", k=8)); o += 4096
        wgb = []
        for i in range(4):
            wgb.append(A(o, 2048, BF16).rearrange("p (k n) -> p k n", k=8)); o += 2048
        cB0 = o
        wu = []
        for i in range(4):
            wu.append(A(o, 2048, BF16).rearrange("p (k n) -> p k n", k=8)); o += 2048
        gin = A(o, 12288, BF16); o += 12288
        diag = []
        for i in range(2):
            diag.append(A(o, NTAP * 256, BF16).rearrange("p (j n) -> p j n", j=NTAP)); o += NTAP * 256
        th = []
        for i in range(2):
            th.append(A(o, 2048, F32)); o += 2048
        assert o <= ARENA_BYTES, o

        for s in (range(4) if SUB > 10 else ([2] if SUB == 6 else [0])):
            rowmode = s < 2
            wU, wG = wu[(s % 2) * 2], wu[(s % 2) * 2 + 1]
            load_w(wU, wst[0], 2560 + s * 128)
            load_w(wG, wst[1], 3072 + s * 128)
            dg = diag[s % 2]
            for j in range(NTAP if SUB >= 2 else 0):
                ts("dve", dg[:, j, :], ident_bf, cwh[:, s, j:j + 1], None, ALU.mult)
            if (s == 0 or s == 2) and SUB >= 3:
                memset("pool", gin, 0.0)
            for blk in range(8 if SUB >= 4 else 0):
                pu, pg = bank(0 + (blk % 2) * 2), bank(1 + (blk % 2) * 2)
                for k in range(8):
                    mm(pu, wU[:, k, :], aT[:, k, blk * 512:(blk + 1) * 512], start=(k == 0), stop=(k == 7))
                for k in range(8):
                    mm(pg, wG[:, k, :], aT[:, k, blk * 512:(blk + 1) * 512], start=(k == 0), stop=(k == 7))
                tb = th[blk % 2]
                act(tb, pg, AF.Tanh, scale=0.5)
                if rowmode:
                    r0 = blk * 8
                    dst = gin[:, r0 * 80 + 15:r0 * 80 + 15 + 640].rearrange("p (r c) -> p r c", c=80)[:, :, 0:64]
                else:
                    dst = gin[:, (blk * 8 + 15) * 64:(blk * 8 + 15) * 64 + 512].rearrange("p (r c) -> p r c", c=64)
                stt(dst, tb.rearrange("p (r c) -> p r c", c=64), 1.0, pu.rearrange("p (r c) -> p r c", c=64),
                    ALU.add, ALU.mult)
            for blk in range(8 if SUB >= 5 else 0):
                py = bank(4 + blk % 2)
                for j in range(NTAP):
                    if rowmode:
                        r0 = blk * 8
                        src = gin[:, r0 * 80 + j:r0 * 80 + j + 640].rearrange("p (r c) -> p r c", c=80)[:, :, 0:64]
                        dstp = py.rearrange("p (r c) -> p r c", c=64)
                    else:
                        src = gin[:, (blk * 8 + j) * 64:(blk * 8 + j) * 64 + 512]
                        dstp = py
                    mm(dstp, dg[:, j, :], src, start=(j == 0), stop=(j == NTAP - 1))
                if VAR == 0:
                    act(br[:, 4 + s, blk * 512:(blk + 1) * 512], py, AF.Identity, bias=convb[:, s:s + 1])
                elif VAR == 1:
                    ts("dve", br[:, 4 + s, blk * 512:(blk + 1) * 512], py, convb[:, s:s + 1], None, ALU.add)
                elif VAR == 2:
                    act(br[:, 4 + s, blk * 512:(blk + 1) * 512], py, AF.Copy)

    if MAXPHASE >= 3:
        o = cB0
        sqb = []
        for i in range(4):
            sqb.append(A(o, 1024, BF16)); o += 1024
        mean = A(o, 2048, F32); o += 2048
        msq = A(o, 2048, F32); o += 2048
        rstdc = A(o, 2048, F32); o += 2048
        mhalf = A(o, 2048, F32); o += 2048
        tbuf = []
        for i in range(2):
            tbuf.append(A(o, 2048, F32)); o += 2048
        sgb = []
        for i in range(2):
            sgb.append(A(o, 2048, F32)); o += 2048
        assert o <= ARENA_BYTES, o
        memset("pool", mhalf, -0.5)
        for s in range(4):
            load_w(wgb[s], wst[s % 2], 3584 + s * 128)
        for blk in range(8):
            sl = slice(blk * 512, (blk + 1) * 512)
            p1, p2 = bank(0 + (blk % 2) * 4), bank(1 + (blk % 2) * 4)
            for s in range(4):
                act(sqb[s], br[:, 4 + s, sl], AF.Square)
            for s in range(4):
                mm(p1, ones_bf, br[:, 4 + s, sl], start=(s == 0), stop=(s == 3))
            for s in range(4):
                mm(p2, ones_bf, sqb[s], start=(s == 0), stop=(s == 3))
            act(mean, p1, AF.Copy, scale=1.0 / 512)
            tt("dve", msq, mean, mean, ALU.mult)
            ts("dve", rstdc, p2, 1.0 / 512, EPS, ALU.mult, ALU.add)
            tt("dve", rstdc, rstdc, msq, ALU.subtract)
            act(rstdc, rstdc, AF.Ln)
            act(rstdc, rstdc, AF.Exp, scale=-0.5)
            for s in range(4):
                tb = tbuf[s % 2]
                tt("dve", tb, br[:, 4 + s, sl], mean, ALU.subtract)
                tt("dve", tb, tb, rstdc, ALU.mult)
                act(tb, tb, AF.Silu, scale=lng[:, s:s + 1], bias=lnb[:, s:s + 1])
                pgb = bank(2 + s % 2)
                for k in range(8):
                    mm(pgb, wgb[s][:, k, :], aT[:, k, sl], start=(k == 0), stop=(k == 7))
                sg = sgb[s % 2]
                act(sg, pgb, AF.Silu)
                tt("dve", br[:, 4 + s, sl], tb, sg, ALU.mult)

    if MAXPHASE >= 4:
        o = PH0
        wst3 = A(o, 4096, F32).rearrange("p (k n) -> p k n", k=8); o += 4096
        wAs, wBs = [], []
        for i in range(2):
            wAs.append(A(o, 2048, BF16).rearrange("p (k n) -> p k n", k=8)); o += 2048
            wBs.append(A(o, 2048, BF16).rearrange("p (k n) -> p k n", k=8)); o += 2048
        wC = A(o, 2048, BF16).rearrange("p (k n) -> p k n", k=8); o += 2048
        vh = A(o, 34 * 256, BF16).rearrange("p (c e) -> p c e", e=128); o += 34 * 256
        qbf = A(o, 8192, BF16); o += 8192
        e2b, l1b, l2b, bb, bxb = [], [], [], [], []
        for lst in (e2b, l1b, l2b, bb):
            for i in range(2):
                lst.append(A(o, 2048, F32)); o += 2048
        bxb.append(A(o, 2048, F32)); o += 2048
        bxb.append(bxb[0])
        qtb, ktb = [], []
        for lst in (qtb, ktb):
            for i in range(2):
                lst.append(A(o, 1024, BF16)); o += 1024
        kTall = [A(o, 1024, BF16)]; o += 1024
        kTall.append(kTall[0])
        osb1 = A(o, 2048, F32); o += 2048
        etmp = A(o, 2048, F32); o += 2048
        PFall = A(o, 1024, BF16); o += 1024
        PBall = A(o, 1024, BF16); o += 1024
        Zb = []
        for i in range(2):
            Zb.append(A(o, 256, BF16)); o += 256
        Yst = A(o, 512, F32); o += 512
        sc3 = A(o, 512, F32); o += 512
        sq3 = A(o, 1024, BF16); o += 1024
        Uall = A(o, 2048, F32); o += 2048
        assert o <= ARENA_BYTES, o

        memset("pool", PFall, 0.0)
        memset("pool", PBall, 0.0)

        blkctr = [0]
        chctr = [0]

        items = []
        sweep_no = [0]

        def make_vproj(h):
            def vproj():
                for c in range(34):
                    src = aTc[:, :, c * 128:(c + 1) * 128] if c < 2 else aT[:, :, (c - 2) * 128:(c - 1) * 128]
                    pv = bank(4 + c % 2)[:, 0:128]
                    for k in range(8):
                        mm(pv, src[:, k, :], wC[:, k, :], start=(k == 0), stop=(k == 7))
                    if c % 2 == 0:
                        act(vh[:, c, :], pv, AF.Copy)
                    else:
                        cp("dve", vh[:, c, :], pv)
                if h + 1 < NH:
                    load_w(wC, wst3, 1536 + (h + 1) * 128)
            return vproj

        NH = 4 if S3 > 10 else 1
        load_w(wC, wst3, 1536)
        for h in range(NH):
            head_init = make_vproj(h)

            for dirn in range(2):
                fwd = dirn == 0
                lb_ap = lbv[:, dirn * 4 + h:dirn * 4 + h + 1]
                l1m_ap = l1mlb[:, dirn * 4 + h:dirn * 4 + h + 1]
                sw = sweep_no[0]
                sweep_no[0] += 1
                wA, wB = wAs[sw % 2], wBs[sw % 2]

                def wload(wA=wA, wB=wB, fwd=fwd, h=h):
                    load_w(wA, wst3, (512 if fwd else 1024) + h * 128)
                    load_w(wB, wst3, (0 if fwd else 2048) + h * 128)

                def sweep_init():
                    memset("pool", Yst, 0.0)
                blocks = [("ctx", 0)] + [("lat", j) for j in (range(8) if fwd else range(7, -1, -1))]
                msk = maskF if fwd else maskB

                def stage1(kind, j, fwd=fwd, lb_ap=lb_ap, l1m_ap=l1m_ap, wA=wA, wB=wB):
                    bi = blkctr[0]
                    blkctr[0] += 1
                    par = bi % 2
                    lat = kind == "lat"
                    nt = 512 if lat else 256
                    nch = nt // 128
                    src = aT[:, :, j * 512:(j + 1) * 512] if lat else aTc
                    tsl = slice(j * 512, (j + 1) * 512)
                    pz = bank(0 + par * 2)[:, 0:nt]
                    pq = bank(1 + par * 2)[:, 0:nt]
                    for k in range(8):
                        mm(pz, wA[:, k, :], src[:, k, :], start=(k == 0), stop=(k == 7))
                    if lat and fwd:
                        for k in range(8):
                            mm(pq, wB[:, k, :], src[:, k, :], start=(k == 0), stop=(k == 7))
                    e2, l1, l2, bq, bx = (e2b[par][:, 0:nt], l1b[par][:, 0:nt], l2b[par][:, 0:nt],
                                          bb[par][:, 0:nt], bxb[par][:, 0:nt])
                    qt, kt = qtb[par][:, 0:nt], ktb[par][:, 0:nt]
                    act(e2, pz, AF.Exp)
                    act(l1, e2, AF.Ln, bias=lb_ap)
                    act(l2, e2, AF.Ln, bias=1.0)
                    tt("dve", l1, l1, l2, ALU.subtract)
                    S.op("dve", lambda e, bq=bq, l1=l1, nt=nt: e.tensor_tensor_scan(
                        bq, rmask[:, 0:nt], l1, 0.0, ALU.mult, ALU.add), reads=[rmask[:, 0:nt], l1], writes=[bq],
                        dur=0.1 + nt / 480.0)
                    bq3 = bq.rearrange("p (c t) -> p c t", t=128)
                    m4b = bq3[:, :, 63:64].to_broadcast([128, nch, 128])
                    bx3 = bx.rearrange("p (c t) -> p c t", t=128)
                    if fwd:
                        tt("dve", bx3, bq3, m4b, ALU.subtract)
                        tt("dve", l2, l2, bx, ALU.add)
                    else:
                        tt("dve", bx, bq, l1, ALU.subtract)
                        tt("dve", bx3, bx3, m4b, ALU.subtract)
                        tt("dve", l2, bx, l2, ALU.subtract)
                    base = 8 + par * 40
                    m4 = bq3[:, :, 63]
                    tot4 = bq3[:, :, 127]
                    d4 = sc3[:, base:base + nch]
                    cA4 = sc3[:, base + 12:base + 12 + nch]
                    cB4 = sc3[:, base + 16:base + 16 + nch]
                    tt("dve", d4, tot4, m4, ALU.subtract)
                    cAB4 = sc3[:, base + 20:base + 20 + nch]
                    if fwd:
                        act(cA4, m4, AF.Exp)
                        act(cB4, d4, AF.Exp)
                    else:
                        act(cA4, d4, AF.Exp)
                        act(cB4, m4, AF.Exp)
                    act(cAB4, tot4, AF.Exp)
                    if lat:
                        act(e2, bx, AF.Exp, scale=(1.0 if fwd else -1.0))
                    act(kt, l2, AF.Exp, scale=(-1.0 if fwd else 1.0), bias=l1m_ap)
                    if lat:
                        if fwd:
                            tt("dve", qt, pq, e2, ALU.mult)
                            act(qbf[:, tsl], pq, AF.Copy)
                        else:
                            tt("dve", qt, qbf[:, tsl], e2, ALU.mult)
                    return dict(par=par, lat=lat, nt=nt, nch=nch, src=src, tsl=tsl, pz=pz, pq=pq, qt=qt, kt=kt,
                                cA4=cA4, cB4=cB4, cAB4=cAB4, j=j)

                def stage1b(st):
                    par, nch, kt = st["par"], st["nch"], st["kt"]
                    pzb = bank(0 + par * 2).bitcast(BF16)
                    for c in range(nch):
                        tr(pzb[:, c * 128:(c + 1) * 128], kt[:, c * 128:(c + 1) * 128], ident_bf)
                    act(kTall[par][:, 0:nch * 128], pzb[:, 0:nch * 128], AF.Copy)

                def stage2(st, fwd=fwd, msk=msk, h=h, wB=wB):
                    par, lat, nch, tsl, qt, kt = st["par"], st["lat"], st["nch"], st["tsl"], st["qt"], st["kt"]
                    cA4, cB4, j, pq, src = st["cA4"], st["cB4"], st["j"], st["pq"], st["src"]
                    cAB4 = st["cAB4"]
                    po = bank(4 + par)
                    bD, bS = bank(6 + par), bank(1 + par * 2)
                    Pall = PFall if fwd else PBall
                    kTa = kTall[par]
                    corder = list(range(nch)) if fwd else list(range(nch - 1, -1, -1))
                    for c in corder:
                        gch = (j * 4 + c + 2) if lat else c
                        cs = slice(c * 128, (c + 1) * 128)
                        mm(bD[:, cs], kTa[:, cs], vh[:, gch, :])
                    for c in corder:
                        cs = slice(c * 128, (c + 1) * 128)
                        ts("dve", Uall[:, cs], bD[:, cs], cB4[:, c:c + 1], None, ALU.mult)
                    if lat:
                        if fwd:
                            pieces = [(slice(0, 128), slice(64, 128)), (slice(0, 64), slice(0, 64))]
                        else:
                            pieces = [(slice(0, 128), slice(0, 64)), (slice(64, 128), slice(64, 128))]
                        for c in corder:
                            for (sp, tp) in pieces:
                                tq = slice(c * 128 + tp.start, c * 128 + tp.stop)
                                mm(bS[sp, tq], kt[:, c * 128 + sp.start:c * 128 + sp.stop], qt[:, tq])
                        for (sp, tp) in pieces:
                            w = tp.stop - tp.start
                            np_ = sp.stop - sp.start
                            P3 = Pall[sp, :].rearrange("p (c t) -> p c t", t=128)[:, :, tp]
                            S3v = bS[sp, :].rearrange("p (c t) -> p c t", t=128)[:, :, tp]
                            M3 = msk[sp, tp].unsqueeze(1).to_broadcast([np_, nch, w])
                            tt("dve", P3, S3v, M3, ALU.mult)
                    for c in corder:
                        ci = chctr[0]
                        chctr[0] += 1
                        cpar = ci % 2
                        cs = slice(c * 128, (c + 1) * 128)
                        gch = (j * 4 + c + 2) if lat else c
                        if lat:
                            ts("pool", Zb[cpar], Yst, cA4[:, c:c + 1], 1.0, ALU.mult, ALU.mult)
                            mm(po[:, cs], vh[:, gch, :], Pall[:, cs], start=True, stop=False)
                            mm(po[:, cs], Zb[cpar], qt[:, cs], start=False, stop=True)
                        stt(Yst, Yst, cAB4[:, c:c + 1], Uall[:, cs], ALU.mult, ALU.add)
                    if lat and fwd:
                        act(br[:, h, tsl], po, AF.Copy)
                    if lat and not fwd and S3 == 7:
                        act(br[:, h, tsl], po, AF.Copy)
                    elif lat and not fwd:
                        osb = osb1
                        tt("dve", osb, po, br[:, h, tsl], ALU.add)
                        act(sq3, osb, AF.Square)
                        pn = bank(0 + par * 2)
                        mm(pn, ones_bf, sq3)
                        for k in range(8):
                            mm(pq, wB[:, k, :], src[:, k, :], start=(k == 0), stop=(k == 7))
                        act(etmp, pn, AF.Ln, scale=1.0 / 128, bias=EPS)
                        act(etmp, etmp, AF.Exp, scale=-0.5)
                        tt("pool", osb, osb, etmp, ALU.mult)
                        act(etmp, pq, AF.Exp, scale=-1.0)
                        act(etmp, etmp, AF.Ln, bias=1.0)
                        act(etmp, etmp, AF.Exp, scale=-1.0)
                        tt("dve", etmp, pq, etmp, ALU.mult)
                        sg3 = etmp
                        tt("pool", br[:, h, tsl], osb, sg3, ALU.mult)

                for bi_, blk in enumerate(blocks):
                    items.append(dict(s1=stage1, s1b=stage1b, s2=stage2, blk=blk, first_sweep=(bi_ == 0),
                                      first_head=(bi_ == 0 and dirn == 0), wload=wload, sweep_init=sweep_init,
                                      head_init=head_init))

        sweeps_first = [it for it in items if it["first_sweep"]]
        sweeps_first[0]["wload"]()
        nxt_sweep = 1
        prev, prev_st = None, None
        for it in items:
            cur_st = it["s1"](*it["blk"])
            if prev is not None:
                if prev["first_head"]:
                    prev["head_init"]()
                if prev["first_sweep"]:
                    prev["sweep_init"]()
                prev["s2"](prev_st)
            if it["first_sweep"] and nxt_sweep < len(sweeps_first):
                sweeps_first[nxt_sweep]["wload"]()
                nxt_sweep += 1
            it["s1b"](cur_st)
            prev, prev_st = it, cur_st
        if prev["first_head"]:
            prev["head_init"]()
        if prev["first_sweep"]:
            prev["sweep_init"]()
        prev["s2"](prev_st)

    if MAXPHASE >= 5:
        o = PH0
        wo = A(o, 16384, BF16).rearrange("p (k n) -> p k n", k=8); o += 16384
        wos = A(o, 4096, F32); o += 4096
        xr = []
        for i in range(3):
            xr.append(A(o, 4096, F32)); o += 4096
        ht = []
        for i in range(2):
            ht.append(A(o, 4096, F32)); o += 4096
        junk4 = A(o, 2048, BF16); o += 2048
        fs = A(o, 512, F32); o += 512
        fg_rep = A(o, 4096, F32); o += 4096
        dma(fg_rep, fg_d)
        assert o <= ARENA_BYTES, o
        hgn = vecs[:, 40:44]
        wout_v = wout_d.rearrange("(k p) n -> p k n", p=128)
        for k in range(8):
            dma(wos, wout_v[:, k, :])
            if k < 4:
                stt(wo[:, k, :], wos, hgn[:, k:k + 1], gt_rep, ALU.mult, ALU.mult)
            else:
                tt("dve", wo[:, k, :], wos, gt_rep, ALU.mult)
        for t in range(2 if P5 >= 2 else 0):
            dma(xr[t % 3], x_d[t * 128:(t + 1) * 128, :])
        for t in range(32 if P5 >= 2 else 0):
            rows = slice(t * 128, (t + 1) * 128)
            xt = xr[t % 3]
            if t + 2 < 32:
                dma(xr[(t + 2) % 3], x_d[(t + 2) * 128:(t + 3) * 128, :])
            hb = ht[t % 2]
            for half in range(2):
                pb = bank((t % 2) * 2 + half)
                for g in range(8):
                    mm(pb, br[:, g, rows], wo[:, g, half * 512:(half + 1) * 512], start=(g == 0), stop=(g == 7))
                tt("dve", hb[:, half * 512:(half + 1) * 512], pb, xt[:, half * 512:(half + 1) * 512], ALU.add)
            ss = fs[:, (t % 16) * 2:(t % 16) * 2 + 1]
            rs = fs[:, (t % 16) * 2 + 1:(t % 16) * 2 + 2]
            if P5 >= 3:
                act(junk4, hb, AF.Square, accum_out=ss)
                act(rs, ss, AF.Ln, scale=1.0 / DM, bias=EPS)
                act(rs, rs, AF.Exp, scale=-0.5)
                stt(hb, hb, rs, fg_rep, ALU.mult, ALU.mult)
            if P5 >= 4:
                dma(out_d[rows, :], hb)

    if DEBUG:
        if MAXPHASE >= 1:
            dma(dbg_d[:, 0:8 * 4096], A(AT0, 65536, BF16))
        if MAXPHASE >= 2 and SUB > 10:
            dma(dbg_d[:, 12 * 4096:16 * 4096], A(BR0 + 32768, 32768, BF16))
        if MAXPHASE >= 4 and S3 > 10:
            dma(dbg_d[:, 8 * 4096:12 * 4096], A(BR0, 32768, BF16))
        if MAXPHASE >= 4 and S3 in (5, 7):
            dma(dbg_d[:, 8 * 4096:9 * 4096], A(BR0, 8192, BF16))

    if REORDER:
        S.reorder()
    S.emit()
    es.close()
    return nc


def _fm(v, k):
    return np.ascontiguousarray(np.asarray(v, np.float32).reshape(k, 128).T)


_NC_CACHE = {}


def kernel(x, c, ctx, c_ctx, norm_g, w_mod, b_mod, w_in, lb_logits, hgrn_norm_g,
           conv_w, conv_b, conv_ln_g, conv_ln_b, w_out, final_norm_g):
    f32 = np.float32
    x = np.asarray(x, f32)
    ctx = np.asarray(ctx, f32)
    c = np.asarray(c, f32)
    c_ctx = np.asarray(c_ctx, f32)
    w_mod0 = np.ascontiguousarray(np.asarray(w_mod, f32)[0])
    w_in0 = np.ascontiguousarray(np.asarray(w_in, f32)[0])
    w_out0 = np.ascontiguousarray(np.asarray(w_out, f32)[0])
    b_mod2 = np.ascontiguousarray(np.tile(np.asarray(b_mod, f32)[0][None, :], (2, 1)))
    fg_rep = np.ascontiguousarray(np.tile(np.asarray(final_norm_g, f32)[None, :], (128, 1)))
    ident = np.eye(128, dtype=f32)
    s_idx = np.arange(128)[:, None]
    t_idx = np.arange(128)[None, :]
    masks = np.concatenate([(s_idx <= t_idx), (s_idx >= t_idx)], axis=1).astype(np.float32)
    sel = np.zeros((2, 256), f32)
    sel[0, 0:128] = 1.0
    sel[1, 128:256] = 1.0
    rmask = np.ones((128, 512), f32)
    rmask[:, 0::128] = 0.0

    lbl = np.asarray(lb_logits, f32)
    lbl_fm = lbl.reshape(2, 2, 4, 128).transpose(3, 0, 1, 2).reshape(128, 16)
    cwv = np.asarray(conv_w, f32)[0]
    cw_fm = cwv.reshape(NTAP, 4, 128).transpose(2, 1, 0).reshape(128, 4 * NTAP)

    in_maps = []
    for b in range(N_CORES):
        vecs = np.zeros((128, 192), f32)
        cvv = np.stack([_fm(c[b], 8), _fm(c_ctx, 8)], axis=-1)
        vecs[:, 0:16] = cvv.reshape(128, 16)
        vecs[:, 16:24] = _fm(np.asarray(norm_g, f32)[0], 8)
        vecs[:, 24:40] = lbl_fm
        vecs[:, 40:44] = _fm(np.asarray(hgrn_norm_g, f32)[0], 4)
        vecs[:, 44:168] = cw_fm
        vecs[:, 168:172] = _fm(np.asarray(conv_b, f32)[0], 4)
        vecs[:, 172:176] = _fm(np.asarray(conv_ln_g, f32)[0], 4)
        vecs[:, 176:180] = _fm(np.asarray(conv_ln_b, f32)[0], 4)
        in_maps.append({
            "x": np.ascontiguousarray(x[b]),
            "ctx": np.ascontiguousarray(ctx[b]),
            "w_mod": w_mod0, "b_mod2": b_mod2, "w_in": w_in0, "w_out": w_out0,
            "vecs": vecs, "ident": ident, "masks": masks, "sel": sel,
            "fg_rep": fg_rep, "rmask": rmask,
        })
    if "nc" not in _NC_CACHE:
        _NC_CACHE["nc"] = build_program()
    nc = _NC_CACHE["nc"]
    res = run_bass_kernel_spmd(nc, in_maps, core_ids=list(range(N_CORES)))
    out = np.stack([np.asarray(r["out"], f32) for r in res.results], axis=0)
    if DEBUG:
        kernel.dbg = [np.asarray(r["dbg"]) for r in res.results]
    return out
```
